# Optimizing a Trainium2 kernel written in Bass

```python
import jax, jax.numpy as jnp
from jax import lax
import numpy as np

D_MODEL = 1024
BATCH = 4
SEQ = 8192
DEPTH = 1

GRID_W = 64
EPS = 1e-6
MLSTM_HEADS = 4
MLSTM_INNER = D_MODEL
MLSTM_HEAD_DIM = MLSTM_INNER // MLSTM_HEADS
MLSTM_CHUNK = 64
MLSTM_CONV_W = 5
ATTN_HEAD_DIM = 128
ATTN_HEADS = D_MODEL // ATTN_HEAD_DIM
ATTN_KV_HEADS = 2
ATTN_GROUP = ATTN_HEADS // ATTN_KV_HEADS
ATTN_INNER = ATTN_HEADS * ATTN_HEAD_DIM
ATTN_KV_INNER = ATTN_KV_HEADS * ATTN_HEAD_DIM
Q_BLOCK = 128
ROPE_THETA = 10000.0
ROPE_AXIS_DIM = ATTN_HEAD_DIM // 2
N_BRANCH = 2
D_FF = 4 * D_MODEL
IN_SPLITS = (2 * MLSTM_INNER, MLSTM_INNER, MLSTM_INNER, 4 * MLSTM_HEADS,
             ATTN_INNER, ATTN_KV_INNER, ATTN_KV_INNER, N_BRANCH * D_MODEL)
IN_WIDTH = sum(IN_SPLITS)

kernel_name = 'hybrid_mlstm_gqa2drope_sqrelu_encoder_block'


def rmsnorm(x, w):
    x32 = x.astype(jnp.float32)
    y = x32 * lax.rsqrt(jnp.mean(x32 * x32, axis=-1, keepdims=True) + EPS)
    return (y * w.astype(jnp.float32)).astype(x.dtype)


def split_cols(p, sizes):
    offsets = [int(o) for o in np.cumsum(sizes)[:-1]]
    return jnp.split(p, offsets, axis=-1)


def mlstm_chunked(q, k, v, ig, fg):
    N, S, H, d = q.shape
    L = MLSTM_CHUNK
    NC = S // L
    to_chunks = lambda a: a.reshape(N, NC, L, H, d).transpose(1, 0, 3, 2, 4)
    gate_chunks = lambda a: a.astype(jnp.float32).reshape(N, NC, L, H).transpose(1, 0, 3, 2)
    qc, kc, vc = to_chunks(q), to_chunks(k), to_chunks(v)
    li = gate_chunks(ig)
    b = jnp.cumsum(jax.nn.log_sigmoid(gate_chunks(fg)), axis=-1)
    tril = jnp.tril(jnp.ones((L, L), dtype=bool))

    def step(carry, inp):
        C, n, m = carry
        qb, kb, vb, bb, lib = inp
        log_d = bb[..., :, None] - bb[..., None, :] + lib[..., None, :]
        log_d = jnp.where(tril, log_d, -jnp.inf)
        m_inter = bb + m[..., None]
        m_t = jnp.maximum(jnp.max(log_d, axis=-1), m_inter)
        w = jnp.einsum('nhtd,nhsd->nhts', qb, kb) * jnp.exp(log_d - m_t[..., None])
        inter_w = jnp.exp(m_inter - m_t)
        num = jnp.einsum('nhts,nhsd->nhtd', w, vb) + inter_w[..., None] * jnp.einsum('nhtd,nhde->nhte', qb, C)
        den = jnp.sum(w, axis=-1) + inter_w * jnp.einsum('nhtd,nhd->nht', qb, n)
        h = num / jnp.maximum(jnp.abs(den), jnp.exp(-m_t))[..., None]
        b_last = bb[..., -1]
        log_wk = b_last[..., None] - bb + lib
        m_new = jnp.maximum(b_last + m, jnp.max(log_wk, axis=-1))
        decay = jnp.exp(b_last + m - m_new)
        wk = jnp.exp(log_wk - m_new[..., None])
        C_new = decay[..., None, None] * C + jnp.einsum('nhs,nhsd,nhse->nhde', wk, kb, vb)
        n_new = decay[..., None] * n + jnp.einsum('nhs,nhsd->nhd', wk, kb)
        return (C_new, n_new, m_new), h

    init = (jnp.zeros((N, H, d, d), jnp.float32), jnp.zeros((N, H, d), jnp.float32),
            jnp.full((N, H), -jnp.inf, jnp.float32))
    _, hs = lax.scan(step, init, (qc, kc, vc, b, li))
    return hs.transpose(1, 0, 3, 2, 4).reshape(N, S, H, d)


def mlstm_branch(qk_raw, v, o_pre, gates, conv_w, conv_b, gn_w):
    B, S, _ = v.shape
    H, d = MLSTM_HEADS, MLSTM_HEAD_DIM
    pad = MLSTM_CONV_W // 2
    qk = lax.conv_general_dilated(qk_raw, conv_w[:, None, :], window_strides=(1,), padding=[(pad, pad)],
                                  dimension_numbers=('NWC', 'WIO', 'NWC'),
                                  feature_group_count=2 * MLSTM_INNER) + conv_b
    qk = jax.nn.silu(qk)
    q, k = jnp.split(qk, 2, axis=-1)
    q = q.reshape(B, S, H, d) * (d ** -0.5)
    k = k.reshape(B, S, H, d)
    vh = v.reshape(B, S, H, d)
    i_f, f_f, i_b, f_b = jnp.split(gates, 4, axis=-1)
    flip = lambda a: jnp.flip(a, axis=1)
    h = mlstm_chunked(jnp.concatenate([q, flip(q)], 0), jnp.concatenate([k, flip(k)], 0),
                      jnp.concatenate([vh, flip(vh)], 0), jnp.concatenate([i_f, flip(i_b)], 0),
                      jnp.concatenate([f_f, flip(f_b)], 0))
    h = h[:B] + flip(h[B:])
    mu = jnp.mean(h, axis=-1, keepdims=True)
    var = jnp.mean(jnp.square(h - mu), axis=-1, keepdims=True)
    hn = (h - mu) * lax.rsqrt(var + EPS) * gn_w.astype(jnp.float32).reshape(H, d)
    return hn.reshape(B, S, MLSTM_INNER).astype(v.dtype) * jax.nn.sigmoid(o_pre)


def rope_rotate(x, ang):
    half = x.shape[-1] // 2
    c = jnp.cos(ang)[None, :, None, :].astype(x.dtype)
    s = jnp.sin(ang)[None, :, None, :].astype(x.dtype)
    x1, x2 = x[..., :half], x[..., half:]
    return jnp.concatenate([x1 * c - x2 * s, x1 * s + x2 * c], axis=-1)


def rope_2d(x, row, col):
    n_freq = ROPE_AXIS_DIM // 2
    freqs = ROPE_THETA ** (-jnp.arange(n_freq, dtype=jnp.float32) / n_freq)
    ang_row = row.astype(jnp.float32)[:, None] * freqs[None, :]
    ang_col = col.astype(jnp.float32)[:, None] * freqs[None, :]
    return jnp.concatenate([rope_rotate(x[..., :ROPE_AXIS_DIM], ang_row),
                            rope_rotate(x[..., ROPE_AXIS_DIM:], ang_col)], axis=-1)


def attention_branch(q, k, v, qn_w, kn_w):
    B, S, _ = q.shape
    dh = ATTN_HEAD_DIM
    q = rmsnorm(q.reshape(B, S, ATTN_HEADS, dh), qn_w)
    k = rmsnorm(k.reshape(B, S, ATTN_KV_HEADS, dh), kn_w)
    v = v.reshape(B, S, ATTN_KV_HEADS, dh)
    rows = S // GRID_W
    row = jnp.repeat(jnp.arange(rows, dtype=jnp.int32), GRID_W)
    col = jnp.tile(jnp.arange(GRID_W, dtype=jnp.int32), rows)
    q = rope_2d(q, row, col)
    k = rope_2d(k, row, col)
    nb = S // Q_BLOCK
    qb = q.reshape(B, nb, Q_BLOCK, ATTN_KV_HEADS, ATTN_GROUP, dh).transpose(1, 0, 3, 4, 2, 5)
    kt = k.transpose(0, 2, 1, 3)
    vt = v.transpose(0, 2, 1, 3)
    scale = dh ** -0.5

    def block(qblk):
        s = jnp.einsum('bkgqd,bksd->bkgqs', qblk, kt).astype(jnp.float32) * scale
        p = jax.nn.softmax(s, axis=-1)
        return jnp.einsum('bkgqs,bksd->bkgqd', p.astype(vt.dtype), vt)

    o = lax.map(block, qb)
    return o.transpose(1, 0, 4, 2, 3, 5).reshape(B, S, ATTN_INNER)


def setup_inputs(seed: int = 0) -> dict:
    key = jax.random.key(seed)
    ks = jax.random.split(key, 24)
    nrm = lambda k, shape: jax.random.normal(k, shape, jnp.float32)
    dense = lambda k, fan_in, shape: nrm(k, shape) * (fan_in ** -0.5)
    gain = lambda k, shape: 1.0 + 0.02 * nrm(k, shape)
    i_bias = 0.1 * nrm(ks[8], (DEPTH, 2, 1, MLSTM_HEADS))
    f_bias = jnp.linspace(3.0, 6.0, MLSTM_HEADS, dtype=jnp.float32) + 0.1 * nrm(ks[9], (DEPTH, 2, 1, MLSTM_HEADS))
    b_gates = jnp.concatenate([i_bias, f_bias], axis=2).reshape(DEPTH, 4 * MLSTM_HEADS)
    return {
        'x': nrm(ks[0], (BATCH, SEQ, D_MODEL)),
        'c': nrm(ks[1], (BATCH, D_MODEL)),
        'w_ada': dense(ks[2], D_MODEL, (DEPTH, D_MODEL, 6 * D_MODEL)),
        'b_ada': 0.02 * nrm(ks[3], (DEPTH, 6 * D_MODEL)),
        'norm1_pre': gain(ks[4], (DEPTH, D_MODEL)),
        'norm1_post': gain(ks[5], (DEPTH, D_MODEL)),
        'w_in': dense(ks[6], D_MODEL, (DEPTH, D_MODEL, IN_WIDTH)),
        'b_gates': b_gates,
        'conv_w': dense(ks[10], MLSTM_CONV_W, (DEPTH, MLSTM_CONV_W, 2 * MLSTM_INNER)),
        'conv_b': 0.02 * nrm(ks[11], (DEPTH, 2 * MLSTM_INNER)),
        'mlstm_gn': gain(ks[12], (DEPTH, MLSTM_INNER)),
        'attn_qnorm': gain(ks[13], (DEPTH, ATTN_HEAD_DIM)),
        'attn_knorm': gain(ks[14], (DEPTH, ATTN_HEAD_DIM)),
        'w_branch_m': dense(ks[15], MLSTM_INNER, (DEPTH, MLSTM_INNER, D_MODEL)),
        'w_branch_a': dense(ks[16], ATTN_INNER, (DEPTH, ATTN_INNER, D_MODEL)),
        'w_out': dense(ks[17], D_MODEL, (DEPTH, D_MODEL, D_MODEL)),
        'norm2_pre': gain(ks[18], (DEPTH, D_MODEL)),
        'norm2_post': gain(ks[19], (DEPTH, D_MODEL)),
        'w_mlp_in': dense(ks[20], D_MODEL, (DEPTH, D_MODEL, D_FF)),
        'w_mlp_out': dense(ks[21], D_FF, (DEPTH, D_FF, D_MODEL)),
    }


def reference(x, c, w_ada, b_ada, norm1_pre, norm1_post, w_in, b_gates, conv_w, conv_b, mlstm_gn,
              attn_qnorm, attn_knorm, w_branch_m, w_branch_a, w_out, norm2_pre, norm2_post,
              w_mlp_in, w_mlp_out):
    sc = jax.nn.silu(c)
    for l in range(DEPTH):
        mod = (sc @ w_ada[l] + b_ada[l])[:, None, :]
        shift1, scale1, gate1, shift2, scale2, gate2 = jnp.split(mod, 6, axis=-1)
        h = rmsnorm(x, norm1_pre[l]) * (1.0 + scale1) + shift1
        proj = h @ w_in[l]
        qk_m, v_m, o_m, g_m, q_a, k_a, v_a, br = split_cols(proj, IN_SPLITS)
        y_m = mlstm_branch(qk_m, v_m, o_m, g_m + b_gates[l], conv_w[l], conv_b[l], mlstm_gn[l])
        y_a = attention_branch(q_a, k_a, v_a, attn_qnorm[l], attn_knorm[l])
        g_mlstm, g_attn = jnp.split(jax.nn.sigmoid(br), 2, axis=-1)
        y = g_mlstm * (y_m @ w_branch_m[l]) + g_attn * (y_a @ w_branch_a[l])
        y = y @ w_out[l]
        x = x + gate1 * rmsnorm(y, norm1_post[l])
        h2 = rmsnorm(x, norm2_pre[l]) * (1.0 + scale2) + shift2
        u = jnp.square(jax.nn.relu(h2 @ w_mlp_in[l]))
        x = x + gate2 * rmsnorm(u @ w_mlp_out[l], norm2_post[l])
    return x
```

```python
import contextlib
import numpy as np
import concourse.bass as bass
import concourse.mybir as mybir
from concourse.bass_utils import run_bass_kernel_spmd

F32 = mybir.dt.float32
BF16 = mybir.dt.bfloat16
AF = mybir.ActivationFunctionType
ALU = mybir.AluOpType

ENGS = ('pe', 'act', 'dve', 'pool', 'sp')


class _Op:
    __slots__ = ('eng', 'fn', 'deps', 'seq', 'blk', 'signal', 'ticket', 'dma', 'sem', 'val', 'key')


class Prog:
    def __init__(self, nc):
        self.nc = nc
        self.stack = contextlib.ExitStack()
        self.engs = {'pe': nc.tensor, 'act': nc.scalar, 'dve': nc.vector, 'pool': nc.gpsimd, 'sp': nc.sync}
        self.sem = {e: self.stack.enter_context(nc.semaphore('sem_' + e)) for e in ENGS}
        self.count = {e: 0 for e in ENGS}
        self.ops = {e: [] for e in ENGS}
        self.seq = {e: 0 for e in ENGS}
        self.last_write = {}
        self.readers = {}
        self.seen = {e: {} for e in ENGS}
        self.dma_sems = {}
        self.free_dma = []
        self.n_dsem = 0
        self.blk = 0
        self.n_instr = 0

    def _deps(self, o, reads, writes):
        deps = []
        for r in reads:
            lw = self.last_write.get(r)
            if lw is not None:
                deps.append((lw, 'raw'))
        for w in writes:
            lw = self.last_write.get(w)
            if lw is not None:
                deps.append((lw, 'waw'))
            for rd in self.readers.get(w, ()):
                deps.append((rd, 'war'))
        out = []
        seen = self.seen[o.eng]
        for d, kind in deps:
            if d is o:
                continue
            if d.dma:
                if d.blk != self.blk:
                    continue
                k = ('dma', d.key)
                if seen.get(k, 0) >= d.val:
                    continue
                seen[k] = d.val
                out.append(d)
            else:
                if d.blk != self.blk:
                    continue
                if d.eng == o.eng and not o.dma:
                    if o.eng == 'pe' or kind == 'war':
                        continue
                if seen.get(d.eng, -1) >= d.seq:
                    continue
                seen[d.eng] = d.seq
                out.append(d)
        o.deps = out
        for r in reads:
            lst = self.readers.setdefault(r, [])
            lst[:] = [x for x in lst if not (x.eng == o.eng and not x.dma and not o.dma)]
            lst.append(o)
        for w in writes:
            self.last_write[w] = o
            self.readers[w] = []

    def op(self, eng, fn, reads=(), writes=()):
        o = _Op()
        o.eng, o.fn, o.dma, o.signal, o.blk = eng, fn, False, False, self.blk
        o.seq = self.seq[eng]
        self.seq[eng] += 1
        self._deps(o, reads, writes)
        self.ops[eng].append(o)
        return o

    def dma(self, eng, fn, reads=(), writes=(), key=None):
        assert key is not None
        o = _Op()
        o.eng, o.fn, o.dma, o.signal, o.blk, o.key = eng, fn, True, False, self.blk, key
        o.seq = self.seq[eng]
        self.seq[eng] += 1
        if key not in self.dma_sems:
            if self.free_dma:
                self.dma_sems[key] = self.free_dma.pop()
            else:
                self.n_dsem += 1
                self.dma_sems[key] = [self.stack.enter_context(self.nc.semaphore('dsem%d' % self.n_dsem)), 0]
        ent = self.dma_sems[key]
        ent[1] += 16
        o.sem, o.val = ent[0], ent[1]
        self._deps(o, reads, writes)
        self.ops[eng].append(o)
        return o

    def flush(self, final=False):
        for e in ENGS:
            for o in self.ops[e]:
                for d in o.deps:
                    if not d.dma:
                        d.signal = True
        for e in ENGS:
            c = self.count[e]
            for o in self.ops[e]:
                if not o.dma and o.signal:
                    c += 1
                    o.ticket = c
            self.count[e] = c
        tail = []
        for key, (sem, tot) in self.dma_sems.items():
            if tot:
                tail.append((sem, tot))
        with self.nc.Block() as block:
            regs = {'pe': block.tensor, 'act': block.scalar, 'dve': block.vector,
                    'pool': block.gpsimd, 'sp': block.sync}
            for e in ENGS:
                ops = self.ops[e]
                if not ops and e != 'sp':
                    continue

                def body(eng, ops=ops, e=e):
                    for o in ops:
                        for d in o.deps:
                            if d.dma:
                                eng.wait_ge(d.sem, d.val)
                            else:
                                eng.wait_ge(self.sem[d.eng], d.ticket)
                        ins = o.fn(eng)
                        self.n_instr += 1
                        if o.dma:
                            ins.then_inc(o.sem, 16)
                        elif o.signal:
                            ins.then_inc(self.sem[e], 1)
                    if e == 'sp':
                        for sem, tot in tail:
                            eng.wait_ge(sem, tot)

                regs[e](body)
        self.ops = {e: [] for e in ENGS}
        self.seen = {e: {} for e in ENGS}
        self.free_dma.extend(self.dma_sems.values())
        self.dma_sems = {}
        self.blk += 1

    def finish(self):
        self.flush(final=True)
        self.stack.close()


D = 1024
NH_M = 4
DH_M = 256
NQ_A = 8
NKV_A = 2
DH_A = 128
DFF = 4096
EPS = 1e-6
C_QM, C_KM, C_QA, C_KA, C_VM, C_OM, C_VA, C_GT, C_END = 0, 1024, 2048, 3072, 3328, 4352, 5376, 5632, 5648
K_ID, K_ONE, K_LE, K_GE, K_ROT, K_END = 0, 128, 256, 384, 512, 640


class Ring:
    def __init__(self, name, n):
        self.name, self.n, self.i = name, n, -1

    def next(self):
        self.i = (self.i + 1) % self.n
        return self.i

    def res(self, i=None):
        return (self.name, self.i if i is None else i)


def build(S=8192, debug=(), stop_after=None):
    SO, NT, NB = S // 2, S // 128, S // 512
    NTO, NBO = NT // 2, NB // 2
    nc = bass.Bass("TRN2", target_bir_lowering=False)
    P = Prog(nc)

    def din(name, shape):
        return nc.dram_tensor(name, shape, F32, kind="ExternalInput").ap()

    def dscr(name, shape, dt):
        kind = "ExternalOutput" if name in debug else "Internal"
        return nc.dram_tensor(name, shape, dt, kind=kind).ap()

    x = din("x", [S, D]); c_in = din("c", [D]); w_ada = din("w_ada", [D, 6 * D]); b_ada = din("b_ada", [6 * D])
    n1pre = din("norm1_pre", [D]); n1post = din("norm1_post", [D]); n2pre = din("norm2_pre", [D]); n2post = din("norm2_post", [D])
    w_in = din("w_in", [D, C_END]); w_br = din("w_br", [D, 2 * D]); b_gates = din("b_gates", [16])
    conv_w = din("conv_w", [2 * D, 5]); conv_b = din("conv_b", [2 * D]); gn_w = din("mlstm_gn", [D])
    qn_w = din("attn_qnorm", [128]); kn_w = din("attn_knorm", [128])
    w_bm = din("w_branch_m", [D, D]); w_ba = din("w_branch_a", [D, D]); w_out = din("w_out", [D, D])
    w_m1 = din("w_mlp_in", [D, DFF]); w_m2 = din("w_mlp_out", [DFF, D])
    cos_t = din("cos_t", [128, S]); sin_t = din("sin_t", [128, S]); cst = din("cst", [128, K_END])
    y_out = nc.dram_tensor("y", [SO, D], F32, kind="ExternalOutput").ap()

    mod_scr = dscr("mod_scr", [6 * D], F32)
    hT_scr = dscr("hT_scr", [D, SO], BF16)
    qT_scr = dscr("qT_scr", [D, SO], BF16)
    kT_scr = dscr("kT_scr", [D, S], BF16)
    v_scr = dscr("v_scr", [S, D], BF16)
    og_scr = dscr("og_scr", [SO, D], BF16)
    QT_scr = dscr("QT_scr", [D, SO], BF16)
    KT_scr = dscr("KT_scr", [2 * 128, S], BF16)
    VA_scr = dscr("VA_scr", [S, 256], BF16)
    hB_scr = dscr("hB_scr", [SO, D], F32)
    ymT_scr = dscr("ymT_scr", [D, SO], BF16)
    yaT_scr = dscr("yaT_scr", [D, SO], BF16)
    x1_scr = dscr("x1_scr", [SO, D], F32)
    h2T_scr = dscr("h2T_scr", [D, SO], BF16)

    es = contextlib.ExitStack()
    em = contextlib.ExitStack()

    def sb(name, shape, dt=F32, stack=None):
        return (stack or es).enter_context(nc.sbuf_tensor(name, shape, dt))

    def ps(name, shape, dt=F32, stack=None):
        return (stack or es).enter_context(nc.psum_tensor(name, shape, dt))

    def mm(out, lhsT, rhs, start, stop, reads, writes):
        P.op('pe', lambda e: e.matmul(out, lhsT=lhsT, rhs=rhs, start=start, stop=stop), reads, writes)

    def tr(out, in_, ident, reads, writes):
        P.op('pe', lambda e: e.transpose(out, in_, ident), reads, writes)

    def act(out, in_, func, reads, writes, **kw):
        P.op('act', lambda e: e.activation(out=out, in_=in_, func=func, **kw), reads, writes)

    def tt(eng, out, a, b, op, reads, writes):
        P.op(eng, lambda e: e.tensor_tensor(out=out, in0=a, in1=b, op=op), reads, writes)

    def tsc(eng, out, a, s1, s2, op0, op1, reads, writes, **kw):
        P.op(eng, lambda e: e.tensor_scalar(out=out, in0=a, scalar1=s1, scalar2=s2, op0=op0, op1=op1, **kw), reads, writes)

    def stt(out, a, s, b, op0, op1, reads, writes, **kw):
        P.op('dve', lambda e: e.scalar_tensor_tensor(out=out, in0=a, scalar=s, in1=b, op0=op0, op1=op1, **kw), reads, writes)

    def cp(eng, out, in_, reads, writes):
        P.op(eng, lambda e: e.tensor_copy(out=out, in_=in_), reads, writes)

    def dma(q, out, in_, reads, writes, key, **kw):
        P.dma(q, lambda e: e.dma_start(out=out, in_=in_, **kw), reads, writes, key)

    NCD = dict(allow_slow_non_contiguous=True)
    CAST = dict(max_dma_last_dim=8192)

    cst_f = sb("cst_f", [128, K_END])
    ident_f = cst_f[:, K_ID:K_ID + 128]; ones_f = cst_f[:, K_ONE:K_ONE + 128]
    tri_le = cst_f[:, K_LE:K_LE + 128]; tri_ge = cst_f[:, K_GE:K_GE + 128]; rotT = cst_f[:, K_ROT:K_ROT + 128]
    cst_b = sb("cst_b", [128, 256], BF16)
    ident_b = cst_b[:, 0:128]; ones_b = cst_b[:, 128:256]
    modcols = sb("modcols", [128, 48])
    gs1 = sb("gs1", [128, 8]); gs2 = sb("gs2", [128, 8])
    G1 = sb("G1", [128, D]); G2 = sb("G2", [128, D])
    npre = sb("npre", [128, 16])
    eps_c = sb("eps_c", [128, 1])
    cw = sb("cw", [128, 16, 5], stack=em); cb = sb("cb", [128, 16], stack=em)
    qkn = sb("qkn", [128, 2], stack=em)
    bg_bc = sb("bg_bc", [128, 16], stack=em)
    gncol = sb("gncol", [128, 8], stack=em)
    GI = sb("GI", [128, NT, 16], stack=em)

    with contextlib.ExitStack() as sa:
        dma('sp', cst_f[:], cst, [], ['cst_f'], 'cst_f')
        cp('dve', cst_b[:], cst_f[:, 0:256], ['cst_f'], ['cst_b'])
        P.op('dve', lambda e: e.memset(eps_c[:], EPS), [], ['eps_c'])
        c_col = sb("c_col", [128, 8], stack=sa); sc_col = sb("sc_col", [128, 8], stack=sa)
        dma('sp', c_col[:], c_in.rearrange("(k p) -> p k", p=128), [], ['c_col'], 'c_col', **NCD)
        act(sc_col[:], c_col[:], AF.Silu, ['c_col'], ['sc_col'])
        badar = sb("badar", [1, 6 * D], stack=sa); modrow = sb("modrow", [1, 6 * D], stack=sa)
        dma('sp', badar[:], b_ada.rearrange("(o n) -> o n", o=1), [], ['badar'], 'badar')
        wa = sb("wa", [128, 2, 8, 512], stack=sa)
        mp = ps("mp", [128, 512], stack=sa)
        w_ada_v = w_ada.rearrange("(k p) n -> p k n", p=128)
        for g in range(12):
            b = g % 2
            dma('sp', wa[:, b], w_ada_v[:, :, g * 512:(g + 1) * 512], [], [('wa', b)], ('wa', b))
            for k in range(8):
                mm(mp[0:1, :], sc_col[:, k:k + 1], wa[:, b, k, :], k == 0, k == 7, ['sc_col', ('wa', b)], ['mp'])
            tt('dve', modrow[:, g * 512:(g + 1) * 512], mp[0:1, :], badar[:, g * 512:(g + 1) * 512], ALU.add,
               ['mp', 'badar'], ['modrow'])
        dma('sp', mod_scr.rearrange("(o n) -> o n", o=1), modrow[:], ['modrow'], ['mod_scr'], 'modrow')
        dma('sp', modcols[:], mod_scr.rearrange("(w p) -> p w", p=128), ['mod_scr'], ['modcols'], 'modcols', **NCD)
        g1b = sb("g1b", [128, D], stack=sa); g2b = sb("g2b", [128, D], stack=sa)
        dma('sp', g1b[:], mod_scr[2 * D:3 * D].partition_broadcast(128), ['mod_scr'], ['g1b'], 'g1b')
        dma('sp', g2b[:], mod_scr[5 * D:6 * D].partition_broadcast(128), ['mod_scr'], ['g2b'], 'g2b')
        dma('sp', G1[:], n1post.partition_broadcast(128), [], ['G1'], 'G1')
        dma('sp', G2[:], n2post.partition_broadcast(128), [], ['G2'], 'G2')
        tt('dve', G1[:], G1[:], g1b[:], ALU.mult, ['G1', 'g1b'], ['G1'])
        tt('dve', G2[:], G2[:], g2b[:], ALU.mult, ['G2', 'g2b'], ['G2'])
        dma('sp', npre[:, 0:8], n1pre.rearrange("(k p) -> p k", p=128), [], ['npre1'], 'npre1', **NCD)
        dma('sp', npre[:, 8:16], n2pre.rearrange("(k p) -> p k", p=128), [], ['npre2'], 'npre2', **NCD)
        stt(gs1[:], modcols[:, 8:16], 1.0, npre[:, 0:8], ALU.add, ALU.mult, ['modcols', 'npre1'], ['gs1'])
        stt(gs2[:], modcols[:, 32:40], 1.0, npre[:, 8:16], ALU.add, ALU.mult, ['modcols', 'npre2'], ['gs2'])
        dma('sp', cw[:], conv_w.rearrange("(c p) j -> p c j", p=128), [], ['cw'], 'cw')
        dma('sp', cb[:], conv_b.rearrange("(c p) -> p c", p=128), [], ['cb'], 'cb', **NCD)
        dma('sp', qkn[:, 0:1], qn_w.rearrange("(p o) -> p o", o=1), [], ['qkn0'], 'qkn0', **NCD)
        dma('sp', qkn[:, 1:2], kn_w.rearrange("(p o) -> p o", o=1), [], ['qkn1'], 'qkn1', **NCD)
        dma('sp', bg_bc[:], b_gates.partition_broadcast(128), [], ['bg_bc'], 'bg_bc')
        dma('sp', gncol[:], gn_w.rearrange("(k p) -> p k", p=128), [], ['gncol'], 'gncol', **NCD)
        P.flush()
    sh1 = modcols[:, 0:8]; sh2 = modcols[:, 24:32]
    if stop_after == 'A':
        P.finish(); em.close(); es.close(); return nc

    with contextlib.ExitStack() as sbk:
        w_sb = sb("w_sb", [128, 8, C_END], BF16, stack=sbk)
        w_in_v = w_in.rearrange("(k p) n -> p k n", p=128)
        wgroups = [(C_QM, C_KM), (C_KM, C_QA), (C_QA, C_KA), (C_KA, C_VM), (C_VM, C_OM), (C_OM, C_VA), (C_VA, C_END)]

        def wres(c0):
            for a, b in wgroups:
                if a <= c0 < b:
                    return ('w_sb', a)

        for a, b in wgroups:
            dma('pool', w_sb[:, :, a:b], w_in_v[:, :, a:b], [], [('w_sb', a)], ('w_sb', a), **CAST)
        xt = sb("xt", [128, 3, D], stack=sbk); junk = sb("junk", [128, D], BF16, stack=sbk)
        sst = sb("sst", [128, 3, 4], stack=sbk)
        hn = sb("hn", [128, 2, D], stack=sbk)
        hT = sb("hT", [128, 2, 8, 512], BF16, stack=sbk)
        rawb = sb("rawb", [128, 3, 520], stack=sbk); halo = sb("halo", [128, 16, 4], stack=sbk)
        cacc = sb("cacc", [128, 2, 512], stack=sbk); sout = sb("sout", [128, 3, 512], BF16, stack=sbk)
        cs = sb("cs", [128, 2, 2, 512], stack=sbk)
        sq = sb("sq", [128, 3, 512], BF16, stack=sbk); qw = sb("qw", [128, 3, 512], stack=sbk)
        rs = sb("rs", [128, 3, 512], stack=sbk); t1 = sb("t1", [128, 3, 512], stack=sbk)
        t2 = sb("t2", [128, 2, 512], stack=sbk); rout = sb("rout", [128, 3, 512], BF16, stack=sbk)
        vt = sb("vt", [128, 2, D], BF16, stack=sbk); ogt = sb("ogt", [128, 2, D], BF16, stack=sbk)
        vat = sb("vat", [128, 2, 256], BF16, stack=sbk)
        tp = [ps("tp%d" % i, [128, 512], stack=sbk) for i in range(2)]
        fa = [ps("fa%d" % i, [128, 512], stack=sbk) for i in range(3)]
        pss = ps("pss", [128, 512], stack=sbk)
        psr = [ps("psr%d" % i, [128, 512], stack=sbk) for i in range(2)]
        R_x, R_fa, R_raw, R_cacc, R_sout = Ring('xt', 3), Ring('fa', 3), Ring('rawb', 3), Ring('cacc', 2), Ring('sout', 3)
        R_rope, R_psr, R_t2, R_hn = Ring('rope', 3), Ring('psr', 2), Ring('t2', 2), Ring('hn', 2)
        P.op('pool', lambda e: e.memset(halo[:], 0.0), [], [('halo', c) for c in range(16)])
        mhalf = sb("mhalf", [128, 1], stack=sbk)
        P.op('pool', lambda e: e.memset(mhalf[:], -0.5), [], ['mhalf'])
        hT_sv = hT_scr.rearrange("(k p) n -> p k n", p=128)

        def hres(hb):
            return [('hT', hb, t) for t in range(4)]

        def conv_item(ch, hb, blk, tok0, dst, tmax, width=512, tail=False):
            st = {}

            def s0():
                if tail:
                    return
                c0 = (C_QM if ch < 8 else C_KM - 1024) + ch * 128
                st['f'] = f = R_fa.next()
                for k in range(8):
                    mm(fa[f][:], w_sb[:, k, c0:c0 + 128], hT[:, hb, k, :], k == 0, k == 7, [wres(c0)] + hres(hb), [('fa', f)])

            def s1():
                st['rb'] = rb = R_raw.next(); rr = ('rawb', rb)
                if tail:
                    P.op('pool', lambda e: e.memset(rawb[:, rb, 4:8], 0.0), [], [rr])
                else:
                    act(rawb[:, rb, 4:4 + 512], fa[st['f']][:], AF.Copy, [('fa', st['f'])], [rr])
                cp('pool', rawb[:, rb, 0:4], halo[:, ch, :], [('halo', ch), rr], [rr])
                if not tail:
                    cp('pool', halo[:, ch, :], rawb[:, rb, 512:516], [rr], [('halo', ch)])

            def s2():
                rb = st['rb']; rr = ('rawb', rb)
                st['ca'] = ca = R_cacc.next(); rc = ('cacc', ca)
                tsc('dve', cacc[:, ca, 0:width], rawb[:, rb, 0:width], cw[:, ch, 0:1], None, ALU.mult, ALU.bypass, [rr, 'cw'], [rc])
                for j in range(1, 5):
                    stt(cacc[:, ca, 0:width], rawb[:, rb, j:j + width], cw[:, ch, j:j + 1], cacc[:, ca, 0:width],
                        ALU.mult, ALU.add, [rr, 'cw', rc], [rc])

            def s3():
                ca = st['ca']; rc = ('cacc', ca)
                so = R_sout.next(); rso = ('sout', so)
                act(sout[:, so, 0:width], cacc[:, ca, 0:width], AF.Silu, [rc, 'cb'], [rso], bias=cb[:, ch:ch + 1])
                lo = max(tok0 - 2, 0); hi = min(tok0 - 2 + width, tmax)
                if hi > lo:
                    j0 = lo - (tok0 - 2)
                    cc = ch % 8
                    dma('sp', dst[cc * 128:(cc + 1) * 128, lo:hi], sout[:, so, j0:j0 + (hi - lo)], [rso], [], rso)

            return [s0, s1, s2, s3]

        def rope_item(c0, wcol, hb, csb, dst_ap):
            st = {}

            def s0():
                st['f'] = f = R_fa.next()
                for k in range(8):
                    mm(fa[f][:], w_sb[:, k, c0:c0 + 128], hT[:, hb, k, :], k == 0, k == 7, [wres(c0)] + hres(hb), [('fa', f)])

            def s1():
                st['r'] = r = R_rope.next()
                act(sq[:, r], fa[st['f']][:], AF.Square, [('fa', st['f'])], [('sq', r)])
                act(qw[:, r], fa[st['f']][:], AF.Copy, [('fa', st['f'])], [('qw', r)], scale=qkn[:, wcol:wcol + 1])

            def s2():
                r = st['r']
                st['pr'] = pr = R_psr.next()
                mm(pss[:], ones_b, sq[:, r], True, True, [('sq', r), 'cst_b'], ['pss'])
                mm(psr[pr][:], rotT, qw[:, r], True, True, [('qw', r), 'cst_f'], [('psr', pr)])
                act(rs[:, r], pss[:], AF.Ln, ['pss', 'eps_c'], [('rs', r)], scale=1.0 / 128.0, bias=eps_c[:, 0:1])
                act(rs[:, r], rs[:, r], AF.Exp, [('rs', r)], [('rs', r)], scale=-0.5)
                tt('dve', t1[:, r], qw[:, r], cs[:, csb, 0], ALU.mult, [('qw', r), ('cs', csb)], [('t1', r)])

            def s3():
                r = st['r']; pr = st['pr']
                t = R_t2.next()
                tt('dve', t2[:, t], psr[pr][:], cs[:, csb, 1], ALU.mult, [('psr', pr), ('cs', csb)], [('t2', t)])
                tt('dve', t1[:, r], t1[:, r], t2[:, t], ALU.add, [('t1', r), ('t2', t)], [('t1', r)])
                tt('dve', rout[:, r], t1[:, r], rs[:, r], ALU.mult, [('t1', r), ('rs', r)], [('rout', r)])
                dma('sp', dst_ap, rout[:, r], [('rout', r)], [], ('rout', r))

            return [s0, s1, s2, s3]

        def tm_item(t, c0, n, hb, post):
            st = {}

            def s0():
                st['f'] = f = R_fa.next()
                for k in range(8):
                    mm(fa[f][:, 0:n], hT[:, hb, k, t * 128:(t + 1) * 128], w_sb[:, k, c0:c0 + n], k == 0, k == 7,
                       [wres(c0), ('hT', hb, t)], [('fa', f)])

            def s1():
                post(fa[st['f']], ('fa', st['f']))

            return [s0, s1]

        def x_items(blk, hb):
            items = []; dmas = []
            for t in range(4):
                st = {}

                def xd(t=t, st=st):
                    tile = blk * 4 + t
                    st['xb'] = xb = R_x.next(); rx = ('xt', xb)
                    dma('sp', xt[:, xb], x[tile * 128:(tile + 1) * 128, :], [], [rx], rx)

                def xa(t=t, st=st):
                    xb = st['xb']; rx = ('xt', xb)
                    st['hb_'] = hb_ = R_hn.next()
                    stt(junk[:], xt[:, xb], 1.0, xt[:, xb], ALU.mult, ALU.mult, [rx], ['junk', ('sst0', xb)], accum_out=sst[:, xb, 0:1])
                    tsc('dve', sst[:, xb, 1:2], sst[:, xb, 0:1], 1.0 / D, EPS, ALU.mult, ALU.add, [('sst0', xb)], [('sst1', xb)])
                    tt('pool', sst[:, xb, 2:3], sst[:, xb, 1:2], mhalf[:, 0:1], ALU.pow, [('sst1', xb), 'mhalf'], [('sst2', xb)])
                    act(hn[:, hb_], xt[:, xb], AF.Copy, [rx, ('sst2', xb)], [('hn', hb_)], scale=sst[:, xb, 2:3])

                def xb_(t=t, st=st):
                    hb_ = st['hb_']
                    for half in range(2):
                        for kk in range(4):
                            k = half * 4 + kk
                            tr(tp[half][:, kk * 128:(kk + 1) * 128], hn[:, hb_, k * 128:(k + 1) * 128], ident_f,
                               [('hn', hb_), 'cst_f'], [('tp', half)])
                        for kk in range(4):
                            k = half * 4 + kk
                            act(hT[:, hb, k, t * 128:(t + 1) * 128], tp[half][:, kk * 128:(kk + 1) * 128], AF.Identity,
                                [('tp', half), 'gs1', 'modcols'], [('hT', hb, t)], scale=gs1[:, k:k + 1], bias=sh1[:, k:k + 1])
                    if t == 3 and blk < NBO:
                        dma('sp', hT_sv[:, :, blk * 512:(blk + 1) * 512], hT[:, hb], hres(hb), [('hT_scr', blk)], ('hTst', hb))

                items.append([xa]); items.append([xb_]); dmas.append([xd])
            return items, dmas

        def block_items(blk, hb):
            own = blk < NBO
            csb = blk % 2
            items = []

            def csload():
                dma('sp', cs[:, csb, 0], cos_t[:, blk * 512:(blk + 1) * 512], [], [('cs', csb)], ('cs', csb))
                dma('sp', cs[:, csb, 1], sin_t[:, blk * 512:(blk + 1) * 512], [], [('cs', csb)], ('cs', csb))

            items.append([csload])
            LC, LR, LT, LTV = [], [], [], []
            if blk <= NBO:
                for ch in range(8):
                    LC.append(conv_item(ch, hb, blk, blk * 512, qT_scr, SO))
            for ch in range(8):
                LC.append(conv_item(8 + ch, hb, blk, blk * 512, kT_scr, S))
            if own:
                for h in range(8):
                    LR.append(rope_item(C_QA + h * 128, 0, hb, csb, QT_scr[h * 128:(h + 1) * 128, blk * 512:(blk + 1) * 512]))
            for g in range(2):
                LR.append(rope_item(C_KA + g * 128, 1, hb, csb, KT_scr[g * 128:(g + 1) * 128, blk * 512:(blk + 1) * 512]))
            items_tm = LT
            for t in range(4):
                tile = blk * 4 + t
                tk = slice(tile * 128, (tile + 1) * 128)
                vb = tile % 2

                def post_v(half, vb=vb, tk=tk):
                    def f(pt, ra):
                        rv = ('vt', vb, half)
                        if half:
                            act(vt[:, vb, 512:1024], pt[:], AF.Copy, [ra], [rv])
                            dma('sp', v_scr[tk, :], vt[:, vb], [('vt', vb, 0), rv], [], ('vtst', vb))
                        else:
                            act(vt[:, vb, 0:512], pt[:], AF.Copy, [ra], [rv])
                    return f

                def post_o(half, vb=vb, tk=tk):
                    def f(pt, ra):
                        ro = ('ogt', vb, half)
                        act(ogt[:, vb, half * 512:(half + 1) * 512], pt[:], AF.Sigmoid, [ra], [ro])
                        if half:
                            dma('sp', og_scr[tk, :], ogt[:, vb], [('ogt', vb, 0), ro], [], ('ogst', vb))
                    return f

                def post_a(pt, ra, vb=vb, tk=tk, tile=tile):
                    rva = ('vat', vb)
                    cp('dve', vat[:, vb], pt[:, 0:256], [ra], [rva])
                    tt('dve', GI[:, tile, :], pt[:, 256:272], bg_bc[:], ALU.add, [ra, 'bg_bc'], [('GI', tile)])
                    dma('sp', VA_scr[tk, :], vat[:, vb], [rva], [], rva)

                for half in range(2):
                    LTV.append(tm_item(t, C_VM + half * 512, 512, hb, post_v(half)))
                if own:
                    for half in range(2):
                        LT.append(tm_item(t, C_OM + half * 512, 512, hb, post_o(half)))
                LT.append(tm_item(t, C_VA, 272, hb, post_a))
            ic, iv = 0, 0
            while ic < len(LC) or iv < len(LTV):
                take_c = (len(LC) - ic) * max(1, len(LTV)) >= (len(LTV) - iv) * max(1, len(LC))
                if ic < len(LC) and (take_c or iv >= len(LTV)):
                    items.append(LC[ic]); ic += 1
                else:
                    items.append(LTV[iv]); iv += 1
            items.extend(LR); items.extend(LT)
            return items

        sched = []
        xi0, xd0 = x_items(0, 0)
        pend_dma = list(xd0)
        sched.append(pend_dma.pop(0))
        for it_ in xi0:
            if it_[0].__name__ == 'xa' and pend_dma:
                sched.append(pend_dma.pop(0))
            sched.append(it_)
        for blk in range(NB):
            bi = block_items(blk, blk % 2)
            xi, xdn = x_items(blk + 1, (blk + 1) % 2) if blk + 1 < NB else ([], [])
            pend_dma.extend(xdn)
            if pend_dma:
                sched.append(pend_dma.pop(0))
            n = len(bi)
            pos = {}
            for j in range(len(xi)):
                p_ = min(n - 1, (n * (2 * j + 1)) // (2 * len(xi)))
                pos.setdefault(p_, []).append(xi[j])
            for i_, it_ in enumerate(bi):
                sched.append(it_)
                for xj in pos.get(i_, []):
                    if xj[0].__name__ == 'xa' and pend_dma:
                        sched.append(pend_dma.pop(0))
                    sched.append(xj)
        for ch in range(8):
            sched.append(conv_item(8 + ch, 0, NB, S, kT_scr, S, width=2, tail=True))
        depth = 4
        for it in range(len(sched) + depth - 1):
            for k in [0, 3, 2, 1]:
                i_ = it - k
                if 0 <= i_ < len(sched) and k < len(sched[i_]):
                    sched[i_][k]()
        P.flush()
    if stop_after == 'B':
        P.finish(); em.close(); es.close(); return nc

    NG = NT * 4
    ea = sb("ea", [128, 2, NG], stack=em); thr = sb("thr", [128, 2, NG], stack=em); dec = sb("dec", [128, 2, NG], stack=em)
    one_c = sb("one_c", [128, 2], stack=em)
    with contextlib.ExitStack() as sc_:
        nlf = sb("nlf", [128, 2, NG], stack=sc_); Dn = sb("Dn", [128, 2, NG], stack=sc_)
        tmpg = sb("tmpg", [128, 2, NG], stack=sc_)
        bbp = [ps("bbp%d" % i, [128, 512], stack=sc_) for i in range(2)]
        btp = [ps("btp%d" % i, [128, 512], stack=sc_) for i in range(2)]
        P.op('dve', lambda e: e.memset(one_c[:, 0:1], 1.0), [], ['one_c'])
        P.op('dve', lambda e: e.memset(one_c[:, 1:2], float(np.log(16.0))), [], ['one_c'])
        v3 = lambda ap: ap.rearrange("p (t h) -> p t h", h=4)
        for d in range(2):
            rd = ('gate', d)
            act(v3(nlf[:, d]), GI[:, :, 4 + 8 * d:8 + 8 * d], AF.Exp, [('GI', t) for t in range(NT)], [rd], scale=-1.0)
            act(nlf[:, d], nlf[:, d], AF.Ln, [rd, 'one_c'], [rd], bias=one_c[:, 0:1])
            mm(bbp[d][:, 0:NG], tri_le if d == 0 else tri_ge, nlf[:, d], True, True, [rd, 'cst_f'], [('bbp', d)])
            mm(btp[d][:, 0:NG], ones_f, nlf[:, d], True, True, [rd, 'cst_f'], [('btp', d)])
            act(Dn[:, d], bbp[d][:, 0:NG], AF.Copy, [('bbp', d)], [rd])
            tt('dve', Dn[:, d], Dn[:, d], btp[d][:, 0:NG], ALU.subtract, [rd, ('btp', d)], [rd])
            tt('dve', v3(tmpg[:, d]), v3(Dn[:, d]), GI[:, :, 8 * d:8 * d + 4], ALU.add,
               [rd] + [('GI', t) for t in range(NT)], [rd])
            act(thr[:, d], Dn[:, d], AF.Exp, [rd, 'one_c'], [('thr', d)], bias=one_c[:, 1:2])
            act(ea[:, d], tmpg[:, d], AF.Exp, [rd], [('ea', d)])
            act(dec[:, d], btp[d][:, 0:NG], AF.Exp, [('btp', d)], [('dec', d)], scale=-1.0)
        P.flush()
    if stop_after == 'C':
        P.finish(); em.close(); es.close(); return nc

    with contextlib.ExitStack() as sd:
        kTt = sb("kTt", [128, 3, 8, 128], BF16, stack=sd); qTt = sb("qTt", [128, 3, 8, 128], BF16, stack=sd)
        vext = sb("vext", [128, 3, 4, 258], BF16, stack=sd)
        ktok = sb("ktok", [128, 2, 1024], BF16, stack=sd); vs = sb("vs", [128, 2, 4, 258], BF16, stack=sd)
        wT = sb("wT", [128, 2, 4, 128], BF16, stack=sd)
        Cacc = sb("Cacc", [128, 4, 2, 258], stack=sd); Cbf = sb("Cbf", [128, 4, 2, 258], BF16, stack=sd)
        ddt = sb("ddt", [128, 2, 8], stack=sd)
        hdir = sb("hdir", [128, 2, D], stack=sd); hBt = sb("hBt", [128, 2, D], stack=sd)
        ogl = sb("ogl", [128, 2, D], BF16, stack=sd)
        stats = sb("stats", [128, 4, 6], stack=sd); mv = sb("mv", [128, 4, 2], stack=sd); rg = sb("rg", [128, 8], stack=sd)
        ym = sb("ym", [128, 2, D], BF16, stack=sd); ymT = sb("ymT", [128, 2, 8, 128], BF16, stack=sd)
        bT = ps("bT", [128, 1024], BF16, stack=sd)
        bS = ps("bS", [128, 512], stack=sd)
        bN = [ps("bN%d" % i, [128, 512], stack=sd) for i in range(2)]
        bX = ps("bX", [128, 512], stack=sd)
        bC = [ps("bC%d" % i, [128, 512], stack=sd) for i in range(3)]
        R_ld, R_kk, R_vv, R_ww, R_bC, R_dd = Ring('ld', 3), Ring('ktok', 2), Ring('vs', 2), Ring('wT', 2), Ring('bC', 3), Ring('dd', 2)
        R_hd, R_hB, R_ogl, R_ym = (Ring(n, 2) for n in ('hdir', 'hBt', 'ogl', 'ym'))
        kT_v = kT_scr.rearrange("(c p) n -> p c n", p=128); qT_v = qT_scr.rearrange("(c p) n -> p c n", p=128)
        ymT_v = ymT_scr.rearrange("(c p) n -> p c n", p=128)
        for i in range(3):
            P.op('pool', lambda e, i=i: e.memset(vext[:, i, :, 256:258], 1.0), [], [('vext', i)])

        for d in (1, 0):
            P.op('pool', lambda e: e.memset(Cacc[:], 0.0), [], [('Cacc', h) for h in range(4)])
            mask = tri_ge if d == 1 else tri_le
            tiles = list(range(NT - 1, -1, -1)) if d == 1 else list(range(NTO))
            nst = len(tiles)
            info = {}

            def load(i):
                tile = tiles[i]; full = tile < NTO
                lb = R_ld.next()
                tk = slice(tile * 128, (tile + 1) * 128)
                dma('sp', kTt[:, lb], kT_v[:, :, tk], [], [('kTt', lb)], ('kTt', lb))
                dma('sp', vext[:, lb, :, 0:256], v_scr[tk, :].rearrange("p (h e) -> p h e", h=4), [], [('vext', lb)], ('vext', lb))
                if full:
                    dma('sp', qTt[:, lb], qT_v[:, :, tk], [], [('qTt', lb)], ('qTt', lb))
                info[i] = dict(tile=tile, full=full, lb=lb, tk=tk)

            def pe_a(i):
                st = info[i]; lb = st['lb']
                for c in range(8):
                    tr(bT[:, c * 128:(c + 1) * 128], kTt[:, lb, c, :], ident_b, [('kTt', lb), 'cst_b'], ['bT'])
                if st['full']:
                    for h in range(4):
                        for j in range(2):
                            mm(bS[:, h * 128:(h + 1) * 128], kTt[:, lb, 2 * h + j, :], qTt[:, lb, 2 * h + j, :], j == 0, j == 1,
                               [('kTt', lb), ('qTt', lb)], ['bS'])

            def ev_a(i):
                st = info[i]; lb = st['lb']; c0 = st['tile'] * 4
                st['kk'] = kk = R_kk.next(); st['vv'] = vv = R_vv.next()
                act(ktok[:, kk], bT[:], AF.Copy, ['bT'], [('ktok', kk)])
                for h in range(4):
                    tsc('dve', vs[:, vv, h, :], vext[:, lb, h, :], ea[:, d, c0 + h:c0 + h + 1], None, ALU.mult, ALU.bypass,
                        [('vext', lb), ('ea', d)], [('vs', vv)])
                if st['full']:
                    st['ww'] = ww = R_ww.next()
                    for h in range(4):
                        tt('dve', wT[:, ww, h, :], bS[:, h * 128:(h + 1) * 128], mask, ALU.mult, ['bS', 'cst_f'], [('wT', ww)])

            def pe_b(i):
                pass

            def ev_b(i):
                st = info[i]; lb = st['lb']; kk = st['kk']; vv = st['vv']
                c0 = st['tile'] * 4; tile = st['tile']; tk = st['tk']
                bc = {}

                def dC(h):
                    bc[h] = c = R_bC.next()
                    for j in range(2):
                        mm(bC[c][:, j * 256:(j + 1) * 256], ktok[:, kk, h * 256 + j * 128:h * 256 + (j + 1) * 128],
                           vs[:, vv, h, 0:256], True, True, [('ktok', kk), ('vs', vv)], [('bC', c)])

                def upd(h):
                    dcc = dec[:, d, c0 + h:c0 + h + 1]; c = bc[h]
                    stt(Cacc[:, h, :, 0:256], Cacc[:, h, :, 0:256], dcc, bC[c][:].rearrange("p (j e) -> p j e", j=2), ALU.mult, ALU.add,
                        [('Cacc', h), ('dec', d), ('bC', c)], [('Cacc', h)])

                for h in range(3):
                    dC(h)
                for h in range(4):
                    for j in range(2):
                        mm(bX[:, 8 + 4 * h + 2 * j:10 + 4 * h + 2 * j], ktok[:, kk, h * 256 + j * 128:h * 256 + (j + 1) * 128],
                           vs[:, vv, h, 256:258], True, True, [('ktok', kk), ('vs', vv)], ['bX'])
                for h in range(3):
                    upd(h)
                for h in range(4):
                    dcc = dec[:, d, c0 + h:c0 + h + 1]
                    stt(Cacc[:, h, :, 256:258], Cacc[:, h, :, 256:258], dcc,
                        bX[:, 8 + 4 * h:12 + 4 * h].rearrange("p (j e) -> p j e", j=2), ALU.mult, ALU.add,
                        [('Cacc', h), ('dec', d), 'bX'], [('Cacc', h)])
                dC(3)
                upd(3)
                if st['full']:
                    ww = st['ww']
                    for h in range(4):
                        nbk = bN[h // 2]; o0 = (h % 2) * 256
                        mm(nbk[:, o0:o0 + 256], wT[:, ww, h, :], vs[:, vv, h, 0:256], True, False, [('wT', ww), ('vs', vv)], [('bN', h // 2)])
                        for j in range(2):
                            mm(nbk[:, o0:o0 + 256], qTt[:, lb, 2 * h + j, :], Cbf[:, h, j, 0:256], False, j == 1,
                               [('qTt', lb), ('Cbf', h)], [('bN', h // 2)])
                        mm(bX[:, 2 * h:2 * h + 2], wT[:, ww, h, :], vs[:, vv, h, 256:258], True, False, [('wT', ww), ('vs', vv)], ['bX'])
                        for j in range(2):
                            mm(bX[:, 2 * h:2 * h + 2], qTt[:, lb, 2 * h + j, :], Cbf[:, h, j, 256:258], False, j == 1,
                               [('qTt', lb), ('Cbf', h)], ['bX'])
                if not st['full']:
                    return
                di = R_dd.next(); rdd = ('dd', di)
                hb = R_hd.next(); rhd = ('hdir', hb)
                act(ddt[:, di, 0:4], bX[:, 0:8].rearrange("p (h e) -> p h e", e=2)[:, :, 0], AF.Abs, ['bX'], [rdd])
                tt('dve', ddt[:, di, 0:4], ddt[:, di, 0:4], thr[:, d, c0:c0 + 4], ALU.max, [rdd, ('thr', d)], [rdd])
                P.op('dve', lambda e, di=di: e.reciprocal(out=ddt[:, di, 4:8], in_=ddt[:, di, 0:4]), [rdd], [rdd])
                for h in range(4):
                    nbk = bN[h // 2]; o0 = (h % 2) * 256
                    if d == 0:
                        stt(hdir[:, hb, h * 256:(h + 1) * 256], nbk[:, o0:o0 + 256], ddt[:, di, 4 + h:5 + h],
                            hBt[:, st['hBb'], h * 256:(h + 1) * 256], ALU.mult, ALU.add,
                            [('bN', h // 2), rdd, ('hBt', st['hBb'])], [rhd])
                    elif h % 2 == 0:
                        act(hdir[:, hb, h * 256:(h + 1) * 256], nbk[:, o0:o0 + 256], AF.Copy, [('bN', h // 2), rdd], [rhd],
                            scale=ddt[:, di, 4 + h:5 + h])
                    else:
                        tsc('dve', hdir[:, hb, h * 256:(h + 1) * 256], nbk[:, o0:o0 + 256], ddt[:, di, 4 + h:5 + h], None,
                            ALU.mult, ALU.bypass, [('bN', h // 2), rdd], [rhd])
                if d == 1:
                    dma('sp', hB_scr[tk, :], hdir[:, hb], [rhd], [('hB_scr', tile)], rhd)
                    return
                bb_ = st['hBb']; rhB = ('hBt', bb_); ob = st['ogb']; rol = ('ogl', ob)
                for h in range(4):
                    P.op('dve', lambda e, h=h, hb=hb: e.bn_stats(out=stats[:, h, :], in_=hdir[:, hb, h * 256:(h + 1) * 256]),
                         [rhd], [('stats', h)])
                    P.op('dve', lambda e, h=h: e.bn_aggr(out=mv[:, h, :], in_=stats[:, h, :]), [('stats', h)], [('mv', h)])
                act(rg[:, 0:4], mv[:, :, 1], AF.Sqrt, [('mv', h) for h in range(4)] + ['eps_c'], ['rg'], bias=eps_c[:, 0:1])
                P.op('dve', lambda e: e.reciprocal(out=rg[:, 4:8], in_=rg[:, 0:4]), ['rg'], ['rg'])
                for h in range(4):
                    tsc('dve', hdir[:, hb, h * 256:(h + 1) * 256], hdir[:, hb, h * 256:(h + 1) * 256], mv[:, h, 0:1],
                        rg[:, 4 + h:5 + h], ALU.subtract, ALU.mult, [rhd, ('mv', h), 'rg'], [rhd])
                yb = R_ym.next(); rym = ('ym', yb)
                tt('dve', ym[:, yb], hdir[:, hb], ogl[:, ob], ALU.mult, [rhd, rol], [rym])
                st['yb'] = yb

            def pe_c(i):
                st = info[i]
                if d != 0 or not st['full']:
                    return
                yb = st['yb']; rym = ('ym', yb); rymT = ('ymT', yb)
                for k in range(8):
                    tr(bT[:, k * 128:(k + 1) * 128], ym[:, yb, k * 128:(k + 1) * 128], ident_b, [rym, 'cst_b'], ['bT'])
                for k in range(8):
                    act(ymT[:, yb, k, :], bT[:, k * 128:(k + 1) * 128], AF.Copy, ['bT', 'gncol'], [rymT], scale=gncol[:, k:k + 1])
                dma('sp', ymT_v[:, :, st['tk']], ymT[:, yb], [rymT], [('ymT_scr', st['tile'] // 4)], rymT)

            def cbf(i):
                st = info[i]
                if not st['full']:
                    return
                c0 = st['tile'] * 4
                for h in range(4):
                    act(Cbf[:, h], Cacc[:, h], AF.Copy, [('Cacc', h), ('dec', d)], [('Cbf', h)], scale=dec[:, d, c0 + h:c0 + h + 1])

            def epi_loads(i):
                st = info[i]
                if d == 0 and st['full']:
                    st['hBb'] = bb_ = R_hB.next(); st['ogb'] = ob = R_ogl.next()
                    dma('sp', hBt[:, bb_], hB_scr[st['tk'], :], [('hB_scr', st['tile'])], [('hBt', bb_)], ('hBt', bb_))
                    dma('sp', ogl[:, ob], og_scr[st['tk'], :], [], [('ogl', ob)], ('ogl', ob))

            load(0)
            for i in range(nst + 2):
                if i + 1 < nst:
                    load(i + 1)
                if i < nst:
                    epi_loads(i)
                    pe_a(i)
                    ev_a(i)
                if 0 <= i - 2 < nst:
                    pe_c(i - 2)
                if 0 <= i - 1 < nst:
                    pe_b(i - 1)
                    ev_b(i - 1)
                if i < nst:
                    cbf(i)
        P.flush()
    if stop_after == 'D':
        P.finish(); em.close(); es.close(); return nc

    with contextlib.ExitStack() as se:
        KT = sb("KT", [128, 2, S], BF16, stack=se); VA = sb("VA", [128, NT, 256], BF16, stack=se)
        for g in range(2):
            dma('sp', KT[:, g, :], KT_scr[g * 128:(g + 1) * 128, :], [], [('KT', g)], ('KT', g))
        VA_v = VA_scr.rearrange("(t p) c -> p t c", p=128)
        nv = max(1, NT // 8)
        for i in range(0, NT, nv):
            dma('sp', VA[:, i:i + nv, :], VA_v[:, i:i + nv, :], [], [('VA', i)], ('VA', i))
        vres = lambda k: ('VA', (k // nv) * nv)
        QTt = sb("QTt", [128, 3, 512], BF16, stack=se); pT = sb("pT", [128, 6, 2, 512], BF16, stack=se)
        rden = sb("rden", [128, 2, 512], stack=se); yo = sb("yo", [128, 2, 512], BF16, stack=se)
        accd = sb("accd", [128, 2, 2, 512], BF16, stack=se)
        scp = [ps("scp%d" % i, [128, 1024], stack=se) for i in range(2)]
        op2 = [ps("op%d" % i, [128, 512], stack=se) for i in range(2)]
        dn2_ = [ps("dnp%d" % i, [128, 512], stack=se) for i in range(2)]
        R_Q, R_sc, R_pT, R_yo, R_acc = Ring('QTt', 3), Ring('scp', 2), Ring('pT', 6), Ring('yo', 2), Ring('accd', 2)
        att_scale = float(DH_A) ** -0.5
        NP = NT // 2
        heads = [(qb, hd) for qb in range(NBO) for hd in range(8)]
        qslot = {}

        def qload(ix):
            qb_, hd_ = heads[ix]
            qslot[ix] = s_ = R_Q.next()
            dma('sp', QTt[:, s_], QT_scr[hd_ * 128:(hd_ + 1) * 128, qb_ * 512:(qb_ + 1) * 512], [], [('QTt', s_)], ('QTt', s_))

        qload(0)
        for ix, (qb, hd) in enumerate(heads):
            if True:
                g = hd // 4
                if ix + 1 < len(heads):
                    qload(ix + 1)
                q_ = qslot[ix]; rQ = ('QTt', q_)
                ab = R_acc.next()
                op_ = op2[ab]; dnp = dn2_[ab]
                ro = ('op', ab); rdn = ('dnp', ab)
                pbuf = {}
                first = {'dve': True, 'pe': True}
                for step in range(NP + 2):
                    if step < NP:
                        si = R_sc.next(); pi = R_pT.next()
                        for u in range(2):
                            kb = 2 * step + u
                            mm(scp[si][:, u * 512:(u + 1) * 512], KT[:, g, kb * 128:(kb + 1) * 128], QTt[:, q_], True, True,
                               [('KT', g), rQ], [('scp', si)])
                        act(pT[:, pi].rearrange("p u n -> p (u n)"), scp[si][:], AF.Exp, [('scp', si)], [('pT', pi)],
                            scale=att_scale)
                        pbuf[step] = pi
                    p_ = step - 2
                    if p_ >= 0:
                        pi = pbuf.pop(p_)
                        for u in range(2):
                            k = 2 * p_ + u
                            mm(op_[:], VA[:, k, g * 128:(g + 1) * 128], pT[:, pi, u], k == 0, k == NT - 1,
                               [vres(k), ('pT', pi)], [ro])
                        for u in range(2):
                            k = 2 * p_ + u
                            who = 'dve'
                            if who == 'pe':
                                mm(dnp[:], ones_b, pT[:, pi, u], first['pe'], False, ['cst_b', ('pT', pi)], [rdn])
                            else:
                                ra = ('accd', ab, 0)
                                if first[who]:
                                    cp(who, accd[:, ab, 0], pT[:, pi, u], [('pT', pi)], [ra])
                                else:
                                    tt(who, accd[:, ab, 0], accd[:, ab, 0], pT[:, pi, u], ALU.add, [ra, ('pT', pi)], [ra])
                            first[who] = False
                mm(dnp[:], ones_b, accd[:, ab, 0], True, True, ['cst_b', ('accd', ab, 0)], [rdn])
                yb = R_yo.next(); ry = ('yo', yb)
                P.op('dve', lambda e, yb=yb, dnp=dnp: e.reciprocal(out=rden[:, yb], in_=dnp[:]), [rdn], [('rden', yb)])
                tt('dve', yo[:, yb], op_[:], rden[:, yb], ALU.mult, [ro, ('rden', yb)], [ry])
                dma('sp', yaT_scr[hd * 128:(hd + 1) * 128, qb * 512:(qb + 1) * 512], yo[:, yb], [ry], [('yaT_scr', qb)], ry)
        P.flush()
    em.close()
    if stop_after == 'E':
        P.finish(); em.close(); es.close(); return nc

    def load_w(dst, src_ap, name, nsplit):
        n = src_ap.shape[-1]
        step = n // nsplit
        for i in range(nsplit):
            dma('pool', dst[:, :, i * step:(i + 1) * step], src_ap[:, :, i * step:(i + 1) * step], [], [(name, i)], (name, i), **CAST)
        return [(name, i) for i in range(nsplit)]

    def prenorm_T(src_tile, rsrc, gs, sh, dstT, rdst, t, junk_, ssb, hn_, tp_, R_tp_, names):
        stt(junk_, src_tile, 1.0, src_tile, ALU.mult, ALU.mult, [rsrc], [names + 'junk', names + 'ss0'], accum_out=ssb[:, 0:1])
        act(ssb[:, 1:2], ssb[:, 0:1], AF.Sqrt, [names + 'ss0', 'eps_c'], [names + 'ss1'], scale=1.0 / D, bias=eps_c[:, 0:1])
        P.op('dve', lambda e: e.reciprocal(out=ssb[:, 2:3], in_=ssb[:, 1:2]), [names + 'ss1'], [names + 'ss2'])
        act(hn_, src_tile, AF.Copy, [rsrc, names + 'ss2'], [names + 'hn'], scale=ssb[:, 2:3])
        for half in range(2):
            tb = R_tp_.next(); rt = (names + 'tp', tb)
            for kk in range(4):
                k = half * 4 + kk
                tr(tp_[tb][:, kk * 128:(kk + 1) * 128], hn_[:, k * 128:(k + 1) * 128], ident_f, [names + 'hn', 'cst_f'], [rt])
            for kk in range(4):
                k = half * 4 + kk
                if kk % 2:
                    act(dstT[:, k, t * 128:(t + 1) * 128], tp_[tb][:, kk * 128:(kk + 1) * 128], AF.Identity,
                        [rt], [rdst], scale=gs[:, k:k + 1], bias=sh[:, k:k + 1])
                else:
                    tsc('dve', dstT[:, k, t * 128:(t + 1) * 128], tp_[tb][:, kk * 128:(kk + 1) * 128], gs[:, k:k + 1],
                        sh[:, k:k + 1], ALU.mult, ALU.add, [rt], [rdst])

    def postnorm_res(po, rpo, Gt, res_tile, rres, out_tile, rout, junk_, ssb, names):
        for half in range(2):
            act(junk_[:, half * 512:(half + 1) * 512], po[half][:], AF.Square, [rpo[half]], [names + 'pj%d' % half, names + 'ps%d' % half],
                accum_out=ssb[:, half:half + 1])
        tt('dve', ssb[:, 2:3], ssb[:, 0:1], ssb[:, 1:2], ALU.add, [names + 'ps0', names + 'ps1'], [names + 'ps2'])
        act(ssb[:, 3:4], ssb[:, 2:3], AF.Sqrt, [names + 'ps2', 'eps_c'], [names + 'ps3'], scale=1.0 / D, bias=eps_c[:, 0:1])
        P.op('dve', lambda e: e.reciprocal(out=ssb[:, 4:5], in_=ssb[:, 3:4]), [names + 'ps3'], [names + 'ps4'])
        for half in range(2):
            stt(out_tile[:, half * 512:(half + 1) * 512], po[half][:], ssb[:, 4:5], Gt[:, half * 512:(half + 1) * 512],
                ALU.mult, ALU.mult, [rpo[half], names + 'ps4'], [rout])
        tt('pool', out_tile, out_tile, res_tile, ALU.add, [rout, rres], [rout])

    with contextlib.ExitStack() as sf:
        wbm = sb("wbm", [128, 8, D], BF16, stack=sf); wba = sb("wba", [128, 8, D], BF16, stack=sf)
        wo = sb("wo", [128, 8, D], BF16, stack=sf); wbr = sb("wbr", [128, 8, 2 * D], BF16, stack=sf)
        kp = lambda a: a.rearrange("(k p) n -> p k n", p=128)
        r_wbm = load_w(wbm, kp(w_bm), 'wbm', 1); r_wba = load_w(wba, kp(w_ba), 'wba', 1)
        r_wbr = load_w(wbr, kp(w_br), 'wbr', 2); r_wo = load_w(wo, kp(w_out), 'wo', 1)
        hTb = sb("hTb", [128, 2, 8, 512], BF16, stack=sf); ymTb = sb("ymTb", [128, 2, 8, 512], BF16, stack=sf)
        yaTb = sb("yaTb", [128, 2, 8, 512], BF16, stack=sf)
        gmt = sb("gmt", [128, 2, 2, 512], stack=sf)
        yT = sb("yT", [128, 1, 8, 512], BF16, stack=sf)
        xt1 = sb("xt1", [128, 3, D], stack=sf); x1t = sb("x1t", [128, 2, D], stack=sf)
        junk1 = sb("junk1", [128, D], BF16, stack=sf); ssf = sb("ssf", [128, 2, 8], stack=sf)
        h2T = sb("h2T", [128, 1, 8, 512], BF16, stack=sf)
        pbr = [ps("pbr%d" % i, [128, 512], stack=sf) for i in range(4)]
        po1 = [ps("po1%d" % i, [128, 512], stack=sf) for i in range(2)]
        tp1 = [ps("tp1%d" % i, [128, 512], stack=sf) for i in range(2)]
        R_blk, R_g, R_yT, R_x1, R_tp1, R_h2 = Ring('f1blk', 2), Ring('gmt', 2), Ring('yT', 1), Ring('x1', 2), Ring('f1tp', 2), Ring('h2T', 1)
        hT_v = kp(hT_scr); ymT_v2 = kp(ymT_scr); yaT_v = kp(yaT_scr); h2T_v = kp(h2T_scr)
        hn4 = sb("hn4", [128, 4, D], stack=sf)
        ss4 = sb("ss4", [128, 4, 4], stack=sf)
        bslot = {}

        def f1_loads(qb):
            cs_ = slice(qb * 512, (qb + 1) * 512)
            bslot[qb] = b = R_blk.next()
            dma('sp', hTb[:, b], hT_v[:, :, cs_], [('hT_scr', qb)], [('hTb', b)], ('hTb', b))
            dma('sp', ymTb[:, b], ymT_v2[:, :, cs_], [('ymT_scr', qb)], [('ymTb', b)], ('ymTb', b))
            dma('sp', yaTb[:, b], yaT_v[:, :, cs_], [('yaT_scr', qb)], [('yaTb', b)], ('yaTb', b))

        def f1_branch(qb):
            b = bslot[qb]
            ryT = ('yT', 0)
            for fo in range(8):
                fs = slice(fo * 128, (fo + 1) * 128)
                for k in range(8):
                    mm(pbr[2][:], wbr[:, k, fs], hTb[:, b, k, :], k == 0, k == 7, r_wbr + [('hTb', b)], [('pbr', 2)])
                for k in range(8):
                    mm(pbr[3][:], wbr[:, k, D + fo * 128:D + (fo + 1) * 128], hTb[:, b, k, :], k == 0, k == 7,
                       r_wbr + [('hTb', b)], [('pbr', 3)])
                for k in range(8):
                    mm(pbr[0][:], wbm[:, k, fs], ymTb[:, b, k, :], k == 0, k == 7, r_wbm + [('ymTb', b)], [('pbr', 0)])
                for k in range(8):
                    mm(pbr[1][:], wba[:, k, fs], yaTb[:, b, k, :], k == 0, k == 7, r_wba + [('yaTb', b)], [('pbr', 1)])
                gi = R_g.next(); rg_ = ('gmt', gi)
                act(gmt[:, gi, 0], pbr[2][:], AF.Sigmoid, [('pbr', 2)], [rg_])
                act(gmt[:, gi, 1], pbr[3][:], AF.Sigmoid, [('pbr', 3)], [rg_])
                tt('dve', gmt[:, gi, 0], pbr[0][:], gmt[:, gi, 0], ALU.mult, [('pbr', 0), rg_], [rg_])
                tt('dve', gmt[:, gi, 1], pbr[1][:], gmt[:, gi, 1], ALU.mult, [('pbr', 1), rg_], [rg_])
                tt('pool', yT[:, 0, fo, :], gmt[:, gi, 0], gmt[:, gi, 1], ALU.add, [rg_], [ryT])

        R_xl = Ring('xt1', 3)
        xls = {}

        def f1_xload(tile):
            xls[tile] = s_ = R_xl.next()
            dma('sp', xt1[:, s_], x[tile * 128:(tile + 1) * 128, :], [], [('xt1', s_)], ('xt1', s_))

        def f1_outproj(qb):
            ryT = ('yT', 0)
            if qb == 0:
                f1_xload(0)
            for t in range(4):
                tile = qb * 4 + t
                tk = slice(tile * 128, (tile + 1) * 128)
                if tile + 1 < NTO:
                    f1_xload(tile + 1)
                pp = [po1[0], po1[1]] if t % 2 == 0 else [pbr[0], pbr[1]]
                rpp = [('po1', 0), ('po1', 1)] if t % 2 == 0 else [('pbr', 0), ('pbr', 1)]
                for half in range(2):
                    for k in range(8):
                        mm(pp[half][:], yT[:, 0, k, t * 128:(t + 1) * 128], wo[:, k, half * 512:(half + 1) * 512],
                           k == 0, k == 7, [ryT] + r_wo, [rpp[half]])
                xb = R_x1.next(); rx1 = ('x1t', xb)
                xs_ = xls.pop(tile); rx = ('xt1', xs_)
                postnorm_res(pp, rpp, G1, xt1[:, xs_], rx, x1t[:, xb], rx1, junk1, ssf[:, 0], 'f1')
                dma('sp', x1_scr[tk, :], x1t[:, xb], [rx1], [('x1_scr', tile)], rx1)
                stt(junk1[:], x1t[:, xb], 1.0, x1t[:, xb], ALU.mult, ALU.mult, [rx1], ['f1njunk', ('f1ss0', t)], accum_out=ss4[:, t, 0:1])
                act(ss4[:, t, 1:2], ss4[:, t, 0:1], AF.Sqrt, [('f1ss0', t), 'eps_c'], [('f1ss1', t)], scale=1.0 / D, bias=eps_c[:, 0:1])
                P.op('dve', lambda e, t=t: e.reciprocal(out=ss4[:, t, 2:3], in_=ss4[:, t, 1:2]), [('f1ss1', t)], [('f1ss2', t)])
                act(hn4[:, t], x1t[:, xb], AF.Copy, [rx1, ('f1ss2', t)], [('hn4', t)], scale=ss4[:, t, 2:3])

        def f1_trans(qb):
            cs_ = slice(qb * 512, (qb + 1) * 512)
            rh2 = ('h2T', 0)
            tpb = [tp1[0], tp1[1], pbr[2], pbr[3]]
            rtpb = [('f1tp', 0), ('f1tp', 1), ('pbr', 2), ('pbr', 3)]
            for t in range(4):
                for half in range(2):
                    tb = (t * 2 + half) % 4; rt = rtpb[tb]
                    for kk in range(4):
                        k = half * 4 + kk
                        tr(tpb[tb][:, kk * 128:(kk + 1) * 128], hn4[:, t, k * 128:(k + 1) * 128], ident_f, [('hn4', t), 'cst_f'], [rt])
                    for kk in range(4):
                        k = half * 4 + kk
                        if kk % 2:
                            act(h2T[:, 0, k, t * 128:(t + 1) * 128], tpb[tb][:, kk * 128:(kk + 1) * 128], AF.Identity,
                                [rt], [rh2], scale=gs2[:, k:k + 1], bias=sh2[:, k:k + 1])
                        else:
                            tsc('dve', h2T[:, 0, k, t * 128:(t + 1) * 128], tpb[tb][:, kk * 128:(kk + 1) * 128], gs2[:, k:k + 1],
                                sh2[:, k:k + 1], ALU.mult, ALU.add, [rt], [rh2])
            dma('sp', h2T_v[:, :, cs_], h2T[:, 0], [rh2], [('h2T_scr', qb)], rh2)

        f1_loads(0)
        f1_branch(0)
        for qb in range(NBO):
            if qb + 1 < NBO:
                f1_loads(qb + 1)
            f1_outproj(qb)
            if qb + 1 < NBO:
                f1_branch(qb + 1)
            f1_trans(qb)
        P.flush()
    if stop_after == 'F1':
        P.finish(); em.close(); es.close(); return nc

    with contextlib.ExitStack() as sg:
        w1 = sb("w1", [128, 8, DFF], BF16, stack=sg); w2 = sb("w2", [128, 32, D], BF16, stack=sg)
        r_w1 = load_w(w1, w_m1.rearrange("(k p) n -> p k n", p=128), 'w1', 4)
        r_w2 = load_w(w2, w_m2.rearrange("(c p) n -> p c n", p=128), 'w2', 1)
        h2Tb = sb("h2Tb", [128, 1, 8, 512], BF16, stack=sg)
        rl = sb("rl", [128, 2, 512], stack=sg); uT = sb("uT", [128, 32, 512], BF16, stack=sg)
        x1l = sb("x1l", [128, 2, D], stack=sg); ot = sb("ot", [128, 2, D], stack=sg)
        junk2 = sb("junk2", [128, D], BF16, stack=sg); ssg = sb("ssg", [128, 8], stack=sg)
        pu = [ps("pu%d" % i, [128, 512], stack=sg) for i in range(3)]
        po2 = [[ps("po2%d%d" % (i, j), [128, 512], stack=sg) for j in range(2)] for i in range(2)]
        R_b2, R_pu, R_rl, R_po2, R_x2 = Ring('f2blk', 1), Ring('pu', 3), Ring('rl', 2), Ring('po2', 2), Ring('x2', 2)
        for qb in range(NBO):
            cs_ = slice(qb * 512, (qb + 1) * 512)
            b = R_b2.next(); rb = ('h2Tb', b)
            dma('sp', h2Tb[:, b], h2T_v[:, :, cs_], [('h2T_scr', qb)], [rb], rb)
            for fc in range(32):
                pi = R_pu.next(); rp = ('pu', pi)
                for k in range(8):
                    mm(pu[pi][:], w1[:, k, fc * 128:(fc + 1) * 128], h2Tb[:, b, k, :], k == 0, k == 7,
                       [r_w1[fc // 8], rb], [rp])
                ri = R_rl.next(); rr_ = ('rl', ri)
                act(rl[:, ri], pu[pi][:], AF.Relu, [rp], [rr_])
                tt('pool' if fc % 2 else 'dve', uT[:, fc, :], rl[:, ri], rl[:, ri], ALU.mult, [rr_], [('uT', fc)])
            for t in range(4):
                tile = qb * 4 + t
                tk = slice(tile * 128, (tile + 1) * 128)
                pp = R_po2.next()
                for half in range(2):
                    for fc in range(32):
                        mm(po2[pp][half][:], uT[:, fc, t * 128:(t + 1) * 128], w2[:, fc, half * 512:(half + 1) * 512],
                           fc == 0, fc == 31, [('uT', fc)] + r_w2, [('po2', pp, half)])
                xb = R_x2.next(); rx = ('x1l', xb); rot = ('ot', xb)
                dma('sp', x1l[:, xb], x1_scr[tk, :], [('x1_scr', tile)], [rx], rx)
                postnorm_res(po2[pp], [('po2', pp, 0), ('po2', pp, 1)], G2, x1l[:, xb], rx, ot[:, xb], rot, junk2, ssg, 'f2')
                dma('sp', y_out[tk, :], ot[:, xb], [rot], [], rot)
    P.finish()
    es.close()
    return nc


def _host_consts():
    cst = np.zeros((128, K_END), np.float32)
    cst[:, K_ID:K_ID + 128] = np.eye(128, dtype=np.float32)
    cst[:, K_ONE:K_ONE + 128] = 1.0
    s = np.arange(128)[:, None]; t = np.arange(128)[None, :]
    cst[:, K_LE:K_LE + 128] = (s <= t)
    cst[:, K_GE:K_GE + 128] = (s >= t)
    R = np.zeros((128, 128), np.float32)
    for j in range(128):
        if (j % 64) < 32:
            R[j, j + 32] = -1.0
        else:
            R[j, j - 32] = 1.0
    cst[:, K_ROT:K_ROT + 128] = R.T
    return cst


def _rope_tables(S, flip):
    pos = np.arange(S)
    if flip:
        pos = S - 1 - pos
    row = (pos // 64).astype(np.float32); col = (pos % 64).astype(np.float32)
    freqs = (np.float32(10000.0) ** (-np.arange(32, dtype=np.float32) / np.float32(32))).astype(np.float32)
    j = np.arange(128)
    p = np.where((j // 64)[:, None] == 0, row[None, :], col[None, :]).astype(np.float32)
    ang = (p * freqs[j % 32][:, None]).astype(np.float32)
    return np.cos(ang).astype(np.float32), np.sin(ang).astype(np.float32)


def core_inputs(inp, core, S):
    b, flip = core // 2, (core % 2) == 1
    f32 = lambda a: np.ascontiguousarray(a, dtype=np.float32)
    xs = inp['x'][b][:S]
    if flip:
        xs = xs[::-1]
    w = inp['w_in'][0]
    offs = np.cumsum([0, 2048, 1024, 1024, 16, 1024, 256, 256, 2048])
    qk, v, o, g, qa, ka, va, br = [w[:, offs[i]:offs[i + 1]] for i in range(8)]
    bgt = inp['b_gates'][0]
    cw = inp['conv_w'][0]
    if flip:
        perm = list(range(8, 16)) + list(range(0, 8))
        g = g[:, perm]; bgt = bgt[perm]; cw = cw[::-1]
    w_dev = np.concatenate([qk, qa, ka, v, o, va, g], axis=1)
    cos_t, sin_t = _rope_tables(S, flip)
    return {
        'x': f32(xs), 'c': f32(inp['c'][b]), 'w_ada': f32(inp['w_ada'][0]), 'b_ada': f32(inp['b_ada'][0]),
        'norm1_pre': f32(inp['norm1_pre'][0]), 'norm1_post': f32(inp['norm1_post'][0]),
        'norm2_pre': f32(inp['norm2_pre'][0]), 'norm2_post': f32(inp['norm2_post'][0]),
        'w_in': f32(w_dev), 'w_br': f32(br), 'b_gates': f32(bgt), 'conv_w': f32(cw.T), 'conv_b': f32(inp['conv_b'][0]),
        'mlstm_gn': f32(inp['mlstm_gn'][0]), 'attn_qnorm': f32(inp['attn_qnorm'][0]), 'attn_knorm': f32(inp['attn_knorm'][0]),
        'w_branch_m': f32(inp['w_branch_m'][0]), 'w_branch_a': f32(inp['w_branch_a'][0]), 'w_out': f32(inp['w_out'][0]),
        'w_mlp_in': f32(inp['w_mlp_in'][0]), 'w_mlp_out': f32(inp['w_mlp_out'][0]),
        'cos_t': cos_t, 'sin_t': sin_t, 'cst': _host_consts(),
    }


def run(inp, S=8192, debug=(), stop_after=None, trace=False):
    inp = {k: np.asarray(v) for k, v in inp.items()}
    nc = build(S, debug, stop_after)
    in_maps = [core_inputs(inp, core, S) for core in range(8)]
    res = run_bass_kernel_spmd(nc, in_maps, core_ids=list(range(8)), **({'trace': True} if trace else {}))
    return res


def kernel(**inputs):
    S = 8192
    res = run(inputs, S)
    out = np.empty((4, S, D), np.float32)
    for core in range(8):
        b, flip = core // 2, (core % 2) == 1
        y = np.asarray(res.results[core]['y'], dtype=np.float32)
        if flip:
            out[b, S // 2:] = y[::-1]
        else:
            out[b, :S // 2] = y
    return out
```

```python
import contextlib
import numpy as np
import concourse.bass as bass
import concourse.mybir as mybir
from concourse.bass_utils import run_bass_kernel_spmd

F32 = mybir.dt.float32
BF16 = mybir.dt.bfloat16
AF = mybir.ActivationFunctionType
ALU = mybir.AluOpType

ENGS = ('pe', 'act', 'dve', 'pool', 'sp')


class _Op:
    __slots__ = ('eng', 'fn', 'deps', 'seq', 'blk', 'signal', 'ticket', 'dma', 'sem', 'val', 'key')


class Prog:
    def __init__(self, nc):
        self.nc = nc
        self.stack = contextlib.ExitStack()
        self.engs = {'pe': nc.tensor, 'act': nc.scalar, 'dve': nc.vector, 'pool': nc.gpsimd, 'sp': nc.sync}
        self.sem = {e: self.stack.enter_context(nc.semaphore('sem_' + e)) for e in ENGS}
        self.count = {e: 0 for e in ENGS}
        self.ops = {e: [] for e in ENGS}
        self.seq = {e: 0 for e in ENGS}
        self.last_write = {}
        self.readers = {}
        self.seen = {e: {} for e in ENGS}
        self.dma_sems = {}
        self.free_dma = []
        self.n_dsem = 0
        self.blk = 0
        self.n_instr = 0

    def _deps(self, o, reads, writes):
        deps = []
        for r in reads:
            lw = self.last_write.get(r)
            if lw is not None:
                deps.append((lw, 'raw'))
        for w in writes:
            lw = self.last_write.get(w)
            if lw is not None:
                deps.append((lw, 'waw'))
            for rd in self.readers.get(w, ()):
                deps.append((rd, 'war'))
        out = []
        seen = self.seen[o.eng]
        for d, kind in deps:
            if d is o:
                continue
            if d.dma:
                if d.blk != self.blk:
                    continue
                k = ('dma', d.key)
                if seen.get(k, 0) >= d.val:
                    continue
                seen[k] = d.val
                out.append(d)
            else:
                if d.blk != self.blk:
                    continue
                if d.eng == o.eng and not o.dma:
                    if o.eng == 'pe' or kind == 'war':
                        continue
                if seen.get(d.eng, -1) >= d.seq:
                    continue
                seen[d.eng] = d.seq
                out.append(d)
        o.deps = out
        for r in reads:
            lst = self.readers.setdefault(r, [])
            lst[:] = [x for x in lst if not (x.eng == o.eng and not x.dma and not o.dma)]
            lst.append(o)
        for w in writes:
            self.last_write[w] = o
            self.readers[w] = []

    def op(self, eng, fn, reads=(), writes=()):
        o = _Op()
        o.eng, o.fn, o.dma, o.signal, o.blk = eng, fn, False, False, self.blk
        o.seq = self.seq[eng]
        self.seq[eng] += 1
        self._deps(o, reads, writes)
        self.ops[eng].append(o)
        return o

    def dma(self, eng, fn, reads=(), writes=(), key=None):
        assert key is not None
        o = _Op()
        o.eng, o.fn, o.dma, o.signal, o.blk, o.key = eng, fn, True, False, self.blk, key
        o.seq = self.seq[eng]
        self.seq[eng] += 1
        if key not in self.dma_sems:
            if self.free_dma:
                self.dma_sems[key] = self.free_dma.pop()
            else:
                self.n_dsem += 1
                self.dma_sems[key] = [self.stack.enter_context(self.nc.semaphore('dsem%d' % self.n_dsem)), 0]
        ent = self.dma_sems[key]
        ent[1] += 16
        o.sem, o.val = ent[0], ent[1]
        self._deps(o, reads, writes)
        self.ops[eng].append(o)
        return o

    def flush(self, final=False):
        for e in ENGS:
            for o in self.ops[e]:
                for d in o.deps:
                    if not d.dma:
                        d.signal = True
        for e in ENGS:
            c = self.count[e]
            for o in self.ops[e]:
                if not o.dma and o.signal:
                    c += 1
                    o.ticket = c
            self.count[e] = c
        tail = []
        for key, (sem, tot) in self.dma_sems.items():
            if tot:
                tail.append((sem, tot))
        with self.nc.Block() as block:
            regs = {'pe': block.tensor, 'act': block.scalar, 'dve': block.vector,
                    'pool': block.gpsimd, 'sp': block.sync}
            for e in ENGS:
                ops = self.ops[e]
                if not ops and e != 'sp':
                    continue

                def body(eng, ops=ops, e=e):
                    for o in ops:
                        for d in o.deps:
                            if d.dma:
                                eng.wait_ge(d.sem, d.val)
                            else:
                                eng.wait_ge(self.sem[d.eng], d.ticket)
                        ins = o.fn(eng)
                        self.n_instr += 1
                        if o.dma:
                            ins.then_inc(o.sem, 16)
                        elif o.signal:
                            ins.then_inc(self.sem[e], 1)
                    if e == 'sp':
                        for sem, tot in tail:
                            eng.wait_ge(sem, tot)

                regs[e](body)
        self.ops = {e: [] for e in ENGS}
        self.seen = {e: {} for e in ENGS}
        self.free_dma.extend(self.dma_sems.values())
        self.dma_sems = {}
        self.blk += 1

    def finish(self):
        self.flush(final=True)
        self.stack.close()


D = 1024
NH_M = 4
DH_M = 256
NQ_A = 8
NKV_A = 2
DH_A = 128
DFF = 4096
EPS = 1e-6
C_QM, C_KM, C_QA, C_KA, C_VM, C_OM, C_VA, C_GT, C_END = 0, 1024, 2048, 3072, 3328, 4352, 5376, 5632, 5648
K_ID, K_ONE, K_LE, K_GE, K_ROT, K_END = 0, 128, 256, 384, 512, 640


class Ring:
    def __init__(self, name, n):
        self.name, self.n, self.i = name, n, -1

    def next(self):
        self.i = (self.i + 1) % self.n
        return self.i

    def res(self, i=None):
        return (self.name, self.i if i is None else i)


def build(S=8192, debug=(), stop_after=None):
    SO, NT, NB = S // 2, S // 128, S // 512
    NTO, NBO = NT // 2, NB // 2
    nc = bass.Bass("TRN2", target_bir_lowering=False)
    P = Prog(nc)

    def din(name, shape):
        return nc.dram_tensor(name, shape, F32, kind="ExternalInput").ap()

    def dscr(name, shape, dt):
        kind = "ExternalOutput" if name in debug else "Internal"
        return nc.dram_tensor(name, shape, dt, kind=kind).ap()

    x = din("x", [S, D]); c_in = din("c", [D]); w_ada = din("w_ada", [D, 6 * D]); b_ada = din("b_ada", [6 * D])
    n1pre = din("norm1_pre", [D]); n1post = din("norm1_post", [D]); n2pre = din("norm2_pre", [D]); n2post = din("norm2_post", [D])
    w_in = din("w_in", [D, C_END]); w_br = din("w_br", [D, 2 * D]); b_gates = din("b_gates", [16])
    conv_w = din("conv_w", [2 * D, 5]); conv_b = din("conv_b", [2 * D]); gn_w = din("mlstm_gn", [D])
    qn_w = din("attn_qnorm", [128]); kn_w = din("attn_knorm", [128])
    w_bm = din("w_branch_m", [D, D]); w_ba = din("w_branch_a", [D, D]); w_out = din("w_out", [D, D])
    w_m1 = din("w_mlp_in", [D, DFF]); w_m2 = din("w_mlp_out", [DFF, D])
    cos_t = din("cos_t", [128, S]); sin_t = din("sin_t", [128, S]); cst = din("cst", [128, K_END])
    y_out = nc.dram_tensor("y", [SO, D], F32, kind="ExternalOutput").ap()

    mod_scr = dscr("mod_scr", [6 * D], F32)
    hT_scr = dscr("hT_scr", [D, SO], BF16)
    qT_scr = dscr("qT_scr", [D, SO], BF16)
    kT_scr = dscr("kT_scr", [D, S], BF16)
    v_scr = dscr("v_scr", [S, D], BF16)
    og_scr = dscr("og_scr", [SO, D], BF16)
    QT_scr = dscr("QT_scr", [D, SO], BF16)
    KT_scr = dscr("KT_scr", [2 * 128, S], BF16)
    VA_scr = dscr("VA_scr", [S, 256], BF16)
    hB_scr = dscr("hB_scr", [SO, D], F32)
    ymT_scr = dscr("ymT_scr", [D, SO], BF16)
    yaT_scr = dscr("yaT_scr", [D, SO], BF16)
    x1_scr = dscr("x1_scr", [SO, D], F32)
    h2T_scr = dscr("h2T_scr", [D, SO], BF16)

    es = contextlib.ExitStack()
    em = contextlib.ExitStack()

    def sb(name, shape, dt=F32, stack=None):
        return (stack or es).enter_context(nc.sbuf_tensor(name, shape, dt))

    def ps(name, shape, dt=F32, stack=None):
        return (stack or es).enter_context(nc.psum_tensor(name, shape, dt))

    def mm(out, lhsT, rhs, start, stop, reads, writes):
        P.op('pe', lambda e: e.matmul(out, lhsT=lhsT, rhs=rhs, start=start, stop=stop), reads, writes)

    def tr(out, in_, ident, reads, writes):
        P.op('pe', lambda e: e.transpose(out, in_, ident), reads, writes)

    def act(out, in_, func, reads, writes, **kw):
        P.op('act', lambda e: e.activation(out=out, in_=in_, func=func, **kw), reads, writes)

    def tt(eng, out, a, b, op, reads, writes):
        P.op(eng, lambda e: e.tensor_tensor(out=out, in0=a, in1=b, op=op), reads, writes)

    def tsc(eng, out, a, s1, s2, op0, op1, reads, writes, **kw):
        P.op(eng, lambda e: e.tensor_scalar(out=out, in0=a, scalar1=s1, scalar2=s2, op0=op0, op1=op1, **kw), reads, writes)

    def stt(out, a, s, b, op0, op1, reads, writes, **kw):
        P.op('dve', lambda e: e.scalar_tensor_tensor(out=out, in0=a, scalar=s, in1=b, op0=op0, op1=op1, **kw), reads, writes)

    def cp(eng, out, in_, reads, writes):
        P.op(eng, lambda e: e.tensor_copy(out=out, in_=in_), reads, writes)

    def dma(q, out, in_, reads, writes, key, **kw):
        P.dma(q, lambda e: e.dma_start(out=out, in_=in_, **kw), reads, writes, key)

    NCD = dict(allow_slow_non_contiguous=True)
    CAST = dict(max_dma_last_dim=8192)

    cst_f = sb("cst_f", [128, K_END])
    ident_f = cst_f[:, K_ID:K_ID + 128]; ones_f = cst_f[:, K_ONE:K_ONE + 128]
    tri_le = cst_f[:, K_LE:K_LE + 128]; tri_ge = cst_f[:, K_GE:K_GE + 128]; rotT = cst_f[:, K_ROT:K_ROT + 128]
    cst_b = sb("cst_b", [128, 256], BF16)
    ident_b = cst_b[:, 0:128]; ones_b = cst_b[:, 128:256]
    modcols = sb("modcols", [128, 48])
    gs1 = sb("gs1", [128, 8]); gs2 = sb("gs2", [128, 8])
    G1 = sb("G1", [128, D]); G2 = sb("G2", [128, D])
    npre = sb("npre", [128, 16])
    eps_c = sb("eps_c", [128, 1])
    cw = sb("cw", [128, 16, 5], stack=em); cb = sb("cb", [128, 16], stack=em)
    qkn = sb("qkn", [128, 2], stack=em)
    bg_bc = sb("bg_bc", [128, 16], stack=em)
    gncol = sb("gncol", [128, 8], stack=em)
    GI = sb("GI", [128, NT, 16], stack=em)

    with contextlib.ExitStack() as sa:
        dma('sp', cst_f[:], cst, [], ['cst_f'], 'cst_f')
        cp('dve', cst_b[:], cst_f[:, 0:256], ['cst_f'], ['cst_b'])
        P.op('dve', lambda e: e.memset(eps_c[:], EPS), [], ['eps_c'])
        c_col = sb("c_col", [128, 8], stack=sa); sc_col = sb("sc_col", [128, 8], stack=sa)
        dma('sp', c_col[:], c_in.rearrange("(k p) -> p k", p=128), [], ['c_col'], 'c_col', **NCD)
        act(sc_col[:], c_col[:], AF.Silu, ['c_col'], ['sc_col'])
        badar = sb("badar", [1, 6 * D], stack=sa); modrow = sb("modrow", [1, 6 * D], stack=sa)
        dma('sp', badar[:], b_ada.rearrange("(o n) -> o n", o=1), [], ['badar'], 'badar')
        wa = sb("wa", [128, 2, 8, 512], stack=sa)
        mp = ps("mp", [128, 512], stack=sa)
        w_ada_v = w_ada.rearrange("(k p) n -> p k n", p=128)
        for g in range(12):
            b = g % 2
            dma('sp', wa[:, b], w_ada_v[:, :, g * 512:(g + 1) * 512], [], [('wa', b)], ('wa', b))
            for k in range(8):
                mm(mp[0:1, :], sc_col[:, k:k + 1], wa[:, b, k, :], k == 0, k == 7, ['sc_col', ('wa', b)], ['mp'])
            tt('dve', modrow[:, g * 512:(g + 1) * 512], mp[0:1, :], badar[:, g * 512:(g + 1) * 512], ALU.add,
               ['mp', 'badar'], ['modrow'])
        dma('sp', mod_scr.rearrange("(o n) -> o n", o=1), modrow[:], ['modrow'], ['mod_scr'], 'modrow')
        dma('sp', modcols[:], mod_scr.rearrange("(w p) -> p w", p=128), ['mod_scr'], ['modcols'], 'modcols', **NCD)
        g1b = sb("g1b", [128, D], stack=sa); g2b = sb("g2b", [128, D], stack=sa)
        dma('sp', g1b[:], mod_scr[2 * D:3 * D].partition_broadcast(128), ['mod_scr'], ['g1b'], 'g1b')
        dma('sp', g2b[:], mod_scr[5 * D:6 * D].partition_broadcast(128), ['mod_scr'], ['g2b'], 'g2b')
        dma('sp', G1[:], n1post.partition_broadcast(128), [], ['G1'], 'G1')
        dma('sp', G2[:], n2post.partition_broadcast(128), [], ['G2'], 'G2')
        tt('dve', G1[:], G1[:], g1b[:], ALU.mult, ['G1', 'g1b'], ['G1'])
        tt('dve', G2[:], G2[:], g2b[:], ALU.mult, ['G2', 'g2b'], ['G2'])
        dma('sp', npre[:, 0:8], n1pre.rearrange("(k p) -> p k", p=128), [], ['npre1'], 'npre1', **NCD)
        dma('sp', npre[:, 8:16], n2pre.rearrange("(k p) -> p k", p=128), [], ['npre2'], 'npre2', **NCD)
        stt(gs1[:], modcols[:, 8:16], 1.0, npre[:, 0:8], ALU.add, ALU.mult, ['modcols', 'npre1'], ['gs1'])
        stt(gs2[:], modcols[:, 32:40], 1.0, npre[:, 8:16], ALU.add, ALU.mult, ['modcols', 'npre2'], ['gs2'])
        dma('sp', cw[:], conv_w.rearrange("(c p) j -> p c j", p=128), [], ['cw'], 'cw')
        dma('sp', cb[:], conv_b.rearrange("(c p) -> p c", p=128), [], ['cb'], 'cb', **NCD)
        dma('sp', qkn[:, 0:1], qn_w.rearrange("(p o) -> p o", o=1), [], ['qkn0'], 'qkn0', **NCD)
        dma('sp', qkn[:, 1:2], kn_w.rearrange("(p o) -> p o", o=1), [], ['qkn1'], 'qkn1', **NCD)
        dma('sp', bg_bc[:], b_gates.partition_broadcast(128), [], ['bg_bc'], 'bg_bc')
        dma('sp', gncol[:], gn_w.rearrange("(k p) -> p k", p=128), [], ['gncol'], 'gncol', **NCD)
        P.flush()
    sh1 = modcols[:, 0:8]; sh2 = modcols[:, 24:32]
    if stop_after == 'A':
        P.finish(); em.close(); es.close(); return nc

    with contextlib.ExitStack() as sbk:
        w_sb = sb("w_sb", [128, 8, C_END], BF16, stack=sbk)
        w_in_v = w_in.rearrange("(k p) n -> p k n", p=128)
        wgroups = [(C_QM, C_KM), (C_KM, C_QA), (C_QA, C_KA), (C_KA, C_VM), (C_VM, C_OM), (C_OM, C_VA), (C_VA, C_END)]

        def wres(c0):
            for a, b in wgroups:
                if a <= c0 < b:
                    return ('w_sb', a)

        for a, b in wgroups:
            dma('pool', w_sb[:, :, a:b], w_in_v[:, :, a:b], [], [('w_sb', a)], ('w_sb', a), **CAST)
        xt = sb("xt", [128, 3, D], stack=sbk); junk = sb("junk", [128, D], BF16, stack=sbk)
        sst = sb("sst", [128, 3, 4], stack=sbk)
        hn = sb("hn", [128, 2, D], stack=sbk)
        hT = sb("hT", [128, 2, 8, 512], BF16, stack=sbk)
        rawb = sb("rawb", [128, 3, 520], stack=sbk); halo = sb("halo", [128, 16, 4], stack=sbk)
        cacc = sb("cacc", [128, 2, 512], stack=sbk); sout = sb("sout", [128, 3, 512], BF16, stack=sbk)
        cs = sb("cs", [128, 2, 2, 512], stack=sbk)
        sq = sb("sq", [128, 3, 512], BF16, stack=sbk); qw = sb("qw", [128, 3, 512], stack=sbk)
        rs = sb("rs", [128, 3, 512], stack=sbk); t1 = sb("t1", [128, 3, 512], stack=sbk)
        t2 = sb("t2", [128, 2, 512], stack=sbk); rout = sb("rout", [128, 3, 512], BF16, stack=sbk)
        vt = sb("vt", [128, 2, D], BF16, stack=sbk); ogt = sb("ogt", [128, 2, D], BF16, stack=sbk)
        vat = sb("vat", [128, 2, 256], BF16, stack=sbk)
        tp = [ps("tp%d" % i, [128, 512], stack=sbk) for i in range(2)]
        fa = [ps("fa%d" % i, [128, 512], stack=sbk) for i in range(3)]
        pss = ps("pss", [128, 512], stack=sbk)
        psr = [ps("psr%d" % i, [128, 512], stack=sbk) for i in range(2)]
        R_x, R_fa, R_raw, R_cacc, R_sout = Ring('xt', 3), Ring('fa', 3), Ring('rawb', 3), Ring('cacc', 2), Ring('sout', 3)
        R_rope, R_psr, R_t2, R_hn = Ring('rope', 3), Ring('psr', 2), Ring('t2', 2), Ring('hn', 2)
        P.op('pool', lambda e: e.memset(halo[:], 0.0), [], [('halo', c) for c in range(16)])
        mhalf = sb("mhalf", [128, 1], stack=sbk)
        P.op('pool', lambda e: e.memset(mhalf[:], -0.5), [], ['mhalf'])
        hT_sv = hT_scr.rearrange("(k p) n -> p k n", p=128)

        def hres(hb):
            return [('hT', hb, t) for t in range(4)]

        def conv_item(ch, hb, blk, tok0, dst, tmax, width=512, tail=False):
            st = {}

            def s0():
                if tail:
                    return
                c0 = (C_QM if ch < 8 else C_KM - 1024) + ch * 128
                st['f'] = f = R_fa.next()
                for k in range(8):
                    mm(fa[f][:], w_sb[:, k, c0:c0 + 128], hT[:, hb, k, :], k == 0, k == 7, [wres(c0)] + hres(hb), [('fa', f)])

            def s1():
                st['rb'] = rb = R_raw.next(); rr = ('rawb', rb)
                if tail:
                    P.op('pool', lambda e: e.memset(rawb[:, rb, 4:8], 0.0), [], [rr])
                else:
                    act(rawb[:, rb, 4:4 + 512], fa[st['f']][:], AF.Copy, [('fa', st['f'])], [rr])
                cp('pool', rawb[:, rb, 0:4], halo[:, ch, :], [('halo', ch), rr], [rr])
                if not tail:
                    cp('pool', halo[:, ch, :], rawb[:, rb, 512:516], [rr], [('halo', ch)])

            def s2():
                rb = st['rb']; rr = ('rawb', rb)
                st['ca'] = ca = R_cacc.next(); rc = ('cacc', ca)
                tsc('dve', cacc[:, ca, 0:width], rawb[:, rb, 0:width], cw[:, ch, 0:1], None, ALU.mult, ALU.bypass, [rr, 'cw'], [rc])
                for j in range(1, 5):
                    stt(cacc[:, ca, 0:width], rawb[:, rb, j:j + width], cw[:, ch, j:j + 1], cacc[:, ca, 0:width],
                        ALU.mult, ALU.add, [rr, 'cw', rc], [rc])

            def s3():
                ca = st['ca']; rc = ('cacc', ca)
                so = R_sout.next(); rso = ('sout', so)
                act(sout[:, so, 0:width], cacc[:, ca, 0:width], AF.Silu, [rc, 'cb'], [rso], bias=cb[:, ch:ch + 1])
                lo = max(tok0 - 2, 0); hi = min(tok0 - 2 + width, tmax)
                if hi > lo:
                    j0 = lo - (tok0 - 2)
                    cc = ch % 8
                    dma('sp', dst[cc * 128:(cc + 1) * 128, lo:hi], sout[:, so, j0:j0 + (hi - lo)], [rso], [], rso)

            return [s0, s1, s2, s3]

        def rope_item(c0, wcol, hb, csb, dst_ap):
            st = {}

            def s0():
                st['f'] = f = R_fa.next()
                for k in range(8):
                    mm(fa[f][:], w_sb[:, k, c0:c0 + 128], hT[:, hb, k, :], k == 0, k == 7, [wres(c0)] + hres(hb), [('fa', f)])

            def s1():
                st['r'] = r = R_rope.next()
                act(sq[:, r], fa[st['f']][:], AF.Square, [('fa', st['f'])], [('sq', r)])
                act(qw[:, r], fa[st['f']][:], AF.Copy, [('fa', st['f'])], [('qw', r)], scale=qkn[:, wcol:wcol + 1])

            def s2():
                r = st['r']
                st['pr'] = pr = R_psr.next()
                mm(pss[:], ones_b, sq[:, r], True, True, [('sq', r), 'cst_b'], ['pss'])
                mm(psr[pr][:], rotT, qw[:, r], True, True, [('qw', r), 'cst_f'], [('psr', pr)])
                act(rs[:, r], pss[:], AF.Ln, ['pss', 'eps_c'], [('rs', r)], scale=1.0 / 128.0, bias=eps_c[:, 0:1])
                act(rs[:, r], rs[:, r], AF.Exp, [('rs', r)], [('rs', r)], scale=-0.5)
                tt('dve', t1[:, r], qw[:, r], cs[:, csb, 0], ALU.mult, [('qw', r), ('cs', csb)], [('t1', r)])

            def s3():
                r = st['r']; pr = st['pr']
                t = R_t2.next()
                tt('dve', t2[:, t], psr[pr][:], cs[:, csb, 1], ALU.mult, [('psr', pr), ('cs', csb)], [('t2', t)])
                tt('dve', t1[:, r], t1[:, r], t2[:, t], ALU.add, [('t1', r), ('t2', t)], [('t1', r)])
                tt('dve', rout[:, r], t1[:, r], rs[:, r], ALU.mult, [('t1', r), ('rs', r)], [('rout', r)])
                dma('sp', dst_ap, rout[:, r], [('rout', r)], [], ('rout', r))

            return [s0, s1, s2, s3]

        def tm_item(t, c0, n, hb, post):
            st = {}

            def s0():
                st['f'] = f = R_fa.next()
                for k in range(8):
                    mm(fa[f][:, 0:n], hT[:, hb, k, t * 128:(t + 1) * 128], w_sb[:, k, c0:c0 + n], k == 0, k == 7,
                       [wres(c0), ('hT', hb, t)], [('fa', f)])

            def s1():
                post(fa[st['f']], ('fa', st['f']))

            return [s0, s1]

        def x_items(blk, hb):
            items = []; dmas = []
            for t in range(4):
                st = {}

                def xd(t=t, st=st):
                    tile = blk * 4 + t
                    st['xb'] = xb = R_x.next(); rx = ('xt', xb)
                    dma('sp', xt[:, xb], x[tile * 128:(tile + 1) * 128, :], [], [rx], rx)

                def xa(t=t, st=st):
                    xb = st['xb']; rx = ('xt', xb)
                    st['hb_'] = hb_ = R_hn.next()
                    stt(junk[:], xt[:, xb], 1.0, xt[:, xb], ALU.mult, ALU.mult, [rx], ['junk', ('sst0', xb)], accum_out=sst[:, xb, 0:1])
                    tsc('dve', sst[:, xb, 1:2], sst[:, xb, 0:1], 1.0 / D, EPS, ALU.mult, ALU.add, [('sst0', xb)], [('sst1', xb)])
                    tt('pool', sst[:, xb, 2:3], sst[:, xb, 1:2], mhalf[:, 0:1], ALU.pow, [('sst1', xb), 'mhalf'], [('sst2', xb)])
                    act(hn[:, hb_], xt[:, xb], AF.Copy, [rx, ('sst2', xb)], [('hn', hb_)], scale=sst[:, xb, 2:3])

                def xb_(t=t, st=st):
                    hb_ = st['hb_']
                    for half in range(2):
                        for kk in range(4):
                            k = half * 4 + kk
                            tr(tp[half][:, kk * 128:(kk + 1) * 128], hn[:, hb_, k * 128:(k + 1) * 128], ident_f,
                               [('hn', hb_), 'cst_f'], [('tp', half)])
                        for kk in range(4):
                            k = half * 4 + kk
                            act(hT[:, hb, k, t * 128:(t + 1) * 128], tp[half][:, kk * 128:(kk + 1) * 128], AF.Identity,
                                [('tp', half), 'gs1', 'modcols'], [('hT', hb, t)], scale=gs1[:, k:k + 1], bias=sh1[:, k:k + 1])
                    if t == 3 and blk < NBO:
                        dma('sp', hT_sv[:, :, blk * 512:(blk + 1) * 512], hT[:, hb], hres(hb), [('hT_scr', blk)], ('hTst', hb))

                items.append([xa]); items.append([xb_]); dmas.append([xd])
            return items, dmas

        def block_items(blk, hb):
            own = blk < NBO
            csb = blk % 2
            items = []

            def csload():
                dma('sp', cs[:, csb, 0], cos_t[:, blk * 512:(blk + 1) * 512], [], [('cs', csb)], ('cs', csb))
                dma('sp', cs[:, csb, 1], sin_t[:, blk * 512:(blk + 1) * 512], [], [('cs', csb)], ('cs', csb))

            items.append([csload])
            LC, LR, LT, LTV = [], [], [], []
            if blk <= NBO:
                for ch in range(8):
                    LC.append(conv_item(ch, hb, blk, blk * 512, qT_scr, SO))
            for ch in range(8):
                LC.append(conv_item(8 + ch, hb, blk, blk * 512, kT_scr, S))
            if own:
                for h in range(8):
                    LR.append(rope_item(C_QA + h * 128, 0, hb, csb, QT_scr[h * 128:(h + 1) * 128, blk * 512:(blk + 1) * 512]))
            for g in range(2):
                LR.append(rope_item(C_KA + g * 128, 1, hb, csb, KT_scr[g * 128:(g + 1) * 128, blk * 512:(blk + 1) * 512]))
            items_tm = LT
            for t in range(4):
                tile = blk * 4 + t
                tk = slice(tile * 128, (tile + 1) * 128)
                vb = tile % 2

                def post_v(half, vb=vb, tk=tk):
                    def f(pt, ra):
                        rv = ('vt', vb, half)
                        if half:
                            act(vt[:, vb, 512:1024], pt[:], AF.Copy, [ra], [rv])
                            dma('sp', v_scr[tk, :], vt[:, vb], [('vt', vb, 0), rv], [], ('vtst', vb))
                        else:
                            cp('dve', vt[:, vb, 0:512], pt[:], [ra], [rv])
                    return f

                def post_o(half, vb=vb, tk=tk):
                    def f(pt, ra):
                        ro = ('ogt', vb, half)
                        act(ogt[:, vb, half * 512:(half + 1) * 512], pt[:], AF.Sigmoid, [ra], [ro])
                        if half:
                            dma('sp', og_scr[tk, :], ogt[:, vb], [('ogt', vb, 0), ro], [], ('ogst', vb))
                    return f

                def post_a(pt, ra, vb=vb, tk=tk, tile=tile):
                    rva = ('vat', vb)
                    cp('dve', vat[:, vb], pt[:, 0:256], [ra], [rva])
                    tt('dve', GI[:, tile, :], pt[:, 256:272], bg_bc[:], ALU.add, [ra, 'bg_bc'], [('GI', tile)])
                    dma('sp', VA_scr[tk, :], vat[:, vb], [rva], [], rva)

                for half in range(2):
                    LTV.append(tm_item(t, C_VM + half * 512, 512, hb, post_v(half)))
                if own:
                    for half in range(2):
                        LT.append(tm_item(t, C_OM + half * 512, 512, hb, post_o(half)))
                LT.append(tm_item(t, C_VA, 272, hb, post_a))
            ic, iv = 0, 0
            while ic < len(LC) or iv < len(LTV):
                take_c = (len(LC) - ic) * max(1, len(LTV)) >= (len(LTV) - iv) * max(1, len(LC))
                if ic < len(LC) and (take_c or iv >= len(LTV)):
                    items.append(LC[ic]); ic += 1
                else:
                    items.append(LTV[iv]); iv += 1
            items.extend(LR); items.extend(LT)
            return items

        sched = []
        xi0, xd0 = x_items(0, 0)
        pend_dma = list(xd0)
        sched.append(pend_dma.pop(0))
        for it_ in xi0:
            if it_[0].__name__ == 'xa' and pend_dma:
                sched.append(pend_dma.pop(0))
            sched.append(it_)
        for blk in range(NB):
            bi = block_items(blk, blk % 2)
            xi, xdn = x_items(blk + 1, (blk + 1) % 2) if blk + 1 < NB else ([], [])
            pend_dma.extend(xdn)
            if pend_dma:
                sched.append(pend_dma.pop(0))
            n = len(bi)
            pos = {}
            for j in range(len(xi)):
                p_ = min(n - 1, (n * (2 * j + 1)) // (2 * len(xi)))
                pos.setdefault(p_, []).append(xi[j])
            for i_, it_ in enumerate(bi):
                sched.append(it_)
                for xj in pos.get(i_, []):
                    if xj[0].__name__ == 'xa' and pend_dma:
                        sched.append(pend_dma.pop(0))
                    sched.append(xj)
        for ch in range(8):
            sched.append(conv_item(8 + ch, 0, NB, S, kT_scr, S, width=2, tail=True))
        depth = 4
        for it in range(len(sched) + depth - 1):
            for k in [0, 3, 2, 1]:
                i_ = it - k
                if 0 <= i_ < len(sched) and k < len(sched[i_]):
                    sched[i_][k]()
        P.flush()
    if stop_after == 'B':
        P.finish(); em.close(); es.close(); return nc

    NG = NT * 4
    ea = sb("ea", [128, 2, NG], stack=em); thr = sb("thr", [128, 2, NG], stack=em); dec = sb("dec", [128, 2, NG], stack=em)
    one_c = sb("one_c", [128, 2], stack=em)
    with contextlib.ExitStack() as sc_:
        nlf = sb("nlf", [128, 2, NG], stack=sc_); Dn = sb("Dn", [128, 2, NG], stack=sc_)
        tmpg = sb("tmpg", [128, 2, NG], stack=sc_)
        bbp = [ps("bbp%d" % i, [128, 512], stack=sc_) for i in range(2)]
        btp = [ps("btp%d" % i, [128, 512], stack=sc_) for i in range(2)]
        P.op('dve', lambda e: e.memset(one_c[:, 0:1], 1.0), [], ['one_c'])
        P.op('dve', lambda e: e.memset(one_c[:, 1:2], float(np.log(16.0))), [], ['one_c'])
        v3 = lambda ap: ap.rearrange("p (t h) -> p t h", h=4)
        for d in range(2):
            rd = ('gate', d)
            act(v3(nlf[:, d]), GI[:, :, 4 + 8 * d:8 + 8 * d], AF.Exp, [('GI', t) for t in range(NT)], [rd], scale=-1.0)
            act(nlf[:, d], nlf[:, d], AF.Ln, [rd, 'one_c'], [rd], bias=one_c[:, 0:1])
            mm(bbp[d][:, 0:NG], tri_le if d == 0 else tri_ge, nlf[:, d], True, True, [rd, 'cst_f'], [('bbp', d)])
            mm(btp[d][:, 0:NG], ones_f, nlf[:, d], True, True, [rd, 'cst_f'], [('btp', d)])
            act(Dn[:, d], bbp[d][:, 0:NG], AF.Copy, [('bbp', d)], [rd])
            tt('dve', Dn[:, d], Dn[:, d], btp[d][:, 0:NG], ALU.subtract, [rd, ('btp', d)], [rd])
            tt('dve', v3(tmpg[:, d]), v3(Dn[:, d]), GI[:, :, 8 * d:8 * d + 4], ALU.add,
               [rd] + [('GI', t) for t in range(NT)], [rd])
            act(thr[:, d], Dn[:, d], AF.Exp, [rd, 'one_c'], [('thr', d)], bias=one_c[:, 1:2])
            act(ea[:, d], tmpg[:, d], AF.Exp, [rd], [('ea', d)])
            act(dec[:, d], btp[d][:, 0:NG], AF.Exp, [('btp', d)], [('dec', d)], scale=-1.0)
        P.flush()
    if stop_after == 'C':
        P.finish(); em.close(); es.close(); return nc

    with contextlib.ExitStack() as sd:
        kTt = sb("kTt", [128, 3, 8, 128], BF16, stack=sd); qTt = sb("qTt", [128, 3, 8, 128], BF16, stack=sd)
        vext = sb("vext", [128, 3, 4, 258], BF16, stack=sd)
        ktok = sb("ktok", [128, 2, 1024], BF16, stack=sd); vs = sb("vs", [128, 2, 4, 258], BF16, stack=sd)
        wT = sb("wT", [128, 2, 4, 128], BF16, stack=sd)
        Cacc = sb("Cacc", [128, 4, 2, 258], stack=sd); Cbf = sb("Cbf", [128, 4, 2, 258], BF16, stack=sd)
        ddt = sb("ddt", [128, 2, 8], stack=sd)
        hdir = sb("hdir", [128, 2, D], stack=sd); hBt = sb("hBt", [128, 2, D], stack=sd)
        ogl = sb("ogl", [128, 2, D], BF16, stack=sd)
        stats = sb("stats", [128, 4, 6], stack=sd); mv = sb("mv", [128, 4, 2], stack=sd); rg = sb("rg", [128, 8], stack=sd)
        ym = sb("ym", [128, 2, D], BF16, stack=sd); ymT = sb("ymT", [128, 2, 8, 128], BF16, stack=sd)
        bT = ps("bT", [128, 1024], BF16, stack=sd)
        bS = ps("bS", [128, 512], stack=sd)
        bN = [ps("bN%d" % i, [128, 512], stack=sd) for i in range(2)]
        bX = ps("bX", [128, 512], stack=sd)
        bC = [ps("bC%d" % i, [128, 512], stack=sd) for i in range(3)]
        R_ld, R_kk, R_vv, R_ww, R_bC, R_dd = Ring('ld', 3), Ring('ktok', 2), Ring('vs', 2), Ring('wT', 2), Ring('bC', 3), Ring('dd', 2)
        R_hd, R_hB, R_ogl, R_ym = (Ring(n, 2) for n in ('hdir', 'hBt', 'ogl', 'ym'))
        kT_v = kT_scr.rearrange("(c p) n -> p c n", p=128); qT_v = qT_scr.rearrange("(c p) n -> p c n", p=128)
        ymT_v = ymT_scr.rearrange("(c p) n -> p c n", p=128)
        for i in range(3):
            P.op('pool', lambda e, i=i: e.memset(vext[:, i, :, 256:258], 1.0), [], [('vext', i)])

        for d in (1, 0):
            P.op('pool', lambda e: e.memset(Cacc[:], 0.0), [], [('Cacc', h) for h in range(4)])
            mask = tri_ge if d == 1 else tri_le
            tiles = list(range(NT - 1, -1, -1)) if d == 1 else list(range(NTO))
            nst = len(tiles)
            info = {}

            def load(i):
                tile = tiles[i]; full = tile < NTO
                lb = R_ld.next()
                tk = slice(tile * 128, (tile + 1) * 128)
                dma('sp', kTt[:, lb], kT_v[:, :, tk], [], [('kTt', lb)], ('kTt', lb))
                dma('sp', vext[:, lb, :, 0:256], v_scr[tk, :].rearrange("p (h e) -> p h e", h=4), [], [('vext', lb)], ('vext', lb))
                if full:
                    dma('sp', qTt[:, lb], qT_v[:, :, tk], [], [('qTt', lb)], ('qTt', lb))
                info[i] = dict(tile=tile, full=full, lb=lb, tk=tk)

            def pe_a(i):
                st = info[i]; lb = st['lb']
                for c in range(8):
                    tr(bT[:, c * 128:(c + 1) * 128], kTt[:, lb, c, :], ident_b, [('kTt', lb), 'cst_b'], ['bT'])
                if st['full']:
                    for h in range(4):
                        for j in range(2):
                            mm(bS[:, h * 128:(h + 1) * 128], kTt[:, lb, 2 * h + j, :], qTt[:, lb, 2 * h + j, :], j == 0, j == 1,
                               [('kTt', lb), ('qTt', lb)], ['bS'])

            def ev_a(i):
                st = info[i]; lb = st['lb']; c0 = st['tile'] * 4
                st['kk'] = kk = R_kk.next(); st['vv'] = vv = R_vv.next()
                act(ktok[:, kk], bT[:], AF.Copy, ['bT'], [('ktok', kk)])
                for h in range(4):
                    tsc('dve', vs[:, vv, h, :], vext[:, lb, h, :], ea[:, d, c0 + h:c0 + h + 1], None, ALU.mult, ALU.bypass,
                        [('vext', lb), ('ea', d)], [('vs', vv)])
                if st['full']:
                    st['ww'] = ww = R_ww.next()
                    for h in range(4):
                        tt('dve', wT[:, ww, h, :], bS[:, h * 128:(h + 1) * 128], mask, ALU.mult, ['bS', 'cst_f'], [('wT', ww)])

            def pe_b(i):
                pass

            def ev_b(i):
                st = info[i]; lb = st['lb']; kk = st['kk']; vv = st['vv']
                c0 = st['tile'] * 4; tile = st['tile']; tk = st['tk']
                bc = {}

                def dC(h):
                    bc[h] = c = R_bC.next()
                    for j in range(2):
                        mm(bC[c][:, j * 256:(j + 1) * 256], ktok[:, kk, h * 256 + j * 128:h * 256 + (j + 1) * 128],
                           vs[:, vv, h, 0:256], True, True, [('ktok', kk), ('vs', vv)], [('bC', c)])

                def upd(h):
                    dcc = dec[:, d, c0 + h:c0 + h + 1]; c = bc[h]
                    stt(Cacc[:, h, :, 0:256], Cacc[:, h, :, 0:256], dcc, bC[c][:].rearrange("p (j e) -> p j e", j=2), ALU.mult, ALU.add,
                        [('Cacc', h), ('dec', d), ('bC', c)], [('Cacc', h)])

                for h in range(3):
                    dC(h)
                for h in range(4):
                    for j in range(2):
                        mm(bX[:, 8 + 4 * h + 2 * j:10 + 4 * h + 2 * j], ktok[:, kk, h * 256 + j * 128:h * 256 + (j + 1) * 128],
                           vs[:, vv, h, 256:258], True, True, [('ktok', kk), ('vs', vv)], ['bX'])
                for h in range(3):
                    upd(h)
                for h in range(4):
                    dcc = dec[:, d, c0 + h:c0 + h + 1]
                    stt(Cacc[:, h, :, 256:258], Cacc[:, h, :, 256:258], dcc,
                        bX[:, 8 + 4 * h:12 + 4 * h].rearrange("p (j e) -> p j e", j=2), ALU.mult, ALU.add,
                        [('Cacc', h), ('dec', d), 'bX'], [('Cacc', h)])
                dC(3)
                upd(3)
                if st['full']:
                    ww = st['ww']
                    for h in range(4):
                        nbk = bN[h // 2]; o0 = (h % 2) * 256
                        mm(nbk[:, o0:o0 + 256], wT[:, ww, h, :], vs[:, vv, h, 0:256], True, False, [('wT', ww), ('vs', vv)], [('bN', h // 2)])
                        for j in range(2):
                            mm(nbk[:, o0:o0 + 256], qTt[:, lb, 2 * h + j, :], Cbf[:, h, j, 0:256], False, j == 1,
                               [('qTt', lb), ('Cbf', h)], [('bN', h // 2)])
                        mm(bX[:, 2 * h:2 * h + 2], wT[:, ww, h, :], vs[:, vv, h, 256:258], True, False, [('wT', ww), ('vs', vv)], ['bX'])
                        for j in range(2):
                            mm(bX[:, 2 * h:2 * h + 2], qTt[:, lb, 2 * h + j, :], Cbf[:, h, j, 256:258], False, j == 1,
                               [('qTt', lb), ('Cbf', h)], ['bX'])
                if not st['full']:
                    return
                di = R_dd.next(); rdd = ('dd', di)
                hb = R_hd.next(); rhd = ('hdir', hb)
                act(ddt[:, di, 0:4], bX[:, 0:8].rearrange("p (h e) -> p h e", e=2)[:, :, 0], AF.Abs, ['bX'], [rdd])
                tt('dve', ddt[:, di, 0:4], ddt[:, di, 0:4], thr[:, d, c0:c0 + 4], ALU.max, [rdd, ('thr', d)], [rdd])
                P.op('dve', lambda e, di=di: e.reciprocal(out=ddt[:, di, 4:8], in_=ddt[:, di, 0:4]), [rdd], [rdd])
                for h in range(4):
                    nbk = bN[h // 2]; o0 = (h % 2) * 256
                    if d == 0:
                        stt(hdir[:, hb, h * 256:(h + 1) * 256], nbk[:, o0:o0 + 256], ddt[:, di, 4 + h:5 + h],
                            hBt[:, st['hBb'], h * 256:(h + 1) * 256], ALU.mult, ALU.add,
                            [('bN', h // 2), rdd, ('hBt', st['hBb'])], [rhd])
                    elif h % 2 == 0:
                        act(hdir[:, hb, h * 256:(h + 1) * 256], nbk[:, o0:o0 + 256], AF.Copy, [('bN', h // 2), rdd], [rhd],
                            scale=ddt[:, di, 4 + h:5 + h])
                    else:
                        tsc('dve', hdir[:, hb, h * 256:(h + 1) * 256], nbk[:, o0:o0 + 256], ddt[:, di, 4 + h:5 + h], None,
                            ALU.mult, ALU.bypass, [('bN', h // 2), rdd], [rhd])
                if d == 1:
                    dma('sp', hB_scr[tk, :], hdir[:, hb], [rhd], [('hB_scr', tile)], rhd)
                    return
                bb_ = st['hBb']; rhB = ('hBt', bb_); ob = st['ogb']; rol = ('ogl', ob)
                for h in range(4):
                    P.op('dve', lambda e, h=h, hb=hb: e.bn_stats(out=stats[:, h, :], in_=hdir[:, hb, h * 256:(h + 1) * 256]),
                         [rhd], [('stats', h)])
                    P.op('dve', lambda e, h=h: e.bn_aggr(out=mv[:, h, :], in_=stats[:, h, :]), [('stats', h)], [('mv', h)])
                act(rg[:, 0:4], mv[:, :, 1], AF.Sqrt, [('mv', h) for h in range(4)] + ['eps_c'], ['rg'], bias=eps_c[:, 0:1])
                P.op('dve', lambda e: e.reciprocal(out=rg[:, 4:8], in_=rg[:, 0:4]), ['rg'], ['rg'])
                for h in range(4):
                    tsc('dve', hdir[:, hb, h * 256:(h + 1) * 256], hdir[:, hb, h * 256:(h + 1) * 256], mv[:, h, 0:1],
                        rg[:, 4 + h:5 + h], ALU.subtract, ALU.mult, [rhd, ('mv', h), 'rg'], [rhd])
                yb = R_ym.next(); rym = ('ym', yb)
                tt('dve', ym[:, yb], hdir[:, hb], ogl[:, ob], ALU.mult, [rhd, rol], [rym])
                st['yb'] = yb

            def pe_c(i):
                st = info[i]
                if d != 0 or not st['full']:
                    return
                yb = st['yb']; rym = ('ym', yb); rymT = ('ymT', yb)
                for k in range(8):
                    tr(bT[:, k * 128:(k + 1) * 128], ym[:, yb, k * 128:(k + 1) * 128], ident_b, [rym, 'cst_b'], ['bT'])
                for k in range(8):
                    act(ymT[:, yb, k, :], bT[:, k * 128:(k + 1) * 128], AF.Copy, ['bT', 'gncol'], [rymT], scale=gncol[:, k:k + 1])
                dma('sp', ymT_v[:, :, st['tk']], ymT[:, yb], [rymT], [('ymT_scr', st['tile'] // 4)], rymT)

            def cbf(i):
                st = info[i]
                if not st['full']:
                    return
                c0 = st['tile'] * 4
                for h in range(4):
                    act(Cbf[:, h], Cacc[:, h], AF.Copy, [('Cacc', h), ('dec', d)], [('Cbf', h)], scale=dec[:, d, c0 + h:c0 + h + 1])

            def epi_loads(i):
                st = info[i]
                if d == 0 and st['full']:
                    st['hBb'] = bb_ = R_hB.next(); st['ogb'] = ob = R_ogl.next()
                    dma('sp', hBt[:, bb_], hB_scr[st['tk'], :], [('hB_scr', st['tile'])], [('hBt', bb_)], ('hBt', bb_))
                    dma('sp', ogl[:, ob], og_scr[st['tk'], :], [], [('ogl', ob)], ('ogl', ob))

            load(0)
            for i in range(nst + 2):
                if i + 1 < nst:
                    load(i + 1)
                if i < nst:
                    epi_loads(i)
                    pe_a(i)
                    ev_a(i)
                if 0 <= i - 2 < nst:
                    pe_c(i - 2)
                if 0 <= i - 1 < nst:
                    pe_b(i - 1)
                    ev_b(i - 1)
                if i < nst:
                    cbf(i)
        P.flush()
    if stop_after == 'D':
        P.finish(); em.close(); es.close(); return nc

    with contextlib.ExitStack() as se:
        KT = sb("KT", [128, 2, S], BF16, stack=se); VA = sb("VA", [128, NT, 256], BF16, stack=se)
        for g in range(2):
            dma('sp', KT[:, g, :], KT_scr[g * 128:(g + 1) * 128, :], [], [('KT', g)], ('KT', g))
        VA_v = VA_scr.rearrange("(t p) c -> p t c", p=128)
        nv = max(1, NT // 8)
        for i in range(0, NT, nv):
            dma('sp', VA[:, i:i + nv, :], VA_v[:, i:i + nv, :], [], [('VA', i)], ('VA', i))
        vres = lambda k: ('VA', (k // nv) * nv)
        QTt = sb("QTt", [128, 3, 512], BF16, stack=se); pT = sb("pT", [128, 6, 2, 512], BF16, stack=se)
        rden = sb("rden", [128, 2, 512], stack=se); yo = sb("yo", [128, 2, 512], BF16, stack=se)
        accd = sb("accd", [128, 2, 2, 512], BF16, stack=se)
        scp = [ps("scp%d" % i, [128, 1024], stack=se) for i in range(2)]
        op2 = [ps("op%d" % i, [128, 512], stack=se) for i in range(2)]
        dn2_ = [ps("dnp%d" % i, [128, 512], stack=se) for i in range(2)]
        R_Q, R_sc, R_pT, R_yo, R_acc = Ring('QTt', 3), Ring('scp', 2), Ring('pT', 6), Ring('yo', 2), Ring('accd', 2)
        att_scale = float(DH_A) ** -0.5
        NP = NT // 2
        heads = [(qb, hd) for qb in range(NBO) for hd in range(8)]
        qslot = {}

        def qload(ix):
            qb_, hd_ = heads[ix]
            qslot[ix] = s_ = R_Q.next()
            dma('sp', QTt[:, s_], QT_scr[hd_ * 128:(hd_ + 1) * 128, qb_ * 512:(qb_ + 1) * 512], [], [('QTt', s_)], ('QTt', s_))

        qload(0)
        for ix, (qb, hd) in enumerate(heads):
            if True:
                g = hd // 4
                if ix + 1 < len(heads):
                    qload(ix + 1)
                q_ = qslot[ix]; rQ = ('QTt', q_)
                ab = R_acc.next()
                op_ = op2[ab]; dnp = dn2_[ab]
                ro = ('op', ab); rdn = ('dnp', ab)
                pbuf = {}
                first = {'dve': True, 'pe': True}
                for step in range(NP + 2):
                    if step < NP:
                        si = R_sc.next(); pi = R_pT.next()
                        for u in range(2):
                            kb = 2 * step + u
                            mm(scp[si][:, u * 512:(u + 1) * 512], KT[:, g, kb * 128:(kb + 1) * 128], QTt[:, q_], True, True,
                               [('KT', g), rQ], [('scp', si)])
                        act(pT[:, pi].rearrange("p u n -> p (u n)"), scp[si][:], AF.Exp, [('scp', si)], [('pT', pi)],
                            scale=att_scale)
                        pbuf[step] = pi
                    p_ = step - 2
                    if p_ >= 0:
                        pi = pbuf.pop(p_)
                        for u in range(2):
                            k = 2 * p_ + u
                            mm(op_[:], VA[:, k, g * 128:(g + 1) * 128], pT[:, pi, u], k == 0, k == NT - 1,
                               [vres(k), ('pT', pi)], [ro])
                        for u in range(2):
                            k = 2 * p_ + u
                            who = 'pe' if k % 4 == 3 else 'dve'
                            if who == 'pe':
                                mm(dnp[:], ones_b, pT[:, pi, u], first['pe'], False, ['cst_b', ('pT', pi)], [rdn])
                            else:
                                ra = ('accd', ab, 0)
                                if first[who]:
                                    cp(who, accd[:, ab, 0], pT[:, pi, u], [('pT', pi)], [ra])
                                else:
                                    tt(who, accd[:, ab, 0], accd[:, ab, 0], pT[:, pi, u], ALU.add, [ra, ('pT', pi)], [ra])
                            first[who] = False
                mm(dnp[:], ones_b, accd[:, ab, 0], False, True, ['cst_b', ('accd', ab, 0)], [rdn])
                yb = R_yo.next(); ry = ('yo', yb)
                P.op('dve', lambda e, yb=yb, dnp=dnp: e.reciprocal(out=rden[:, yb], in_=dnp[:]), [rdn], [('rden', yb)])
                tt('dve', yo[:, yb], op_[:], rden[:, yb], ALU.mult, [ro, ('rden', yb)], [ry])
                dma('sp', yaT_scr[hd * 128:(hd + 1) * 128, qb * 512:(qb + 1) * 512], yo[:, yb], [ry], [('yaT_scr', qb)], ry)
        P.flush()
    em.close()
    if stop_after == 'E':
        P.finish(); em.close(); es.close(); return nc

    def load_w(dst, src_ap, name, nsplit):
        n = src_ap.shape[-1]
        step = n // nsplit
        for i in range(nsplit):
            dma('pool', dst[:, :, i * step:(i + 1) * step], src_ap[:, :, i * step:(i + 1) * step], [], [(name, i)], (name, i), **CAST)
        return [(name, i) for i in range(nsplit)]

    def prenorm_T(src_tile, rsrc, gs, sh, dstT, rdst, t, junk_, ssb, hn_, tp_, R_tp_, names):
        stt(junk_, src_tile, 1.0, src_tile, ALU.mult, ALU.mult, [rsrc], [names + 'junk', names + 'ss0'], accum_out=ssb[:, 0:1])
        act(ssb[:, 1:2], ssb[:, 0:1], AF.Sqrt, [names + 'ss0', 'eps_c'], [names + 'ss1'], scale=1.0 / D, bias=eps_c[:, 0:1])
        P.op('dve', lambda e: e.reciprocal(out=ssb[:, 2:3], in_=ssb[:, 1:2]), [names + 'ss1'], [names + 'ss2'])
        act(hn_, src_tile, AF.Copy, [rsrc, names + 'ss2'], [names + 'hn'], scale=ssb[:, 2:3])
        for half in range(2):
            tb = R_tp_.next(); rt = (names + 'tp', tb)
            for kk in range(4):
                k = half * 4 + kk
                tr(tp_[tb][:, kk * 128:(kk + 1) * 128], hn_[:, k * 128:(k + 1) * 128], ident_f, [names + 'hn', 'cst_f'], [rt])
            for kk in range(4):
                k = half * 4 + kk
                if kk % 2:
                    act(dstT[:, k, t * 128:(t + 1) * 128], tp_[tb][:, kk * 128:(kk + 1) * 128], AF.Identity,
                        [rt], [rdst], scale=gs[:, k:k + 1], bias=sh[:, k:k + 1])
                else:
                    tsc('dve', dstT[:, k, t * 128:(t + 1) * 128], tp_[tb][:, kk * 128:(kk + 1) * 128], gs[:, k:k + 1],
                        sh[:, k:k + 1], ALU.mult, ALU.add, [rt], [rdst])

    def postnorm_res(po, rpo, Gt, res_tile, rres, out_tile, rout, junk_, ssb, names):
        for half in range(2):
            act(junk_[:, half * 512:(half + 1) * 512], po[half][:], AF.Square, [rpo[half]], [names + 'pj%d' % half, names + 'ps%d' % half],
                accum_out=ssb[:, half:half + 1])
        tt('dve', ssb[:, 2:3], ssb[:, 0:1], ssb[:, 1:2], ALU.add, [names + 'ps0', names + 'ps1'], [names + 'ps2'])
        act(ssb[:, 3:4], ssb[:, 2:3], AF.Sqrt, [names + 'ps2', 'eps_c'], [names + 'ps3'], scale=1.0 / D, bias=eps_c[:, 0:1])
        P.op('dve', lambda e: e.reciprocal(out=ssb[:, 4:5], in_=ssb[:, 3:4]), [names + 'ps3'], [names + 'ps4'])
        for half in range(2):
            stt(out_tile[:, half * 512:(half + 1) * 512], po[half][:], ssb[:, 4:5], Gt[:, half * 512:(half + 1) * 512],
                ALU.mult, ALU.mult, [rpo[half], names + 'ps4'], [rout])
        tt('pool', out_tile, out_tile, res_tile, ALU.add, [rout, rres], [rout])

    with contextlib.ExitStack() as sf:
        wbm = sb("wbm", [128, 8, D], BF16, stack=sf); wba = sb("wba", [128, 8, D], BF16, stack=sf)
        wo = sb("wo", [128, 8, D], BF16, stack=sf); wbr = sb("wbr", [128, 8, 2 * D], BF16, stack=sf)
        kp = lambda a: a.rearrange("(k p) n -> p k n", p=128)
        r_wbm = load_w(wbm, kp(w_bm), 'wbm', 1); r_wba = load_w(wba, kp(w_ba), 'wba', 1)
        r_wbr = load_w(wbr, kp(w_br), 'wbr', 2); r_wo = load_w(wo, kp(w_out), 'wo', 1)
        hTb = sb("hTb", [128, 2, 8, 512], BF16, stack=sf); ymTb = sb("ymTb", [128, 2, 8, 512], BF16, stack=sf)
        yaTb = sb("yaTb", [128, 2, 8, 512], BF16, stack=sf)
        gmt = sb("gmt", [128, 2, 2, 512], stack=sf)
        yT = sb("yT", [128, 1, 8, 512], BF16, stack=sf)
        xt1 = sb("xt1", [128, 2, D], stack=sf); x1t = sb("x1t", [128, 2, D], stack=sf)
        junk1 = sb("junk1", [128, D], BF16, stack=sf); ssf = sb("ssf", [128, 2, 8], stack=sf)
        h2T = sb("h2T", [128, 1, 8, 512], BF16, stack=sf)
        pbr = [ps("pbr%d" % i, [128, 512], stack=sf) for i in range(4)]
        po1 = [ps("po1%d" % i, [128, 512], stack=sf) for i in range(2)]
        tp1 = [ps("tp1%d" % i, [128, 512], stack=sf) for i in range(2)]
        R_blk, R_g, R_yT, R_x1, R_tp1, R_h2 = Ring('f1blk', 2), Ring('gmt', 2), Ring('yT', 1), Ring('x1', 2), Ring('f1tp', 2), Ring('h2T', 1)
        hT_v = kp(hT_scr); ymT_v2 = kp(ymT_scr); yaT_v = kp(yaT_scr); h2T_v = kp(h2T_scr)
        hn4 = sb("hn4", [128, 4, D], stack=sf)
        ss4 = sb("ss4", [128, 4, 4], stack=sf)
        bslot = {}

        def f1_loads(qb):
            cs_ = slice(qb * 512, (qb + 1) * 512)
            bslot[qb] = b = R_blk.next()
            dma('sp', hTb[:, b], hT_v[:, :, cs_], [('hT_scr', qb)], [('hTb', b)], ('hTb', b))
            dma('sp', ymTb[:, b], ymT_v2[:, :, cs_], [('ymT_scr', qb)], [('ymTb', b)], ('ymTb', b))
            dma('sp', yaTb[:, b], yaT_v[:, :, cs_], [('yaT_scr', qb)], [('yaTb', b)], ('yaTb', b))

        def f1_branch(qb):
            b = bslot[qb]
            ryT = ('yT', 0)
            for fo in range(8):
                fs = slice(fo * 128, (fo + 1) * 128)
                for k in range(8):
                    mm(pbr[2][:], wbr[:, k, fs], hTb[:, b, k, :], k == 0, k == 7, r_wbr + [('hTb', b)], [('pbr', 2)])
                for k in range(8):
                    mm(pbr[3][:], wbr[:, k, D + fo * 128:D + (fo + 1) * 128], hTb[:, b, k, :], k == 0, k == 7,
                       r_wbr + [('hTb', b)], [('pbr', 3)])
                for k in range(8):
                    mm(pbr[0][:], wbm[:, k, fs], ymTb[:, b, k, :], k == 0, k == 7, r_wbm + [('ymTb', b)], [('pbr', 0)])
                for k in range(8):
                    mm(pbr[1][:], wba[:, k, fs], yaTb[:, b, k, :], k == 0, k == 7, r_wba + [('yaTb', b)], [('pbr', 1)])
                gi = R_g.next(); rg_ = ('gmt', gi)
                act(gmt[:, gi, 0], pbr[2][:], AF.Sigmoid, [('pbr', 2)], [rg_])
                act(gmt[:, gi, 1], pbr[3][:], AF.Sigmoid, [('pbr', 3)], [rg_])
                tt('dve', gmt[:, gi, 0], pbr[0][:], gmt[:, gi, 0], ALU.mult, [('pbr', 0), rg_], [rg_])
                tt('dve', gmt[:, gi, 1], pbr[1][:], gmt[:, gi, 1], ALU.mult, [('pbr', 1), rg_], [rg_])
                tt('pool', yT[:, 0, fo, :], gmt[:, gi, 0], gmt[:, gi, 1], ALU.add, [rg_], [ryT])

        def f1_outproj(qb):
            ryT = ('yT', 0)
            for t in range(4):
                tile = qb * 4 + t
                tk = slice(tile * 128, (tile + 1) * 128)
                for half in range(2):
                    for k in range(8):
                        mm(po1[half][:], yT[:, 0, k, t * 128:(t + 1) * 128], wo[:, k, half * 512:(half + 1) * 512],
                           k == 0, k == 7, [ryT] + r_wo, [('po1', half)])
                xb = R_x1.next(); rx = ('xt1', xb); rx1 = ('x1t', xb)
                dma('sp', xt1[:, xb], x[tk, :], [], [rx], rx)
                postnorm_res(po1, [('po1', 0), ('po1', 1)], G1, xt1[:, xb], rx, x1t[:, xb], rx1, junk1, ssf[:, 0], 'f1')
                dma('pool', x1_scr[tk, :], x1t[:, xb], [rx1], [('x1_scr', tile)], rx1)
                stt(junk1[:], x1t[:, xb], 1.0, x1t[:, xb], ALU.mult, ALU.mult, [rx1], ['f1njunk', ('f1ss0', t)], accum_out=ss4[:, t, 0:1])
                act(ss4[:, t, 1:2], ss4[:, t, 0:1], AF.Sqrt, [('f1ss0', t), 'eps_c'], [('f1ss1', t)], scale=1.0 / D, bias=eps_c[:, 0:1])
                P.op('dve', lambda e, t=t: e.reciprocal(out=ss4[:, t, 2:3], in_=ss4[:, t, 1:2]), [('f1ss1', t)], [('f1ss2', t)])
                act(hn4[:, t], x1t[:, xb], AF.Copy, [rx1, ('f1ss2', t)], [('hn4', t)], scale=ss4[:, t, 2:3])

        def f1_trans(qb):
            cs_ = slice(qb * 512, (qb + 1) * 512)
            rh2 = ('h2T', 0)
            for t in range(4):
                for half in range(2):
                    tb = R_tp1.next(); rt = ('f1tp', tb)
                    for kk in range(4):
                        k = half * 4 + kk
                        tr(tp1[tb][:, kk * 128:(kk + 1) * 128], hn4[:, t, k * 128:(k + 1) * 128], ident_f, [('hn4', t), 'cst_f'], [rt])
                    for kk in range(4):
                        k = half * 4 + kk
                        if kk % 2:
                            act(h2T[:, 0, k, t * 128:(t + 1) * 128], tp1[tb][:, kk * 128:(kk + 1) * 128], AF.Identity,
                                [rt], [rh2], scale=gs2[:, k:k + 1], bias=sh2[:, k:k + 1])
                        else:
                            tsc('dve', h2T[:, 0, k, t * 128:(t + 1) * 128], tp1[tb][:, kk * 128:(kk + 1) * 128], gs2[:, k:k + 1],
                                sh2[:, k:k + 1], ALU.mult, ALU.add, [rt], [rh2])
            dma('sp', h2T_v[:, :, cs_], h2T[:, 0], [rh2], [('h2T_scr', qb)], rh2)

        f1_loads(0)
        f1_branch(0)
        for qb in range(NBO):
            if qb + 1 < NBO:
                f1_loads(qb + 1)
            f1_outproj(qb)
            if qb + 1 < NBO:
                f1_branch(qb + 1)
            f1_trans(qb)
        P.flush()
    if stop_after == 'F1':
        P.finish(); em.close(); es.close(); return nc

    with contextlib.ExitStack() as sg:
        w1 = sb("w1", [128, 8, DFF], BF16, stack=sg); w2 = sb("w2", [128, 32, D], BF16, stack=sg)
        r_w1 = load_w(w1, w_m1.rearrange("(k p) n -> p k n", p=128), 'w1', 4)
        r_w2 = load_w(w2, w_m2.rearrange("(c p) n -> p c n", p=128), 'w2', 1)
        h2Tb = sb("h2Tb", [128, 1, 8, 512], BF16, stack=sg)
        rl = sb("rl", [128, 2, 512], stack=sg); uT = sb("uT", [128, 32, 512], BF16, stack=sg)
        x1l = sb("x1l", [128, 2, D], stack=sg); ot = sb("ot", [128, 2, D], stack=sg)
        junk2 = sb("junk2", [128, D], BF16, stack=sg); ssg = sb("ssg", [128, 8], stack=sg)
        pu = [ps("pu%d" % i, [128, 512], stack=sg) for i in range(3)]
        po2 = [[ps("po2%d%d" % (i, j), [128, 512], stack=sg) for j in range(2)] for i in range(2)]
        R_b2, R_pu, R_rl, R_po2, R_x2 = Ring('f2blk', 1), Ring('pu', 3), Ring('rl', 2), Ring('po2', 2), Ring('x2', 2)
        for qb in range(NBO):
            cs_ = slice(qb * 512, (qb + 1) * 512)
            b = R_b2.next(); rb = ('h2Tb', b)
            dma('sp', h2Tb[:, b], h2T_v[:, :, cs_], [('h2T_scr', qb)], [rb], rb)
            for fc in range(32):
                pi = R_pu.next(); rp = ('pu', pi)
                for k in range(8):
                    mm(pu[pi][:], w1[:, k, fc * 128:(fc + 1) * 128], h2Tb[:, b, k, :], k == 0, k == 7,
                       [r_w1[fc // 8], rb], [rp])
                ri = R_rl.next(); rr_ = ('rl', ri)
                act(rl[:, ri], pu[pi][:], AF.Relu, [rp], [rr_])
                tt('pool' if fc % 2 else 'dve', uT[:, fc, :], rl[:, ri], rl[:, ri], ALU.mult, [rr_], [('uT', fc)])
            for t in range(4):
                tile = qb * 4 + t
                tk = slice(tile * 128, (tile + 1) * 128)
                pp = R_po2.next()
                for half in range(2):
                    for fc in range(32):
                        mm(po2[pp][half][:], uT[:, fc, t * 128:(t + 1) * 128], w2[:, fc, half * 512:(half + 1) * 512],
                           fc == 0, fc == 31, [('uT', fc)] + r_w2, [('po2', pp, half)])
                xb = R_x2.next(); rx = ('x1l', xb); rot = ('ot', xb)
                dma('sp', x1l[:, xb], x1_scr[tk, :], [('x1_scr', tile)], [rx], rx)
                postnorm_res(po2[pp], [('po2', pp, 0), ('po2', pp, 1)], G2, x1l[:, xb], rx, ot[:, xb], rot, junk2, ssg, 'f2')
                dma('pool', y_out[tk, :], ot[:, xb], [rot], [], rot)
    P.finish()
    es.close()
    return nc


def _host_consts():
    cst = np.zeros((128, K_END), np.float32)
    cst[:, K_ID:K_ID + 128] = np.eye(128, dtype=np.float32)
    cst[:, K_ONE:K_ONE + 128] = 1.0
    s = np.arange(128)[:, None]; t = np.arange(128)[None, :]
    cst[:, K_LE:K_LE + 128] = (s <= t)
    cst[:, K_GE:K_GE + 128] = (s >= t)
    R = np.zeros((128, 128), np.float32)
    for j in range(128):
        if (j % 64) < 32:
            R[j, j + 32] = -1.0
        else:
            R[j, j - 32] = 1.0
    cst[:, K_ROT:K_ROT + 128] = R.T
    return cst


def _rope_tables(S, flip):
    pos = np.arange(S)
    if flip:
        pos = S - 1 - pos
    row = (pos // 64).astype(np.float32); col = (pos % 64).astype(np.float32)
    freqs = (np.float32(10000.0) ** (-np.arange(32, dtype=np.float32) / np.float32(32))).astype(np.float32)
    j = np.arange(128)
    p = np.where((j // 64)[:, None] == 0, row[None, :], col[None, :]).astype(np.float32)
    ang = (p * freqs[j % 32][:, None]).astype(np.float32)
    return np.cos(ang).astype(np.float32), np.sin(ang).astype(np.float32)


def core_inputs(inp, core, S):
    b, flip = core // 2, (core % 2) == 1
    f32 = lambda a: np.ascontiguousarray(a, dtype=np.float32)
    xs = inp['x'][b][:S]
    if flip:
        xs = xs[::-1]
    w = inp['w_in'][0]
    offs = np.cumsum([0, 2048, 1024, 1024, 16, 1024, 256, 256, 2048])
    qk, v, o, g, qa, ka, va, br = [w[:, offs[i]:offs[i + 1]] for i in range(8)]
    bgt = inp['b_gates'][0]
    cw = inp['conv_w'][0]
    if flip:
        perm = list(range(8, 16)) + list(range(0, 8))
        g = g[:, perm]; bgt = bgt[perm]; cw = cw[::-1]
    w_dev = np.concatenate([qk, qa, ka, v, o, va, g], axis=1)
    cos_t, sin_t = _rope_tables(S, flip)
    return {
        'x': f32(xs), 'c': f32(inp['c'][b]), 'w_ada': f32(inp['w_ada'][0]), 'b_ada': f32(inp['b_ada'][0]),
        'norm1_pre': f32(inp['norm1_pre'][0]), 'norm1_post': f32(inp['norm1_post'][0]),
        'norm2_pre': f32(inp['norm2_pre'][0]), 'norm2_post': f32(inp['norm2_post'][0]),
        'w_in': f32(w_dev), 'w_br': f32(br), 'b_gates': f32(bgt), 'conv_w': f32(cw.T), 'conv_b': f32(inp['conv_b'][0]),
        'mlstm_gn': f32(inp['mlstm_gn'][0]), 'attn_qnorm': f32(inp['attn_qnorm'][0]), 'attn_knorm': f32(inp['attn_knorm'][0]),
        'w_branch_m': f32(inp['w_branch_m'][0]), 'w_branch_a': f32(inp['w_branch_a'][0]), 'w_out': f32(inp['w_out'][0]),
        'w_mlp_in': f32(inp['w_mlp_in'][0]), 'w_mlp_out': f32(inp['w_mlp_out'][0]),
        'cos_t': cos_t, 'sin_t': sin_t, 'cst': _host_consts(),
    }


def run(inp, S=8192, debug=(), stop_after=None, trace=False):
    inp = {k: np.asarray(v) for k, v in inp.items()}
    nc = build(S, debug, stop_after)
    in_maps = [core_inputs(inp, core, S) for core in range(8)]
    res = run_bass_kernel_spmd(nc, in_maps, core_ids=list(range(8)), **({'trace': True} if trace else {}))
    return res


def kernel(**inputs):
    S = 8192
    res = run(inputs, S)
    out = np.empty((4, S, D), np.float32)
    for core in range(8):
        b, flip = core // 2, (core % 2) == 1
        y = np.asarray(res.results[core]['y'], dtype=np.float32)
        if flip:
            out[b, S // 2:] = y[::-1]
        else:
            out[b, :S // 2] = y
    return out
```

```python
import contextlib
import numpy as np
import concourse.bass as bass
import concourse.mybir as mybir
from concourse.bass_utils import run_bass_kernel_spmd

F32 = mybir.dt.float32
BF16 = mybir.dt.bfloat16
AF = mybir.ActivationFunctionType
ALU = mybir.AluOpType

ENGS = ('pe', 'act', 'dve', 'pool', 'sp')


class _Op:
    __slots__ = ('eng', 'fn', 'deps', 'seq', 'blk', 'signal', 'ticket', 'dma', 'sem', 'val', 'key')


class Prog:
    def __init__(self, nc):
        self.nc = nc
        self.stack = contextlib.ExitStack()
        self.engs = {'pe': nc.tensor, 'act': nc.scalar, 'dve': nc.vector, 'pool': nc.gpsimd, 'sp': nc.sync}
        self.sem = {e: self.stack.enter_context(nc.semaphore('sem_' + e)) for e in ENGS}
        self.count = {e: 0 for e in ENGS}
        self.ops = {e: [] for e in ENGS}
        self.seq = {e: 0 for e in ENGS}
        self.last_write = {}
        self.readers = {}
        self.seen = {e: {} for e in ENGS}
        self.dma_sems = {}
        self.free_dma = []
        self.n_dsem = 0
        self.blk = 0
        self.n_instr = 0

    def _deps(self, o, reads, writes):
        deps = []
        for r in reads:
            lw = self.last_write.get(r)
            if lw is not None:
                deps.append((lw, 'raw'))
        for w in writes:
            lw = self.last_write.get(w)
            if lw is not None:
                deps.append((lw, 'waw'))
            for rd in self.readers.get(w, ()):
                deps.append((rd, 'war'))
        out = []
        seen = self.seen[o.eng]
        for d, kind in deps:
            if d is o:
                continue
            if d.dma:
                if d.blk != self.blk:
                    continue
                k = ('dma', d.key)
                if seen.get(k, 0) >= d.val:
                    continue
                seen[k] = d.val
                out.append(d)
            else:
                if d.blk != self.blk:
                    continue
                if d.eng == o.eng and not o.dma:
                    if o.eng == 'pe' or kind == 'war':
                        continue
                if seen.get(d.eng, -1) >= d.seq:
                    continue
                seen[d.eng] = d.seq
                out.append(d)
        o.deps = out
        for r in reads:
            lst = self.readers.setdefault(r, [])
            lst[:] = [x for x in lst if not (x.eng == o.eng and not x.dma and not o.dma)]
            lst.append(o)
        for w in writes:
            self.last_write[w] = o
            self.readers[w] = []

    def op(self, eng, fn, reads=(), writes=()):
        o = _Op()
        o.eng, o.fn, o.dma, o.signal, o.blk = eng, fn, False, False, self.blk
        o.seq = self.seq[eng]
        self.seq[eng] += 1
        self._deps(o, reads, writes)
        self.ops[eng].append(o)
        return o

    def dma(self, eng, fn, reads=(), writes=(), key=None):
        assert key is not None
        o = _Op()
        o.eng, o.fn, o.dma, o.signal, o.blk, o.key = eng, fn, True, False, self.blk, key
        o.seq = self.seq[eng]
        self.seq[eng] += 1
        if key not in self.dma_sems:
            if self.free_dma:
                self.dma_sems[key] = self.free_dma.pop()
            else:
                self.n_dsem += 1
                self.dma_sems[key] = [self.stack.enter_context(self.nc.semaphore('dsem%d' % self.n_dsem)), 0]
        ent = self.dma_sems[key]
        ent[1] += 16
        o.sem, o.val = ent[0], ent[1]
        self._deps(o, reads, writes)
        self.ops[eng].append(o)
        return o

    def flush(self, final=False):
        for e in ENGS:
            for o in self.ops[e]:
                for d in o.deps:
                    if not d.dma:
                        d.signal = True
        for e in ENGS:
            c = self.count[e]
            for o in self.ops[e]:
                if not o.dma and o.signal:
                    c += 1
                    o.ticket = c
            self.count[e] = c
        tail = []
        for key, (sem, tot) in self.dma_sems.items():
            if tot:
                tail.append((sem, tot))
        with self.nc.Block() as block:
            regs = {'pe': block.tensor, 'act': block.scalar, 'dve': block.vector,
                    'pool': block.gpsimd, 'sp': block.sync}
            for e in ENGS:
                ops = self.ops[e]
                if not ops and e != 'sp':
                    continue

                def body(eng, ops=ops, e=e):
                    for o in ops:
                        for d in o.deps:
                            if d.dma:
                                eng.wait_ge(d.sem, d.val)
                            else:
                                eng.wait_ge(self.sem[d.eng], d.ticket)
                        ins = o.fn(eng)
                        self.n_instr += 1
                        if o.dma:
                            ins.then_inc(o.sem, 16)
                        elif o.signal:
                            ins.then_inc(self.sem[e], 1)
                    if e == 'sp':
                        for sem, tot in tail:
                            eng.wait_ge(sem, tot)

                regs[e](body)
        self.ops = {e: [] for e in ENGS}
        self.seen = {e: {} for e in ENGS}
        self.free_dma.extend(self.dma_sems.values())
        self.dma_sems = {}
        self.blk += 1

    def finish(self):
        self.flush(final=True)
        self.stack.close()


D = 1024
NH_M = 4
DH_M = 256
NQ_A = 8
NKV_A = 2
DH_A = 128
DFF = 4096
EPS = 1e-6
C_QM, C_KM, C_QA, C_KA, C_VM, C_OM, C_VA, C_GT, C_END = 0, 1024, 2048, 3072, 3328, 4352, 5376, 5632, 5648
K_ID, K_ONE, K_LE, K_GE, K_ROT, K_END = 0, 128, 256, 384, 512, 640


class Ring:
    def __init__(self, name, n):
        self.name, self.n, self.i = name, n, -1

    def next(self):
        self.i = (self.i + 1) % self.n
        return self.i

    def res(self, i=None):
        return (self.name, self.i if i is None else i)


def build(S=8192, debug=(), stop_after=None):
    SO, NT, NB = S // 2, S // 128, S // 512
    NTO, NBO = NT // 2, NB // 2
    nc = bass.Bass("TRN2", target_bir_lowering=False)
    P = Prog(nc)

    def din(name, shape):
        return nc.dram_tensor(name, shape, F32, kind="ExternalInput").ap()

    def dscr(name, shape, dt):
        kind = "ExternalOutput" if name in debug else "Internal"
        return nc.dram_tensor(name, shape, dt, kind=kind).ap()

    x = din("x", [S, D]); c_in = din("c", [D]); w_ada = din("w_ada", [D, 6 * D]); b_ada = din("b_ada", [6 * D])
    n1pre = din("norm1_pre", [D]); n1post = din("norm1_post", [D]); n2pre = din("norm2_pre", [D]); n2post = din("norm2_post", [D])
    w_in = din("w_in", [D, C_END]); w_br = din("w_br", [D, 2 * D]); b_gates = din("b_gates", [16])
    conv_w = din("conv_w", [2 * D, 5]); conv_b = din("conv_b", [2 * D]); gn_w = din("mlstm_gn", [D])
    qn_w = din("attn_qnorm", [128]); kn_w = din("attn_knorm", [128])
    w_bm = din("w_branch_m", [D, D]); w_ba = din("w_branch_a", [D, D]); w_out = din("w_out", [D, D])
    w_m1 = din("w_mlp_in", [D, DFF]); w_m2 = din("w_mlp_out", [DFF, D])
    cos_t = din("cos_t", [128, S]); sin_t = din("sin_t", [128, S]); cst = din("cst", [128, K_END])
    y_out = nc.dram_tensor("y", [SO, D], F32, kind="ExternalOutput").ap()

    mod_scr = dscr("mod_scr", [6 * D], F32)
    hT_scr = dscr("hT_scr", [D, SO], BF16)
    qT_scr = dscr("qT_scr", [D, SO], BF16)
    kT_scr = dscr("kT_scr", [D, S], BF16)
    v_scr = dscr("v_scr", [S, D], BF16)
    og_scr = dscr("og_scr", [SO, D], BF16)
    QT_scr = dscr("QT_scr", [D, SO], BF16)
    KT_scr = dscr("KT_scr", [2 * 128, S], BF16)
    VA_scr = dscr("VA_scr", [S, 256], BF16)
    hB_scr = dscr("hB_scr", [SO, D], F32)
    ymT_scr = dscr("ymT_scr", [D, SO], BF16)
    yaT_scr = dscr("yaT_scr", [D, SO], BF16)
    x1_scr = dscr("x1_scr", [SO, D], F32)
    h2T_scr = dscr("h2T_scr", [D, SO], BF16)

    es = contextlib.ExitStack()
    em = contextlib.ExitStack()

    def sb(name, shape, dt=F32, stack=None):
        return (stack or es).enter_context(nc.sbuf_tensor(name, shape, dt))

    def ps(name, shape, dt=F32, stack=None):
        return (stack or es).enter_context(nc.psum_tensor(name, shape, dt))

    def mm(out, lhsT, rhs, start, stop, reads, writes):
        P.op('pe', lambda e: e.matmul(out, lhsT=lhsT, rhs=rhs, start=start, stop=stop), reads, writes)

    def tr(out, in_, ident, reads, writes):
        P.op('pe', lambda e: e.transpose(out, in_, ident), reads, writes)

    def act(out, in_, func, reads, writes, **kw):
        P.op('act', lambda e: e.activation(out=out, in_=in_, func=func, **kw), reads, writes)

    def tt(eng, out, a, b, op, reads, writes):
        P.op(eng, lambda e: e.tensor_tensor(out=out, in0=a, in1=b, op=op), reads, writes)

    def tsc(eng, out, a, s1, s2, op0, op1, reads, writes, **kw):
        P.op(eng, lambda e: e.tensor_scalar(out=out, in0=a, scalar1=s1, scalar2=s2, op0=op0, op1=op1, **kw), reads, writes)

    def stt(out, a, s, b, op0, op1, reads, writes, **kw):
        P.op('dve', lambda e: e.scalar_tensor_tensor(out=out, in0=a, scalar=s, in1=b, op0=op0, op1=op1, **kw), reads, writes)

    def cp(eng, out, in_, reads, writes):
        P.op(eng, lambda e: e.tensor_copy(out=out, in_=in_), reads, writes)

    def dma(q, out, in_, reads, writes, key, **kw):
        P.dma(q, lambda e: e.dma_start(out=out, in_=in_, **kw), reads, writes, key)

    NCD = dict(allow_slow_non_contiguous=True)
    CAST = dict(max_dma_last_dim=8192)

    cst_f = sb("cst_f", [128, K_END])
    ident_f = cst_f[:, K_ID:K_ID + 128]; ones_f = cst_f[:, K_ONE:K_ONE + 128]
    tri_le = cst_f[:, K_LE:K_LE + 128]; tri_ge = cst_f[:, K_GE:K_GE + 128]; rotT = cst_f[:, K_ROT:K_ROT + 128]
    cst_b = sb("cst_b", [128, 256], BF16)
    ident_b = cst_b[:, 0:128]; ones_b = cst_b[:, 128:256]
    modcols = sb("modcols", [128, 48])
    gs1 = sb("gs1", [128, 8]); gs2 = sb("gs2", [128, 8])
    G1 = sb("G1", [128, D]); G2 = sb("G2", [128, D])
    npre = sb("npre", [128, 16])
    eps_c = sb("eps_c", [128, 1])
    cw = sb("cw", [128, 16, 5], stack=em); cb = sb("cb", [128, 16], stack=em)
    qkn = sb("qkn", [128, 2], stack=em)
    bg_bc = sb("bg_bc", [128, 16], stack=em)
    gncol = sb("gncol", [128, 8], stack=em)
    GI = sb("GI", [128, NT, 16], stack=em)

    with contextlib.ExitStack() as sa:
        dma('sp', cst_f[:], cst, [], ['cst_f'], 'cst_f')
        cp('dve', cst_b[:], cst_f[:, 0:256], ['cst_f'], ['cst_b'])
        P.op('dve', lambda e: e.memset(eps_c[:], EPS), [], ['eps_c'])
        c_col = sb("c_col", [128, 8], stack=sa); sc_col = sb("sc_col", [128, 8], stack=sa)
        dma('sp', c_col[:], c_in.rearrange("(k p) -> p k", p=128), [], ['c_col'], 'c_col', **NCD)
        act(sc_col[:], c_col[:], AF.Silu, ['c_col'], ['sc_col'])
        badar = sb("badar", [1, 6 * D], stack=sa); modrow = sb("modrow", [1, 6 * D], stack=sa)
        dma('sp', badar[:], b_ada.rearrange("(o n) -> o n", o=1), [], ['badar'], 'badar')
        wa = sb("wa", [128, 2, 8, 512], stack=sa)
        mp = ps("mp", [128, 512], stack=sa)
        w_ada_v = w_ada.rearrange("(k p) n -> p k n", p=128)
        for g in range(12):
            b = g % 2
            dma('sp', wa[:, b], w_ada_v[:, :, g * 512:(g + 1) * 512], [], [('wa', b)], ('wa', b))
            for k in range(8):
                mm(mp[0:1, :], sc_col[:, k:k + 1], wa[:, b, k, :], k == 0, k == 7, ['sc_col', ('wa', b)], ['mp'])
            tt('dve', modrow[:, g * 512:(g + 1) * 512], mp[0:1, :], badar[:, g * 512:(g + 1) * 512], ALU.add,
               ['mp', 'badar'], ['modrow'])
        dma('sp', mod_scr.rearrange("(o n) -> o n", o=1), modrow[:], ['modrow'], ['mod_scr'], 'modrow')
        dma('sp', modcols[:], mod_scr.rearrange("(w p) -> p w", p=128), ['mod_scr'], ['modcols'], 'modcols', **NCD)
        g1b = sb("g1b", [128, D], stack=sa); g2b = sb("g2b", [128, D], stack=sa)
        dma('sp', g1b[:], mod_scr[2 * D:3 * D].partition_broadcast(128), ['mod_scr'], ['g1b'], 'g1b')
        dma('sp', g2b[:], mod_scr[5 * D:6 * D].partition_broadcast(128), ['mod_scr'], ['g2b'], 'g2b')
        dma('sp', G1[:], n1post.partition_broadcast(128), [], ['G1'], 'G1')
        dma('sp', G2[:], n2post.partition_broadcast(128), [], ['G2'], 'G2')
        tt('dve', G1[:], G1[:], g1b[:], ALU.mult, ['G1', 'g1b'], ['G1'])
        tt('dve', G2[:], G2[:], g2b[:], ALU.mult, ['G2', 'g2b'], ['G2'])
        dma('sp', npre[:, 0:8], n1pre.rearrange("(k p) -> p k", p=128), [], ['npre1'], 'npre1', **NCD)
        dma('sp', npre[:, 8:16], n2pre.rearrange("(k p) -> p k", p=128), [], ['npre2'], 'npre2', **NCD)
        stt(gs1[:], modcols[:, 8:16], 1.0, npre[:, 0:8], ALU.add, ALU.mult, ['modcols', 'npre1'], ['gs1'])
        stt(gs2[:], modcols[:, 32:40], 1.0, npre[:, 8:16], ALU.add, ALU.mult, ['modcols', 'npre2'], ['gs2'])
        dma('sp', cw[:], conv_w.rearrange("(c p) j -> p c j", p=128), [], ['cw'], 'cw')
        dma('sp', cb[:], conv_b.rearrange("(c p) -> p c", p=128), [], ['cb'], 'cb', **NCD)
        dma('sp', qkn[:, 0:1], qn_w.rearrange("(p o) -> p o", o=1), [], ['qkn0'], 'qkn0', **NCD)
        dma('sp', qkn[:, 1:2], kn_w.rearrange("(p o) -> p o", o=1), [], ['qkn1'], 'qkn1', **NCD)
        dma('sp', bg_bc[:], b_gates.partition_broadcast(128), [], ['bg_bc'], 'bg_bc')
        dma('sp', gncol[:], gn_w.rearrange("(k p) -> p k", p=128), [], ['gncol'], 'gncol', **NCD)
        P.flush()
    sh1 = modcols[:, 0:8]; sh2 = modcols[:, 24:32]
    if stop_after == 'A':
        P.finish(); em.close(); es.close(); return nc

    with contextlib.ExitStack() as sbk:
        w_sb = sb("w_sb", [128, 8, C_END], BF16, stack=sbk)
        w_in_v = w_in.rearrange("(k p) n -> p k n", p=128)
        wgroups = [(C_QM, C_KM), (C_KM, C_QA), (C_QA, C_KA), (C_KA, C_VM), (C_VM, C_OM), (C_OM, C_VA), (C_VA, C_END)]

        def wres(c0):
            for a, b in wgroups:
                if a <= c0 < b:
                    return ('w_sb', a)

        for a, b in wgroups:
            dma('pool', w_sb[:, :, a:b], w_in_v[:, :, a:b], [], [('w_sb', a)], ('w_sb', a), **CAST)
        xt = sb("xt", [128, 3, D], stack=sbk); junk = sb("junk", [128, D], BF16, stack=sbk)
        sst = sb("sst", [128, 3, 4], stack=sbk)
        hn = sb("hn", [128, 2, D], stack=sbk)
        hT = sb("hT", [128, 2, 8, 512], BF16, stack=sbk)
        rawb = sb("rawb", [128, 3, 520], stack=sbk); halo = sb("halo", [128, 16, 4], stack=sbk)
        cacc = sb("cacc", [128, 2, 512], stack=sbk); sout = sb("sout", [128, 3, 512], BF16, stack=sbk)
        cs = sb("cs", [128, 2, 2, 512], stack=sbk)
        sq = sb("sq", [128, 3, 512], BF16, stack=sbk); qw = sb("qw", [128, 3, 512], stack=sbk)
        rs = sb("rs", [128, 3, 512], stack=sbk); t1 = sb("t1", [128, 3, 512], stack=sbk)
        t2 = sb("t2", [128, 2, 512], stack=sbk); rout = sb("rout", [128, 3, 512], BF16, stack=sbk)
        vt = sb("vt", [128, 2, D], BF16, stack=sbk); ogt = sb("ogt", [128, 2, D], BF16, stack=sbk)
        vat = sb("vat", [128, 2, 256], BF16, stack=sbk)
        tp = [ps("tp%d" % i, [128, 512], stack=sbk) for i in range(2)]
        fa = [ps("fa%d" % i, [128, 512], stack=sbk) for i in range(3)]
        pss = ps("pss", [128, 512], stack=sbk)
        psr = [ps("psr%d" % i, [128, 512], stack=sbk) for i in range(2)]
        R_x, R_fa, R_raw, R_cacc, R_sout = Ring('xt', 3), Ring('fa', 3), Ring('rawb', 3), Ring('cacc', 2), Ring('sout', 3)
        R_rope, R_psr, R_t2, R_hn = Ring('rope', 3), Ring('psr', 2), Ring('t2', 2), Ring('hn', 2)
        P.op('pool', lambda e: e.memset(halo[:], 0.0), [], [('halo', c) for c in range(16)])
        mhalf = sb("mhalf", [128, 1], stack=sbk)
        P.op('pool', lambda e: e.memset(mhalf[:], -0.5), [], ['mhalf'])
        hT_sv = hT_scr.rearrange("(k p) n -> p k n", p=128)

        def hres(hb):
            return [('hT', hb, t) for t in range(4)]

        def conv_item(ch, hb, blk, tok0, dst, tmax, width=512, tail=False):
            st = {}

            def s0():
                if tail:
                    return
                c0 = (C_QM if ch < 8 else C_KM - 1024) + ch * 128
                st['f'] = f = R_fa.next()
                for k in range(8):
                    mm(fa[f][:], w_sb[:, k, c0:c0 + 128], hT[:, hb, k, :], k == 0, k == 7, [wres(c0)] + hres(hb), [('fa', f)])

            def s1():
                st['rb'] = rb = R_raw.next(); rr = ('rawb', rb)
                if tail:
                    P.op('pool', lambda e: e.memset(rawb[:, rb, 4:8], 0.0), [], [rr])
                else:
                    act(rawb[:, rb, 4:4 + 512], fa[st['f']][:], AF.Copy, [('fa', st['f'])], [rr])
                cp('pool', rawb[:, rb, 0:4], halo[:, ch, :], [('halo', ch), rr], [rr])
                if not tail:
                    cp('pool', halo[:, ch, :], rawb[:, rb, 512:516], [rr], [('halo', ch)])

            def s2():
                rb = st['rb']; rr = ('rawb', rb)
                st['ca'] = ca = R_cacc.next(); rc = ('cacc', ca)
                tsc('dve', cacc[:, ca, 0:width], rawb[:, rb, 0:width], cw[:, ch, 0:1], None, ALU.mult, ALU.bypass, [rr, 'cw'], [rc])
                for j in range(1, 5):
                    stt(cacc[:, ca, 0:width], rawb[:, rb, j:j + width], cw[:, ch, j:j + 1], cacc[:, ca, 0:width],
                        ALU.mult, ALU.add, [rr, 'cw', rc], [rc])

            def s3():
                ca = st['ca']; rc = ('cacc', ca)
                so = R_sout.next(); rso = ('sout', so)
                act(sout[:, so, 0:width], cacc[:, ca, 0:width], AF.Silu, [rc, 'cb'], [rso], bias=cb[:, ch:ch + 1])
                lo = max(tok0 - 2, 0); hi = min(tok0 - 2 + width, tmax)
                if hi > lo:
                    j0 = lo - (tok0 - 2)
                    cc = ch % 8
                    dma('sp', dst[cc * 128:(cc + 1) * 128, lo:hi], sout[:, so, j0:j0 + (hi - lo)], [rso], [], rso)

            return [s0, s1, s2, s3]

        def rope_item(c0, wcol, hb, csb, dst_ap):
            st = {}

            def s0():
                st['f'] = f = R_fa.next()
                for k in range(8):
                    mm(fa[f][:], w_sb[:, k, c0:c0 + 128], hT[:, hb, k, :], k == 0, k == 7, [wres(c0)] + hres(hb), [('fa', f)])

            def s1():
                st['r'] = r = R_rope.next()
                act(sq[:, r], fa[st['f']][:], AF.Square, [('fa', st['f'])], [('sq', r)])
                act(qw[:, r], fa[st['f']][:], AF.Copy, [('fa', st['f'])], [('qw', r)], scale=qkn[:, wcol:wcol + 1])

            def s2():
                r = st['r']
                st['pr'] = pr = R_psr.next()
                mm(pss[:], ones_b, sq[:, r], True, True, [('sq', r), 'cst_b'], ['pss'])
                mm(psr[pr][:], rotT, qw[:, r], True, True, [('qw', r), 'cst_f'], [('psr', pr)])
                act(rs[:, r], pss[:], AF.Ln, ['pss', 'eps_c'], [('rs', r)], scale=1.0 / 128.0, bias=eps_c[:, 0:1])
                act(rs[:, r], rs[:, r], AF.Exp, [('rs', r)], [('rs', r)], scale=-0.5)
                tt('dve', t1[:, r], qw[:, r], cs[:, csb, 0], ALU.mult, [('qw', r), ('cs', csb)], [('t1', r)])

            def s3():
                r = st['r']; pr = st['pr']
                t = R_t2.next()
                tt('dve', t2[:, t], psr[pr][:], cs[:, csb, 1], ALU.mult, [('psr', pr), ('cs', csb)], [('t2', t)])
                tt('dve', t1[:, r], t1[:, r], t2[:, t], ALU.add, [('t1', r), ('t2', t)], [('t1', r)])
                tt('dve', rout[:, r], t1[:, r], rs[:, r], ALU.mult, [('t1', r), ('rs', r)], [('rout', r)])
                dma('sp', dst_ap, rout[:, r], [('rout', r)], [], ('rout', r))

            return [s0, s1, s2, s3]

        def tm_item(t, c0, n, hb, post):
            st = {}

            def s0():
                st['f'] = f = R_fa.next()
                for k in range(8):
                    mm(fa[f][:, 0:n], hT[:, hb, k, t * 128:(t + 1) * 128], w_sb[:, k, c0:c0 + n], k == 0, k == 7,
                       [wres(c0), ('hT', hb, t)], [('fa', f)])

            def s1():
                post(fa[st['f']], ('fa', st['f']))

            return [s0, s1]

        def x_items(blk, hb):
            items = []; dmas = []
            for t in range(4):
                st = {}

                def xd(t=t, st=st):
                    tile = blk * 4 + t
                    st['xb'] = xb = R_x.next(); rx = ('xt', xb)
                    dma('sp', xt[:, xb], x[tile * 128:(tile + 1) * 128, :], [], [rx], rx)

                def xa(t=t, st=st):
                    xb = st['xb']; rx = ('xt', xb)
                    st['hb_'] = hb_ = R_hn.next()
                    stt(junk[:], xt[:, xb], 1.0, xt[:, xb], ALU.mult, ALU.mult, [rx], ['junk', ('sst0', xb)], accum_out=sst[:, xb, 0:1])
                    tsc('dve', sst[:, xb, 1:2], sst[:, xb, 0:1], 1.0 / D, EPS, ALU.mult, ALU.add, [('sst0', xb)], [('sst1', xb)])
                    tt('pool', sst[:, xb, 2:3], sst[:, xb, 1:2], mhalf[:, 0:1], ALU.pow, [('sst1', xb), 'mhalf'], [('sst2', xb)])
                    act(hn[:, hb_], xt[:, xb], AF.Copy, [rx, ('sst2', xb)], [('hn', hb_)], scale=sst[:, xb, 2:3])

                def xb_(t=t, st=st):
                    hb_ = st['hb_']
                    for half in range(2):
                        for kk in range(4):
                            k = half * 4 + kk
                            tr(tp[half][:, kk * 128:(kk + 1) * 128], hn[:, hb_, k * 128:(k + 1) * 128], ident_f,
                               [('hn', hb_), 'cst_f'], [('tp', half)])
                        for kk in range(4):
                            k = half * 4 + kk
                            act(hT[:, hb, k, t * 128:(t + 1) * 128], tp[half][:, kk * 128:(kk + 1) * 128], AF.Identity,
                                [('tp', half), 'gs1', 'modcols'], [('hT', hb, t)], scale=gs1[:, k:k + 1], bias=sh1[:, k:k + 1])
                    if t == 3 and blk < NBO:
                        dma('sp', hT_sv[:, :, blk * 512:(blk + 1) * 512], hT[:, hb], hres(hb), [('hT_scr', blk)], ('hTst', hb))

                items.append([xa]); items.append([xb_]); dmas.append([xd])
            return items, dmas

        def block_items(blk, hb):
            own = blk < NBO
            csb = blk % 2
            items = []

            def csload():
                dma('sp', cs[:, csb, 0], cos_t[:, blk * 512:(blk + 1) * 512], [], [('cs', csb)], ('cs', csb))
                dma('sp', cs[:, csb, 1], sin_t[:, blk * 512:(blk + 1) * 512], [], [('cs', csb)], ('cs', csb))

            items.append([csload])
            LC, LR, LT, LTV = [], [], [], []
            if blk <= NBO:
                for ch in range(8):
                    LC.append(conv_item(ch, hb, blk, blk * 512, qT_scr, SO))
            for ch in range(8):
                LC.append(conv_item(8 + ch, hb, blk, blk * 512, kT_scr, S))
            if own:
                for h in range(8):
                    LR.append(rope_item(C_QA + h * 128, 0, hb, csb, QT_scr[h * 128:(h + 1) * 128, blk * 512:(blk + 1) * 512]))
            for g in range(2):
                LR.append(rope_item(C_KA + g * 128, 1, hb, csb, KT_scr[g * 128:(g + 1) * 128, blk * 512:(blk + 1) * 512]))
            items_tm = LT
            for t in range(4):
                tile = blk * 4 + t
                tk = slice(tile * 128, (tile + 1) * 128)
                vb = tile % 2

                def post_v(half, vb=vb, tk=tk):
                    def f(pt, ra):
                        rv = ('vt', vb, half)
                        if half:
                            act(vt[:, vb, 512:1024], pt[:], AF.Copy, [ra], [rv])
                            dma('sp', v_scr[tk, :], vt[:, vb], [('vt', vb, 0), rv], [], ('vtst', vb))
                        else:
                            cp('dve', vt[:, vb, 0:512], pt[:], [ra], [rv])
                    return f

                def post_o(half, vb=vb, tk=tk):
                    def f(pt, ra):
                        ro = ('ogt', vb, half)
                        act(ogt[:, vb, half * 512:(half + 1) * 512], pt[:], AF.Sigmoid, [ra], [ro])
                        if half:
                            dma('sp', og_scr[tk, :], ogt[:, vb], [('ogt', vb, 0), ro], [], ('ogst', vb))
                    return f

                def post_a(pt, ra, vb=vb, tk=tk, tile=tile):
                    rva = ('vat', vb)
                    cp('dve', vat[:, vb], pt[:, 0:256], [ra], [rva])
                    tt('dve', GI[:, tile, :], pt[:, 256:272], bg_bc[:], ALU.add, [ra, 'bg_bc'], [('GI', tile)])
                    dma('sp', VA_scr[tk, :], vat[:, vb], [rva], [], rva)

                for half in range(2):
                    LTV.append(tm_item(t, C_VM + half * 512, 512, hb, post_v(half)))
                if own:
                    for half in range(2):
                        LT.append(tm_item(t, C_OM + half * 512, 512, hb, post_o(half)))
                LT.append(tm_item(t, C_VA, 272, hb, post_a))
            ic, iv = 0, 0
            while ic < len(LC) or iv < len(LTV):
                take_c = (len(LC) - ic) * max(1, len(LTV)) >= (len(LTV) - iv) * max(1, len(LC))
                if ic < len(LC) and (take_c or iv >= len(LTV)):
                    items.append(LC[ic]); ic += 1
                else:
                    items.append(LTV[iv]); iv += 1
            items.extend(LR); items.extend(LT)
            return items

        sched = []
        xi0, xd0 = x_items(0, 0)
        pend_dma = list(xd0)
        sched.append(pend_dma.pop(0))
        for it_ in xi0:
            if it_[0].__name__ == 'xa' and pend_dma:
                sched.append(pend_dma.pop(0))
            sched.append(it_)
        for blk in range(NB):
            bi = block_items(blk, blk % 2)
            xi, xdn = x_items(blk + 1, (blk + 1) % 2) if blk + 1 < NB else ([], [])
            pend_dma.extend(xdn)
            if pend_dma:
                sched.append(pend_dma.pop(0))
            n = len(bi)
            pos = {}
            for j in range(len(xi)):
                p_ = min(n - 1, (n * (2 * j + 1)) // (2 * len(xi)))
                pos.setdefault(p_, []).append(xi[j])
            for i_, it_ in enumerate(bi):
                sched.append(it_)
                for xj in pos.get(i_, []):
                    if xj[0].__name__ == 'xa' and pend_dma:
                        sched.append(pend_dma.pop(0))
                    sched.append(xj)
        for ch in range(8):
            sched.append(conv_item(8 + ch, 0, NB, S, kT_scr, S, width=2, tail=True))
        depth = 4
        for it in range(len(sched) + depth - 1):
            for k in [0, 3, 2, 1]:
                i_ = it - k
                if 0 <= i_ < len(sched) and k < len(sched[i_]):
                    sched[i_][k]()
        P.flush()
    if stop_after == 'B':
        P.finish(); em.close(); es.close(); return nc

    NG = NT * 4
    ea = sb("ea", [128, 2, NG], stack=em); thr = sb("thr", [128, 2, NG], stack=em); dec = sb("dec", [128, 2, NG], stack=em)
    one_c = sb("one_c", [128, 2], stack=em)
    with contextlib.ExitStack() as sc_:
        nlf = sb("nlf", [128, 2, NG], stack=sc_); Dn = sb("Dn", [128, 2, NG], stack=sc_)
        tmpg = sb("tmpg", [128, 2, NG], stack=sc_)
        bbp = [ps("bbp%d" % i, [128, 512], stack=sc_) for i in range(2)]
        btp = [ps("btp%d" % i, [128, 512], stack=sc_) for i in range(2)]
        P.op('dve', lambda e: e.memset(one_c[:, 0:1], 1.0), [], ['one_c'])
        P.op('dve', lambda e: e.memset(one_c[:, 1:2], float(np.log(16.0))), [], ['one_c'])
        v3 = lambda ap: ap.rearrange("p (t h) -> p t h", h=4)
        for d in range(2):
            rd = ('gate', d)
            act(v3(nlf[:, d]), GI[:, :, 4 + 8 * d:8 + 8 * d], AF.Exp, [('GI', t) for t in range(NT)], [rd], scale=-1.0)
            act(nlf[:, d], nlf[:, d], AF.Ln, [rd, 'one_c'], [rd], bias=one_c[:, 0:1])
            mm(bbp[d][:, 0:NG], tri_le if d == 0 else tri_ge, nlf[:, d], True, True, [rd, 'cst_f'], [('bbp', d)])
            mm(btp[d][:, 0:NG], ones_f, nlf[:, d], True, True, [rd, 'cst_f'], [('btp', d)])
            act(Dn[:, d], bbp[d][:, 0:NG], AF.Copy, [('bbp', d)], [rd])
            tt('dve', Dn[:, d], Dn[:, d], btp[d][:, 0:NG], ALU.subtract, [rd, ('btp', d)], [rd])
            tt('dve', v3(tmpg[:, d]), v3(Dn[:, d]), GI[:, :, 8 * d:8 * d + 4], ALU.add,
               [rd] + [('GI', t) for t in range(NT)], [rd])
            act(thr[:, d], Dn[:, d], AF.Exp, [rd, 'one_c'], [('thr', d)], bias=one_c[:, 1:2])
            act(ea[:, d], tmpg[:, d], AF.Exp, [rd], [('ea', d)])
            act(dec[:, d], btp[d][:, 0:NG], AF.Exp, [('btp', d)], [('dec', d)], scale=-1.0)
        P.flush()
    if stop_after == 'C':
        P.finish(); em.close(); es.close(); return nc

    with contextlib.ExitStack() as sd:
        kTt = sb("kTt", [128, 3, 8, 128], BF16, stack=sd); qTt = sb("qTt", [128, 3, 8, 128], BF16, stack=sd)
        vext = sb("vext", [128, 3, 4, 258], BF16, stack=sd)
        ktok = sb("ktok", [128, 2, 1024], BF16, stack=sd); vs = sb("vs", [128, 2, 4, 258], BF16, stack=sd)
        wT = sb("wT", [128, 2, 4, 128], BF16, stack=sd)
        Cacc = sb("Cacc", [128, 4, 2, 258], stack=sd); Cbf = sb("Cbf", [128, 4, 2, 258], BF16, stack=sd)
        ddt = sb("ddt", [128, 2, 8], stack=sd)
        hdir = sb("hdir", [128, 2, D], stack=sd); hBt = sb("hBt", [128, 2, D], stack=sd)
        ogl = sb("ogl", [128, 2, D], BF16, stack=sd)
        stats = sb("stats", [128, 4, 6], stack=sd); mv = sb("mv", [128, 4, 2], stack=sd); rg = sb("rg", [128, 8], stack=sd)
        ym = sb("ym", [128, 2, D], BF16, stack=sd); ymT = sb("ymT", [128, 2, 8, 128], BF16, stack=sd)
        bT = ps("bT", [128, 1024], BF16, stack=sd)
        bS = ps("bS", [128, 512], stack=sd)
        bN = [ps("bN%d" % i, [128, 512], stack=sd) for i in range(2)]
        bX = ps("bX", [128, 512], stack=sd)
        bC = [ps("bC%d" % i, [128, 512], stack=sd) for i in range(3)]
        R_ld, R_kk, R_vv, R_ww, R_bC, R_dd = Ring('ld', 3), Ring('ktok', 2), Ring('vs', 2), Ring('wT', 2), Ring('bC', 3), Ring('dd', 2)
        R_hd, R_hB, R_ogl, R_ym = (Ring(n, 2) for n in ('hdir', 'hBt', 'ogl', 'ym'))
        kT_v = kT_scr.rearrange("(c p) n -> p c n", p=128); qT_v = qT_scr.rearrange("(c p) n -> p c n", p=128)
        ymT_v = ymT_scr.rearrange("(c p) n -> p c n", p=128)
        for i in range(3):
            P.op('pool', lambda e, i=i: e.memset(vext[:, i, :, 256:258], 1.0), [], [('vext', i)])

        for d in (1, 0):
            P.op('pool', lambda e: e.memset(Cacc[:], 0.0), [], [('Cacc', h) for h in range(4)])
            mask = tri_ge if d == 1 else tri_le
            tiles = list(range(NT - 1, -1, -1)) if d == 1 else list(range(NTO))
            nst = len(tiles)
            info = {}

            def load(i):
                tile = tiles[i]; full = tile < NTO
                lb = R_ld.next()
                tk = slice(tile * 128, (tile + 1) * 128)
                dma('sp', kTt[:, lb], kT_v[:, :, tk], [], [('kTt', lb)], ('kTt', lb))
                dma('sp', vext[:, lb, :, 0:256], v_scr[tk, :].rearrange("p (h e) -> p h e", h=4), [], [('vext', lb)], ('vext', lb))
                if full:
                    dma('sp', qTt[:, lb], qT_v[:, :, tk], [], [('qTt', lb)], ('qTt', lb))
                info[i] = dict(tile=tile, full=full, lb=lb, tk=tk)

            def pe_a(i):
                st = info[i]; lb = st['lb']
                for c in range(8):
                    tr(bT[:, c * 128:(c + 1) * 128], kTt[:, lb, c, :], ident_b, [('kTt', lb), 'cst_b'], ['bT'])
                if st['full']:
                    for h in range(4):
                        for j in range(2):
                            mm(bS[:, h * 128:(h + 1) * 128], kTt[:, lb, 2 * h + j, :], qTt[:, lb, 2 * h + j, :], j == 0, j == 1,
                               [('kTt', lb), ('qTt', lb)], ['bS'])

            def ev_a(i):
                st = info[i]; lb = st['lb']; c0 = st['tile'] * 4
                st['kk'] = kk = R_kk.next(); st['vv'] = vv = R_vv.next()
                act(ktok[:, kk], bT[:], AF.Copy, ['bT'], [('ktok', kk)])
                for h in range(4):
                    tsc('dve', vs[:, vv, h, :], vext[:, lb, h, :], ea[:, d, c0 + h:c0 + h + 1], None, ALU.mult, ALU.bypass,
                        [('vext', lb), ('ea', d)], [('vs', vv)])
                if st['full']:
                    st['ww'] = ww = R_ww.next()
                    for h in range(4):
                        tt('dve', wT[:, ww, h, :], bS[:, h * 128:(h + 1) * 128], mask, ALU.mult, ['bS', 'cst_f'], [('wT', ww)])

            def pe_b(i):
                pass

            def ev_b(i):
                st = info[i]; lb = st['lb']; kk = st['kk']; vv = st['vv']
                c0 = st['tile'] * 4; tile = st['tile']; tk = st['tk']
                bc = {}

                def dC(h):
                    bc[h] = c = R_bC.next()
                    for j in range(2):
                        mm(bC[c][:, j * 256:(j + 1) * 256], ktok[:, kk, h * 256 + j * 128:h * 256 + (j + 1) * 128],
                           vs[:, vv, h, 0:256], True, True, [('ktok', kk), ('vs', vv)], [('bC', c)])

                def upd(h):
                    dcc = dec[:, d, c0 + h:c0 + h + 1]; c = bc[h]
                    stt(Cacc[:, h, :, 0:256], Cacc[:, h, :, 0:256], dcc, bC[c][:].rearrange("p (j e) -> p j e", j=2), ALU.mult, ALU.add,
                        [('Cacc', h), ('dec', d), ('bC', c)], [('Cacc', h)])

                for h in range(3):
                    dC(h)
                for h in range(4):
                    for j in range(2):
                        mm(bX[:, 8 + 4 * h + 2 * j:10 + 4 * h + 2 * j], ktok[:, kk, h * 256 + j * 128:h * 256 + (j + 1) * 128],
                           vs[:, vv, h, 256:258], True, True, [('ktok', kk), ('vs', vv)], ['bX'])
                for h in range(3):
                    upd(h)
                for h in range(4):
                    dcc = dec[:, d, c0 + h:c0 + h + 1]
                    stt(Cacc[:, h, :, 256:258], Cacc[:, h, :, 256:258], dcc,
                        bX[:, 8 + 4 * h:12 + 4 * h].rearrange("p (j e) -> p j e", j=2), ALU.mult, ALU.add,
                        [('Cacc', h), ('dec', d), 'bX'], [('Cacc', h)])
                dC(3)
                upd(3)
                if st['full']:
                    ww = st['ww']
                    for h in range(4):
                        nbk = bN[h // 2]; o0 = (h % 2) * 256
                        mm(nbk[:, o0:o0 + 256], wT[:, ww, h, :], vs[:, vv, h, 0:256], True, False, [('wT', ww), ('vs', vv)], [('bN', h // 2)])
                        for j in range(2):
                            mm(nbk[:, o0:o0 + 256], qTt[:, lb, 2 * h + j, :], Cbf[:, h, j, 0:256], False, j == 1,
                               [('qTt', lb), ('Cbf', h)], [('bN', h // 2)])
                        mm(bX[:, 2 * h:2 * h + 2], wT[:, ww, h, :], vs[:, vv, h, 256:258], True, False, [('wT', ww), ('vs', vv)], ['bX'])
                        for j in range(2):
                            mm(bX[:, 2 * h:2 * h + 2], qTt[:, lb, 2 * h + j, :], Cbf[:, h, j, 256:258], False, j == 1,
                               [('qTt', lb), ('Cbf', h)], ['bX'])
                if not st['full']:
                    return
                di = R_dd.next(); rdd = ('dd', di)
                hb = R_hd.next(); rhd = ('hdir', hb)
                act(ddt[:, di, 0:4], bX[:, 0:8].rearrange("p (h e) -> p h e", e=2)[:, :, 0], AF.Abs, ['bX'], [rdd])
                tt('dve', ddt[:, di, 0:4], ddt[:, di, 0:4], thr[:, d, c0:c0 + 4], ALU.max, [rdd, ('thr', d)], [rdd])
                P.op('dve', lambda e, di=di: e.reciprocal(out=ddt[:, di, 4:8], in_=ddt[:, di, 0:4]), [rdd], [rdd])
                for h in range(4):
                    nbk = bN[h // 2]; o0 = (h % 2) * 256
                    if d == 0:
                        stt(hdir[:, hb, h * 256:(h + 1) * 256], nbk[:, o0:o0 + 256], ddt[:, di, 4 + h:5 + h],
                            hBt[:, st['hBb'], h * 256:(h + 1) * 256], ALU.mult, ALU.add,
                            [('bN', h // 2), rdd, ('hBt', st['hBb'])], [rhd])
                    elif h % 2 == 0:
                        act(hdir[:, hb, h * 256:(h + 1) * 256], nbk[:, o0:o0 + 256], AF.Copy, [('bN', h // 2), rdd], [rhd],
                            scale=ddt[:, di, 4 + h:5 + h])
                    else:
                        tsc('dve', hdir[:, hb, h * 256:(h + 1) * 256], nbk[:, o0:o0 + 256], ddt[:, di, 4 + h:5 + h], None,
                            ALU.mult, ALU.bypass, [('bN', h // 2), rdd], [rhd])
                if d == 1:
                    dma('sp', hB_scr[tk, :], hdir[:, hb], [rhd], [('hB_scr', tile)], rhd)
                    return
                bb_ = st['hBb']; rhB = ('hBt', bb_); ob = st['ogb']; rol = ('ogl', ob)
                for h in range(4):
                    P.op('dve', lambda e, h=h, hb=hb: e.bn_stats(out=stats[:, h, :], in_=hdir[:, hb, h * 256:(h + 1) * 256]),
                         [rhd], [('stats', h)])
                    P.op('dve', lambda e, h=h: e.bn_aggr(out=mv[:, h, :], in_=stats[:, h, :]), [('stats', h)], [('mv', h)])
                act(rg[:, 0:4], mv[:, :, 1], AF.Sqrt, [('mv', h) for h in range(4)] + ['eps_c'], ['rg'], bias=eps_c[:, 0:1])
                P.op('dve', lambda e: e.reciprocal(out=rg[:, 4:8], in_=rg[:, 0:4]), ['rg'], ['rg'])
                for h in range(4):
                    tsc('dve', hdir[:, hb, h * 256:(h + 1) * 256], hdir[:, hb, h * 256:(h + 1) * 256], mv[:, h, 0:1],
                        rg[:, 4 + h:5 + h], ALU.subtract, ALU.mult, [rhd, ('mv', h), 'rg'], [rhd])
                yb = R_ym.next(); rym = ('ym', yb)
                tt('dve', ym[:, yb], hdir[:, hb], ogl[:, ob], ALU.mult, [rhd, rol], [rym])
                st['yb'] = yb

            def pe_c(i):
                st = info[i]
                if d != 0 or not st['full']:
                    return
                yb = st['yb']; rym = ('ym', yb); rymT = ('ymT', yb)
                for k in range(8):
                    tr(bT[:, k * 128:(k + 1) * 128], ym[:, yb, k * 128:(k + 1) * 128], ident_b, [rym, 'cst_b'], ['bT'])
                for k in range(8):
                    act(ymT[:, yb, k, :], bT[:, k * 128:(k + 1) * 128], AF.Copy, ['bT', 'gncol'], [rymT], scale=gncol[:, k:k + 1])
                dma('sp', ymT_v[:, :, st['tk']], ymT[:, yb], [rymT], [('ymT_scr', st['tile'] // 4)], rymT)

            def cbf(i):
                st = info[i]
                if not st['full']:
                    return
                c0 = st['tile'] * 4
                for h in range(4):
                    act(Cbf[:, h], Cacc[:, h], AF.Copy, [('Cacc', h), ('dec', d)], [('Cbf', h)], scale=dec[:, d, c0 + h:c0 + h + 1])

            def epi_loads(i):
                st = info[i]
                if d == 0 and st['full']:
                    st['hBb'] = bb_ = R_hB.next(); st['ogb'] = ob = R_ogl.next()
                    dma('sp', hBt[:, bb_], hB_scr[st['tk'], :], [('hB_scr', st['tile'])], [('hBt', bb_)], ('hBt', bb_))
                    dma('sp', ogl[:, ob], og_scr[st['tk'], :], [], [('ogl', ob)], ('ogl', ob))

            load(0)
            for i in range(nst + 2):
                if i + 1 < nst:
                    load(i + 1)
                if i < nst:
                    epi_loads(i)
                    pe_a(i)
                    ev_a(i)
                if 0 <= i - 2 < nst:
                    pe_c(i - 2)
                if 0 <= i - 1 < nst:
                    pe_b(i - 1)
                    ev_b(i - 1)
                if i < nst:
                    cbf(i)
        P.flush()
    if stop_after == 'D':
        P.finish(); em.close(); es.close(); return nc

    with contextlib.ExitStack() as se:
        KT = sb("KT", [128, 2, S], BF16, stack=se); VA = sb("VA", [128, NT, 256], BF16, stack=se)
        for g in range(2):
            dma('sp', KT[:, g, :], KT_scr[g * 128:(g + 1) * 128, :], [], [('KT', g)], ('KT', g))
        VA_v = VA_scr.rearrange("(t p) c -> p t c", p=128)
        nv = max(1, NT // 8)
        for i in range(0, NT, nv):
            dma('sp', VA[:, i:i + nv, :], VA_v[:, i:i + nv, :], [], [('VA', i)], ('VA', i))
        vres = lambda k: ('VA', (k // nv) * nv)
        QTt = sb("QTt", [128, 3, 512], BF16, stack=se); pT = sb("pT", [128, 6, 2, 512], BF16, stack=se)
        rden = sb("rden", [128, 2, 512], stack=se); yo = sb("yo", [128, 2, 512], BF16, stack=se)
        accd = sb("accd", [128, 2, 2, 512], BF16, stack=se)
        scp = [ps("scp%d" % i, [128, 1024], stack=se) for i in range(2)]
        op2 = [ps("op%d" % i, [128, 512], stack=se) for i in range(2)]
        dn2_ = [ps("dnp%d" % i, [128, 512], stack=se) for i in range(2)]
        R_Q, R_sc, R_pT, R_yo, R_acc = Ring('QTt', 3), Ring('scp', 2), Ring('pT', 6), Ring('yo', 2), Ring('accd', 2)
        att_scale = float(DH_A) ** -0.5
        NP = NT // 2
        heads = [(qb, hd) for qb in range(NBO) for hd in range(8)]
        qslot = {}

        def qload(ix):
            qb_, hd_ = heads[ix]
            qslot[ix] = s_ = R_Q.next()
            dma('sp', QTt[:, s_], QT_scr[hd_ * 128:(hd_ + 1) * 128, qb_ * 512:(qb_ + 1) * 512], [], [('QTt', s_)], ('QTt', s_))

        qload(0)
        for ix, (qb, hd) in enumerate(heads):
            if True:
                g = hd // 4
                if ix + 1 < len(heads):
                    qload(ix + 1)
                q_ = qslot[ix]; rQ = ('QTt', q_)
                ab = R_acc.next()
                op_ = op2[ab]; dnp = dn2_[ab]
                ro = ('op', ab); rdn = ('dnp', ab)
                pbuf = {}
                first = {'dve': True, 'pe': True}
                for step in range(NP + 2):
                    if step < NP:
                        si = R_sc.next(); pi = R_pT.next()
                        for u in range(2):
                            kb = 2 * step + u
                            mm(scp[si][:, u * 512:(u + 1) * 512], KT[:, g, kb * 128:(kb + 1) * 128], QTt[:, q_], True, True,
                               [('KT', g), rQ], [('scp', si)])
                        act(pT[:, pi].rearrange("p u n -> p (u n)"), scp[si][:], AF.Exp, [('scp', si)], [('pT', pi)],
                            scale=att_scale)
                        pbuf[step] = pi
                    p_ = step - 2
                    if p_ >= 0:
                        pi = pbuf.pop(p_)
                        for u in range(2):
                            k = 2 * p_ + u
                            mm(op_[:], VA[:, k, g * 128:(g + 1) * 128], pT[:, pi, u], k == 0, k == NT - 1,
                               [vres(k), ('pT', pi)], [ro])
                        for u in range(2):
                            k = 2 * p_ + u
                            who = 'pe' if k % 4 == 3 else 'dve'
                            if who == 'pe':
                                mm(dnp[:], ones_b, pT[:, pi, u], first['pe'], False, ['cst_b', ('pT', pi)], [rdn])
                            else:
                                ra = ('accd', ab, 0)
                                if first[who]:
                                    cp(who, accd[:, ab, 0], pT[:, pi, u], [('pT', pi)], [ra])
                                else:
                                    tt(who, accd[:, ab, 0], accd[:, ab, 0], pT[:, pi, u], ALU.add, [ra, ('pT', pi)], [ra])
                            first[who] = False
                mm(dnp[:], ones_b, accd[:, ab, 0], False, True, ['cst_b', ('accd', ab, 0)], [rdn])
                yb = R_yo.next(); ry = ('yo', yb)
                P.op('dve', lambda e, yb=yb, dnp=dnp: e.reciprocal(out=rden[:, yb], in_=dnp[:]), [rdn], [('rden', yb)])
                tt('dve', yo[:, yb], op_[:], rden[:, yb], ALU.mult, [ro, ('rden', yb)], [ry])
                dma('sp', yaT_scr[hd * 128:(hd + 1) * 128, qb * 512:(qb + 1) * 512], yo[:, yb], [ry], [('yaT_scr', qb)], ry)
        P.flush()
    em.close()
    if stop_after == 'E':
        P.finish(); em.close(); es.close(); return nc

    def load_w(dst, src_ap, name, nsplit):
        n = src_ap.shape[-1]
        step = n // nsplit
        for i in range(nsplit):
            dma('pool', dst[:, :, i * step:(i + 1) * step], src_ap[:, :, i * step:(i + 1) * step], [], [(name, i)], (name, i), **CAST)
        return [(name, i) for i in range(nsplit)]

    def prenorm_T(src_tile, rsrc, gs, sh, dstT, rdst, t, junk_, ssb, hn_, tp_, R_tp_, names):
        stt(junk_, src_tile, 1.0, src_tile, ALU.mult, ALU.mult, [rsrc], [names + 'junk', names + 'ss0'], accum_out=ssb[:, 0:1])
        act(ssb[:, 1:2], ssb[:, 0:1], AF.Sqrt, [names + 'ss0', 'eps_c'], [names + 'ss1'], scale=1.0 / D, bias=eps_c[:, 0:1])
        P.op('dve', lambda e: e.reciprocal(out=ssb[:, 2:3], in_=ssb[:, 1:2]), [names + 'ss1'], [names + 'ss2'])
        act(hn_, src_tile, AF.Copy, [rsrc, names + 'ss2'], [names + 'hn'], scale=ssb[:, 2:3])
        for half in range(2):
            tb = R_tp_.next(); rt = (names + 'tp', tb)
            for kk in range(4):
                k = half * 4 + kk
                tr(tp_[tb][:, kk * 128:(kk + 1) * 128], hn_[:, k * 128:(k + 1) * 128], ident_f, [names + 'hn', 'cst_f'], [rt])
            for kk in range(4):
                k = half * 4 + kk
                if kk % 2:
                    act(dstT[:, k, t * 128:(t + 1) * 128], tp_[tb][:, kk * 128:(kk + 1) * 128], AF.Identity,
                        [rt], [rdst], scale=gs[:, k:k + 1], bias=sh[:, k:k + 1])
                else:
                    tsc('dve', dstT[:, k, t * 128:(t + 1) * 128], tp_[tb][:, kk * 128:(kk + 1) * 128], gs[:, k:k + 1],
                        sh[:, k:k + 1], ALU.mult, ALU.add, [rt], [rdst])

    def postnorm_res(po, rpo, Gt, res_tile, rres, out_tile, rout, junk_, ssb, names):
        for half in range(2):
            act(junk_[:, half * 512:(half + 1) * 512], po[half][:], AF.Square, [rpo[half]], [names + 'pj%d' % half, names + 'ps%d' % half],
                accum_out=ssb[:, half:half + 1])
        tt('dve', ssb[:, 2:3], ssb[:, 0:1], ssb[:, 1:2], ALU.add, [names + 'ps0', names + 'ps1'], [names + 'ps2'])
        act(ssb[:, 3:4], ssb[:, 2:3], AF.Sqrt, [names + 'ps2', 'eps_c'], [names + 'ps3'], scale=1.0 / D, bias=eps_c[:, 0:1])
        P.op('dve', lambda e: e.reciprocal(out=ssb[:, 4:5], in_=ssb[:, 3:4]), [names + 'ps3'], [names + 'ps4'])
        for half in range(2):
            stt(out_tile[:, half * 512:(half + 1) * 512], po[half][:], ssb[:, 4:5], Gt[:, half * 512:(half + 1) * 512],
                ALU.mult, ALU.mult, [rpo[half], names + 'ps4'], [rout])
        tt('pool', out_tile, out_tile, res_tile, ALU.add, [rout, rres], [rout])

    with contextlib.ExitStack() as sf:
        wbm = sb("wbm", [128, 8, D], BF16, stack=sf); wba = sb("wba", [128, 8, D], BF16, stack=sf)
        wo = sb("wo", [128, 8, D], BF16, stack=sf); wbr = sb("wbr", [128, 8, 2 * D], BF16, stack=sf)
        kp = lambda a: a.rearrange("(k p) n -> p k n", p=128)
        r_wbm = load_w(wbm, kp(w_bm), 'wbm', 1); r_wba = load_w(wba, kp(w_ba), 'wba', 1)
        r_wbr = load_w(wbr, kp(w_br), 'wbr', 2); r_wo = load_w(wo, kp(w_out), 'wo', 1)
        hTb = sb("hTb", [128, 2, 8, 512], BF16, stack=sf); ymTb = sb("ymTb", [128, 2, 8, 512], BF16, stack=sf)
        yaTb = sb("yaTb", [128, 2, 8, 512], BF16, stack=sf)
        gmt = sb("gmt", [128, 2, 2, 512], stack=sf)
        yT = sb("yT", [128, 1, 8, 512], BF16, stack=sf)
        xt1 = sb("xt1", [128, 3, D], stack=sf); x1t = sb("x1t", [128, 2, D], stack=sf)
        junk1 = sb("junk1", [128, D], BF16, stack=sf); ssf = sb("ssf", [128, 2, 8], stack=sf)
        h2T = sb("h2T", [128, 1, 8, 512], BF16, stack=sf)
        pbr = [ps("pbr%d" % i, [128, 512], stack=sf) for i in range(4)]
        po1 = [ps("po1%d" % i, [128, 512], stack=sf) for i in range(2)]
        tp1 = [ps("tp1%d" % i, [128, 512], stack=sf) for i in range(2)]
        R_blk, R_g, R_yT, R_x1, R_tp1, R_h2 = Ring('f1blk', 2), Ring('gmt', 2), Ring('yT', 1), Ring('x1', 2), Ring('f1tp', 2), Ring('h2T', 1)
        hT_v = kp(hT_scr); ymT_v2 = kp(ymT_scr); yaT_v = kp(yaT_scr); h2T_v = kp(h2T_scr)
        hn4 = sb("hn4", [128, 4, D], stack=sf)
        ss4 = sb("ss4", [128, 4, 4], stack=sf)
        bslot = {}

        def f1_loads(qb):
            cs_ = slice(qb * 512, (qb + 1) * 512)
            bslot[qb] = b = R_blk.next()
            dma('sp', hTb[:, b], hT_v[:, :, cs_], [('hT_scr', qb)], [('hTb', b)], ('hTb', b))
            dma('sp', ymTb[:, b], ymT_v2[:, :, cs_], [('ymT_scr', qb)], [('ymTb', b)], ('ymTb', b))
            dma('sp', yaTb[:, b], yaT_v[:, :, cs_], [('yaT_scr', qb)], [('yaTb', b)], ('yaTb', b))

        def f1_branch(qb):
            b = bslot[qb]
            ryT = ('yT', 0)
            for fo in range(8):
                fs = slice(fo * 128, (fo + 1) * 128)
                for k in range(8):
                    mm(pbr[2][:], wbr[:, k, fs], hTb[:, b, k, :], k == 0, k == 7, r_wbr + [('hTb', b)], [('pbr', 2)])
                for k in range(8):
                    mm(pbr[3][:], wbr[:, k, D + fo * 128:D + (fo + 1) * 128], hTb[:, b, k, :], k == 0, k == 7,
                       r_wbr + [('hTb', b)], [('pbr', 3)])
                for k in range(8):
                    mm(pbr[0][:], wbm[:, k, fs], ymTb[:, b, k, :], k == 0, k == 7, r_wbm + [('ymTb', b)], [('pbr', 0)])
                for k in range(8):
                    mm(pbr[1][:], wba[:, k, fs], yaTb[:, b, k, :], k == 0, k == 7, r_wba + [('yaTb', b)], [('pbr', 1)])
                gi = R_g.next(); rg_ = ('gmt', gi)
                act(gmt[:, gi, 0], pbr[2][:], AF.Sigmoid, [('pbr', 2)], [rg_])
                act(gmt[:, gi, 1], pbr[3][:], AF.Sigmoid, [('pbr', 3)], [rg_])
                tt('dve', gmt[:, gi, 0], pbr[0][:], gmt[:, gi, 0], ALU.mult, [('pbr', 0), rg_], [rg_])
                tt('dve', gmt[:, gi, 1], pbr[1][:], gmt[:, gi, 1], ALU.mult, [('pbr', 1), rg_], [rg_])
                tt('pool', yT[:, 0, fo, :], gmt[:, gi, 0], gmt[:, gi, 1], ALU.add, [rg_], [ryT])

        R_xl = Ring('xt1', 3)
        xls = {}

        def f1_xload(tile):
            xls[tile] = s_ = R_xl.next()
            dma('sp', xt1[:, s_], x[tile * 128:(tile + 1) * 128, :], [], [('xt1', s_)], ('xt1', s_))

        def f1_outproj(qb):
            ryT = ('yT', 0)
            if qb == 0:
                f1_xload(0)
            for t in range(4):
                tile = qb * 4 + t
                tk = slice(tile * 128, (tile + 1) * 128)
                if tile + 1 < NTO:
                    f1_xload(tile + 1)
                pp = [po1[0], po1[1]] if t % 2 == 0 else [pbr[0], pbr[1]]
                rpp = [('po1', 0), ('po1', 1)] if t % 2 == 0 else [('pbr', 0), ('pbr', 1)]
                for half in range(2):
                    for k in range(8):
                        mm(pp[half][:], yT[:, 0, k, t * 128:(t + 1) * 128], wo[:, k, half * 512:(half + 1) * 512],
                           k == 0, k == 7, [ryT] + r_wo, [rpp[half]])
                xb = R_x1.next(); rx1 = ('x1t', xb)
                xs_ = xls.pop(tile); rx = ('xt1', xs_)
                postnorm_res(pp, rpp, G1, xt1[:, xs_], rx, x1t[:, xb], rx1, junk1, ssf[:, 0], 'f1')
                dma('pool', x1_scr[tk, :], x1t[:, xb], [rx1], [('x1_scr', tile)], rx1)
                stt(junk1[:], x1t[:, xb], 1.0, x1t[:, xb], ALU.mult, ALU.mult, [rx1], ['f1njunk', ('f1ss0', t)], accum_out=ss4[:, t, 0:1])
                act(ss4[:, t, 1:2], ss4[:, t, 0:1], AF.Sqrt, [('f1ss0', t), 'eps_c'], [('f1ss1', t)], scale=1.0 / D, bias=eps_c[:, 0:1])
                P.op('dve', lambda e, t=t: e.reciprocal(out=ss4[:, t, 2:3], in_=ss4[:, t, 1:2]), [('f1ss1', t)], [('f1ss2', t)])
                act(hn4[:, t], x1t[:, xb], AF.Copy, [rx1, ('f1ss2', t)], [('hn4', t)], scale=ss4[:, t, 2:3])

        def f1_trans(qb):
            cs_ = slice(qb * 512, (qb + 1) * 512)
            rh2 = ('h2T', 0)
            tpb = [tp1[0], tp1[1], pbr[2], pbr[3]]
            rtpb = [('f1tp', 0), ('f1tp', 1), ('pbr', 2), ('pbr', 3)]
            for t in range(4):
                for half in range(2):
                    tb = (t * 2 + half) % 4; rt = rtpb[tb]
                    for kk in range(4):
                        k = half * 4 + kk
                        tr(tpb[tb][:, kk * 128:(kk + 1) * 128], hn4[:, t, k * 128:(k + 1) * 128], ident_f, [('hn4', t), 'cst_f'], [rt])
                    for kk in range(4):
                        k = half * 4 + kk
                        if kk % 2:
                            act(h2T[:, 0, k, t * 128:(t + 1) * 128], tpb[tb][:, kk * 128:(kk + 1) * 128], AF.Identity,
                                [rt], [rh2], scale=gs2[:, k:k + 1], bias=sh2[:, k:k + 1])
                        else:
                            tsc('dve', h2T[:, 0, k, t * 128:(t + 1) * 128], tpb[tb][:, kk * 128:(kk + 1) * 128], gs2[:, k:k + 1],
                                sh2[:, k:k + 1], ALU.mult, ALU.add, [rt], [rh2])
            dma('sp', h2T_v[:, :, cs_], h2T[:, 0], [rh2], [('h2T_scr', qb)], rh2)

        f1_loads(0)
        f1_branch(0)
        for qb in range(NBO):
            if qb + 1 < NBO:
                f1_loads(qb + 1)
            f1_outproj(qb)
            if qb + 1 < NBO:
                f1_branch(qb + 1)
            f1_trans(qb)
        P.flush()
    if stop_after == 'F1':
        P.finish(); em.close(); es.close(); return nc

    with contextlib.ExitStack() as sg:
        w1 = sb("w1", [128, 8, DFF], BF16, stack=sg); w2 = sb("w2", [128, 32, D], BF16, stack=sg)
        r_w1 = load_w(w1, w_m1.rearrange("(k p) n -> p k n", p=128), 'w1', 4)
        r_w2 = load_w(w2, w_m2.rearrange("(c p) n -> p c n", p=128), 'w2', 1)
        h2Tb = sb("h2Tb", [128, 1, 8, 512], BF16, stack=sg)
        rl = sb("rl", [128, 2, 512], stack=sg); uT = sb("uT", [128, 32, 512], BF16, stack=sg)
        x1l = sb("x1l", [128, 2, D], stack=sg); ot = sb("ot", [128, 2, D], stack=sg)
        junk2 = sb("junk2", [128, D], BF16, stack=sg); ssg = sb("ssg", [128, 8], stack=sg)
        pu = [ps("pu%d" % i, [128, 512], stack=sg) for i in range(3)]
        po2 = [[ps("po2%d%d" % (i, j), [128, 512], stack=sg) for j in range(2)] for i in range(2)]
        R_b2, R_pu, R_rl, R_po2, R_x2 = Ring('f2blk', 1), Ring('pu', 3), Ring('rl', 2), Ring('po2', 2), Ring('x2', 2)
        for qb in range(NBO):
            cs_ = slice(qb * 512, (qb + 1) * 512)
            b = R_b2.next(); rb = ('h2Tb', b)
            dma('sp', h2Tb[:, b], h2T_v[:, :, cs_], [('h2T_scr', qb)], [rb], rb)
            for fc in range(32):
                pi = R_pu.next(); rp = ('pu', pi)
                for k in range(8):
                    mm(pu[pi][:], w1[:, k, fc * 128:(fc + 1) * 128], h2Tb[:, b, k, :], k == 0, k == 7,
                       [r_w1[fc // 8], rb], [rp])
                ri = R_rl.next(); rr_ = ('rl', ri)
                act(rl[:, ri], pu[pi][:], AF.Relu, [rp], [rr_])
                tt('pool' if fc % 2 else 'dve', uT[:, fc, :], rl[:, ri], rl[:, ri], ALU.mult, [rr_], [('uT', fc)])
            for t in range(4):
                tile = qb * 4 + t
                tk = slice(tile * 128, (tile + 1) * 128)
                pp = R_po2.next()
                for half in range(2):
                    for fc in range(32):
                        mm(po2[pp][half][:], uT[:, fc, t * 128:(t + 1) * 128], w2[:, fc, half * 512:(half + 1) * 512],
                           fc == 0, fc == 31, [('uT', fc)] + r_w2, [('po2', pp, half)])
                xb = R_x2.next(); rx = ('x1l', xb); rot = ('ot', xb)
                dma('sp', x1l[:, xb], x1_scr[tk, :], [('x1_scr', tile)], [rx], rx)
                postnorm_res(po2[pp], [('po2', pp, 0), ('po2', pp, 1)], G2, x1l[:, xb], rx, ot[:, xb], rot, junk2, ssg, 'f2')
                dma('pool', y_out[tk, :], ot[:, xb], [rot], [], rot)
    P.finish()
    es.close()
    return nc


def _host_consts():
    cst = np.zeros((128, K_END), np.float32)
    cst[:, K_ID:K_ID + 128] = np.eye(128, dtype=np.float32)
    cst[:, K_ONE:K_ONE + 128] = 1.0
    s = np.arange(128)[:, None]; t = np.arange(128)[None, :]
    cst[:, K_LE:K_LE + 128] = (s <= t)
    cst[:, K_GE:K_GE + 128] = (s >= t)
    R = np.zeros((128, 128), np.float32)
    for j in range(128):
        if (j % 64) < 32:
            R[j, j + 32] = -1.0
        else:
            R[j, j - 32] = 1.0
    cst[:, K_ROT:K_ROT + 128] = R.T
    return cst


def _rope_tables(S, flip):
    pos = np.arange(S)
    if flip:
        pos = S - 1 - pos
    row = (pos // 64).astype(np.float32); col = (pos % 64).astype(np.float32)
    freqs = (np.float32(10000.0) ** (-np.arange(32, dtype=np.float32) / np.float32(32))).astype(np.float32)
    j = np.arange(128)
    p = np.where((j // 64)[:, None] == 0, row[None, :], col[None, :]).astype(np.float32)
    ang = (p * freqs[j % 32][:, None]).astype(np.float32)
    return np.cos(ang).astype(np.float32), np.sin(ang).astype(np.float32)


def core_inputs(inp, core, S):
    b, flip = core // 2, (core % 2) == 1
    f32 = lambda a: np.ascontiguousarray(a, dtype=np.float32)
    xs = inp['x'][b][:S]
    if flip:
        xs = xs[::-1]
    w = inp['w_in'][0]
    offs = np.cumsum([0, 2048, 1024, 1024, 16, 1024, 256, 256, 2048])
    qk, v, o, g, qa, ka, va, br = [w[:, offs[i]:offs[i + 1]] for i in range(8)]
    bgt = inp['b_gates'][0]
    cw = inp['conv_w'][0]
    if flip:
        perm = list(range(8, 16)) + list(range(0, 8))
        g = g[:, perm]; bgt = bgt[perm]; cw = cw[::-1]
    w_dev = np.concatenate([qk, qa, ka, v, o, va, g], axis=1)
    cos_t, sin_t = _rope_tables(S, flip)
    return {
        'x': f32(xs), 'c': f32(inp['c'][b]), 'w_ada': f32(inp['w_ada'][0]), 'b_ada': f32(inp['b_ada'][0]),
        'norm1_pre': f32(inp['norm1_pre'][0]), 'norm1_post': f32(inp['norm1_post'][0]),
        'norm2_pre': f32(inp['norm2_pre'][0]), 'norm2_post': f32(inp['norm2_post'][0]),
        'w_in': f32(w_dev), 'w_br': f32(br), 'b_gates': f32(bgt), 'conv_w': f32(cw.T), 'conv_b': f32(inp['conv_b'][0]),
        'mlstm_gn': f32(inp['mlstm_gn'][0]), 'attn_qnorm': f32(inp['attn_qnorm'][0]), 'attn_knorm': f32(inp['attn_knorm'][0]),
        'w_branch_m': f32(inp['w_branch_m'][0]), 'w_branch_a': f32(inp['w_branch_a'][0]), 'w_out': f32(inp['w_out'][0]),
        'w_mlp_in': f32(inp['w_mlp_in'][0]), 'w_mlp_out': f32(inp['w_mlp_out'][0]),
        'cos_t': cos_t, 'sin_t': sin_t, 'cst': _host_consts(),
    }


def run(inp, S=8192, debug=(), stop_after=None, trace=False):
    inp = {k: np.asarray(v) for k, v in inp.items()}
    nc = build(S, debug, stop_after)
    in_maps = [core_inputs(inp, core, S) for core in range(8)]
    res = run_bass_kernel_spmd(nc, in_maps, core_ids=list(range(8)), **({'trace': True} if trace else {}))
    return res


def kernel(**inputs):
    S = 8192
    res = run(inputs, S)
    out = np.empty((4, S, D), np.float32)
    for core in range(8):
        b, flip = core // 2, (core % 2) == 1
        y = np.asarray(res.results[core]['y'], dtype=np.float32)
        if flip:
            out[b, S // 2:] = y[::-1]
        else:
            out[b, :S // 2] = y
    return out
```
